# Optimizing a Trainium2 kernel written in Bass

```python
import math
import jax, jax.numpy as jnp
from jax import lax
import numpy as np

D_MODEL = 4096
BATCH = 2
SEQ = 4096
DEPTH = 2

D_MIX = D_MODEL
D_POOL = D_MIX // 4
D_ATTN = D_MIX // 2
D_SSM = D_MIX - D_POOL - D_ATTN
POOL_WINDOWS = (2, 4, 8, 16)
N_POOL_GROUPS = len(POOL_WINDOWS)
POOL_GROUP = D_POOL // N_POOL_GROUPS
HEAD_DIM = 128
N_HEADS = D_ATTN // HEAD_DIM
Q_BLOCK = 128
SSM_GROUP = 16
N_SSM_GROUPS = D_SSM // SSM_GROUP
SSM_STATE = 64
D_IN = 2 * D_POOL + 4 * D_ATTN + 2 * D_SSM
EPS = 1e-6

kernel_name = "hybrid_pool_stickbreak_s5_parallel"

_SPLIT_SIZES = (D_POOL, D_POOL, D_ATTN, D_ATTN, D_ATTN, D_ATTN, D_SSM, D_SSM)
_SPLIT_OFFSETS = tuple(int(v) for v in np.cumsum(_SPLIT_SIZES)[:-1])


def rmsnorm(x, g):
    xf = x.astype(jnp.float32)
    y = xf * lax.rsqrt(jnp.mean(xf * xf, axis=-1, keepdims=True) + EPS)
    return y * g.astype(jnp.float32)


def pool_mixer(xp, w_pool, pool_scale):
    bsz, L, _ = xp.shape
    xg = xp.astype(jnp.float32).reshape(bsz, L, N_POOL_GROUPS, POOL_GROUP)
    csum = jnp.cumsum(xg, axis=1)
    pos = jnp.arange(1, L + 1, dtype=jnp.float32)[None, :, None]
    outs = []
    for g, w in enumerate(POOL_WINDOWS):
        cg = csum[:, :, g]
        c_lag = jnp.pad(cg, ((0, 0), (w, 0), (0, 0)))[:, :L]
        mean = (cg - c_lag) / jnp.minimum(pos, float(w))
        outs.append(mean - xg[:, :, g])
    pooled = jnp.stack(outs, axis=2)
    mixed = jnp.einsum('blgc,gcd->blgd', pooled, w_pool.astype(jnp.float32))
    return mixed.reshape(bsz, L, D_POOL) * pool_scale.astype(jnp.float32)


def stick_breaking_attention(q, k, v):
    bsz, L = q.shape[:2]
    nb = L // Q_BLOCK
    qf = q.astype(jnp.float32) * (HEAD_DIM ** -0.5)
    kh = k.astype(jnp.float32).transpose(0, 2, 1, 3)
    vh = v.astype(jnp.float32).transpose(0, 2, 1, 3)
    q_blocks = qf.reshape(bsz, nb, Q_BLOCK, N_HEADS, HEAD_DIM).transpose(1, 0, 3, 2, 4)
    k_pos = jnp.arange(L)

    def one_block(args):
        qb, start = args
        z = jnp.einsum('bhqd,bhkd->bhqk', qb, kh)
        q_pos = start + jnp.arange(Q_BLOCK)
        causal = k_pos[None, :] < q_pos[:, None]
        log_1m = jnp.where(causal, jax.nn.log_sigmoid(-z), 0.0)
        suffix = lax.cumsum(log_1m, axis=3, reverse=True) - log_1m
        wts = jnp.where(causal, jnp.exp(jax.nn.log_sigmoid(z) + suffix), 0.0)
        return jnp.einsum('bhqk,bhkd->bhqd', wts, vh)

    starts = jnp.arange(nb, dtype=jnp.int32) * Q_BLOCK
    out = lax.map(one_block, (q_blocks, starts))
    return out.transpose(1, 0, 3, 2, 4).reshape(bsz, L, D_ATTN)


def s5_mixer(u, lam_re, lam_im, log_dt, b_re, b_im, c_re, c_im, d_skip, w_glu, b_glu):
    bsz, L, _ = u.shape
    uf = u.astype(jnp.float32)
    lam = lax.complex(lam_re.astype(jnp.float32), lam_im.astype(jnp.float32))
    dt = jnp.exp(log_dt.astype(jnp.float32))[:, None]
    lam_bar = jnp.exp(lam * dt)
    b_mat = lax.complex(b_re.astype(jnp.float32), b_im.astype(jnp.float32))
    b_bar = ((lam_bar - 1.0) / lam)[..., None] * b_mat
    c_mat = lax.complex(c_re.astype(jnp.float32), c_im.astype(jnp.float32))
    ug = uf.reshape(bsz, L, N_SSM_GROUPS, SSM_GROUP)
    bu = jnp.einsum('blgc,gpc->blgp', ug.astype(jnp.complex64), b_bar)
    a = jnp.broadcast_to(lam_bar, bu.shape)

    def combine(e_prev, e_next):
        a1, x1 = e_prev
        a2, x2 = e_next
        return a2 * a1, a2 * x1 + x2

    _, states = lax.associative_scan(combine, (a, bu), axis=1)
    y = jnp.einsum('blgp,gcp->blgc', states, c_mat).real.reshape(bsz, L, D_SSM)
    y = y + d_skip.astype(jnp.float32) * uf
    h = jax.nn.gelu(y)
    val, gate = jnp.split(h @ w_glu.astype(jnp.float32) + b_glu.astype(jnp.float32), 2, axis=-1)
    return val * jax.nn.sigmoid(gate)


def hybrid_layer(x, ln_g, w_in, w_pool, pool_scale, lam_re, lam_im, log_dt,
                 b_re, b_im, c_re, c_im, d_skip, w_glu, b_glu, branch_g, w_out):
    bsz, L, _ = x.shape
    h = rmsnorm(x, ln_g).astype(x.dtype)
    proj = h @ w_in
    p_x, p_gate, q, k, v, a_gate, s_u, s_gate = jnp.split(proj, _SPLIT_OFFSETS, axis=-1)

    y_pool = pool_mixer(p_x, w_pool, pool_scale)
    shp = (bsz, L, N_HEADS, HEAD_DIM)
    y_attn = stick_breaking_attention(q.reshape(shp), k.reshape(shp), v.reshape(shp))
    y_ssm = s5_mixer(s_u, lam_re, lam_im, log_dt, b_re, b_im, c_re, c_im, d_skip, w_glu, b_glu)

    g_pool, g_attn, g_ssm = jnp.split(branch_g, [D_POOL, D_POOL + D_ATTN])
    y_pool = rmsnorm(y_pool, g_pool) * jax.nn.silu(p_gate.astype(jnp.float32))
    y_attn = rmsnorm(y_attn, g_attn) * jax.nn.silu(a_gate.astype(jnp.float32))
    y_ssm = rmsnorm(y_ssm, g_ssm) * jax.nn.silu(s_gate.astype(jnp.float32))
    y = jnp.concatenate([y_pool, y_attn, y_ssm], axis=-1).astype(x.dtype)
    return x + y @ w_out


def setup_inputs(seed: int = 0) -> dict:
    key = jax.random.key(seed)
    ks = jax.random.split(key, 20)
    f32 = jnp.float32
    nrm = lambda k, shape, s: jax.random.normal(k, shape, f32) * s
    n_idx = jnp.arange(SSM_STATE, dtype=f32)
    return {
        "x": jax.random.normal(ks[0], (BATCH, SEQ, D_MODEL), f32),
        "ln_g": 1.0 + nrm(ks[1], (DEPTH, D_MODEL), 0.02),
        "w_in": nrm(ks[2], (DEPTH, D_MODEL, D_IN), D_MODEL ** -0.5),
        "w_pool": nrm(ks[3], (DEPTH, N_POOL_GROUPS, POOL_GROUP, POOL_GROUP), POOL_GROUP ** -0.5),
        "pool_scale": 1.0 + nrm(ks[4], (DEPTH, D_POOL), 0.1),
        "lam_re": -0.5 + nrm(ks[5], (DEPTH, N_SSM_GROUPS, SSM_STATE), 0.01),
        "lam_im": math.pi * n_idx + nrm(ks[6], (DEPTH, N_SSM_GROUPS, SSM_STATE), 0.01),
        "log_dt": jax.random.uniform(ks[7], (DEPTH, N_SSM_GROUPS), f32,
                                     math.log(1e-3), math.log(1e-1)),
        "b_re": nrm(ks[8], (DEPTH, N_SSM_GROUPS, SSM_STATE, SSM_GROUP), (2 * SSM_GROUP) ** -0.5),
        "b_im": nrm(ks[9], (DEPTH, N_SSM_GROUPS, SSM_STATE, SSM_GROUP), (2 * SSM_GROUP) ** -0.5),
        "c_re": nrm(ks[10], (DEPTH, N_SSM_GROUPS, SSM_GROUP, SSM_STATE), (2 * SSM_STATE) ** -0.5),
        "c_im": nrm(ks[11], (DEPTH, N_SSM_GROUPS, SSM_GROUP, SSM_STATE), (2 * SSM_STATE) ** -0.5),
        "d_skip": nrm(ks[12], (DEPTH, D_SSM), 1.0),
        "w_glu": nrm(ks[13], (DEPTH, D_SSM, 2 * D_SSM), D_SSM ** -0.5),
        "b_glu": nrm(ks[14], (DEPTH, 2 * D_SSM), 0.01),
        "branch_g": 1.0 + nrm(ks[15], (DEPTH, D_MIX), 0.02),
        "w_out": nrm(ks[16], (DEPTH, D_MIX, D_MODEL), (2 * DEPTH * D_MIX) ** -0.5),
        "final_g": 1.0 + nrm(ks[17], (D_MODEL,), 0.02),
    }


def reference(x, ln_g, w_in, w_pool, pool_scale, lam_re, lam_im, log_dt,
              b_re, b_im, c_re, c_im, d_skip, w_glu, b_glu, branch_g, w_out, final_g):
    h = x
    for l in range(DEPTH):
        h = hybrid_layer(h, ln_g[l], w_in[l], w_pool[l], pool_scale[l], lam_re[l], lam_im[l],
                         log_dt[l], b_re[l], b_im[l], c_re[l], c_im[l], d_skip[l],
                         w_glu[l], b_glu[l], branch_g[l], w_out[l])
    return rmsnorm(h, final_g).astype(x.dtype)
```

```python
import math
import contextlib
import numpy as np
import ml_dtypes
import concourse.bass as bass
import concourse.mybir as mybir
from concourse.bass_utils import run_bass_kernel_spmd

F32 = mybir.dt.float32
BF16 = mybir.dt.bfloat16
AF = mybir.ActivationFunctionType
ALU = mybir.AluOpType

D = 4096
NB = 2
DEPTH = 2
EPS = 1e-6
GROUPS = [[0, 1, 2, 3], [4, 5, 6, 7]]
NEG = -30000.0
ENGS = ("pe", "act", "dve", "pool", "sp")
CENGS = ("pe", "act", "dve", "pool")


class DSem:
    def __init__(self, h):
        self.h = h
        self.n = 0
        self.persist = False


class Buf:
    def __init__(self, name=""):
        self.name = name
        self.w = {}
        self.r = {}
        self.excl = False


class Tile:
    def __init__(self, t, name):
        self.t = t
        self.b = Buf(name)
        self.ds = None

    def __getitem__(self, k):
        return self.t[k]


class Prog:
    def __init__(self, L):
        self.L = L
        self.nc = bass.Bass("TRN2", target_bir_lowering=False)
        nc = self.nc
        self.glob = contextlib.ExitStack()
        self.esem = {e: nc.alloc_semaphore(name=f"e_{e}") for e in CENGS}
        self.ph_a = nc.alloc_semaphore(name="ph_a")
        self.ph_b = nc.alloc_semaphore(name="ph_b")
        self.dsems = [DSem(nc.alloc_semaphore(name=f"d{i}")) for i in range(90)]
        self.psum = []
        for i in range(8):
            t = self.glob.enter_context(nc.psum_tensor(f"ps{i}", [128, 512], F32))
            self.psum.append(Tile(t, f"ps{i}"))
            self.psum[-1].b.excl = True
        self.phase_no = 0
        self.in_phase = False
        self.persist_lo = len(self.dsems)
        self.pnamed = {}

    def begin(self):
        assert not self.in_phase
        self.in_phase = True
        self.ops = {e: [] for e in ENGS}
        self.cnt = {e: 0 for e in CENGS}
        self.seen = {e: {} for e in ENGS}
        self.ds_next = 0
        for d in self.dsems[: self.persist_lo]:
            d.n = 0
        for p in self.psum:
            p.b.w = {}
            p.b.r = {}
        self.sb = contextlib.ExitStack()

    def pds(self, name):
        if name not in self.pnamed:
            self.pnamed[name] = self.new_ds(persist=True)
        return self.pnamed[name]

    def new_ds(self, persist=False):
        if persist:
            self.persist_lo -= 1
            assert self.persist_lo >= self.ds_next
            self.dsems[self.persist_lo].persist = True
            return self.dsems[self.persist_lo]
        d = self.dsems[self.ds_next]
        self.ds_next += 1
        assert self.ds_next <= self.persist_lo
        return d

    def tile(self, name, shape, dt):
        t = self.sb.enter_context(self.nc.sbuf_tensor(f"{name}_{self.phase_no}", list(shape), dt))
        return Tile(t, name)

    def end(self):
        used = self.dsems[: self.ds_next]
        pers = [d for d in self.dsems[self.persist_lo:] if not getattr(d, "nobarrier", False)]
        for e in ENGS:
            for p in CENGS:
                if self.cnt[p] > 0:
                    self._wait1(e, p, self.cnt[p])
            for d in used + pers:
                if d.n > 0:
                    self._wait1(e, d, d.n)
        self.phase_no += 1
        k = self.phase_no
        for e in ("pe", "act", "dve", "sp"):
            self.ops[e].append(lambda E: E.sem_inc(self.ph_a, 1))
        self.ops["pool"].append(lambda E: E.wait_ge(self.ph_a, 4 * k))
        for p in CENGS:
            self.ops["pool"].append(lambda E, s=self.esem[p]: E.sem_clear(s))
        for d in used:
            self.ops["pool"].append(lambda E, s=d.h: E.sem_clear(s))
        self.ops["pool"].append(lambda E: E.sem_inc(self.ph_b, 1))
        for e in ENGS:
            self.ops[e].append(lambda E: E.wait_ge(self.ph_b, k))
        ops = self.ops
        with self.nc.Block() as block:
            @block.tensor
            def _(E):
                for f in ops["pe"]:
                    f(E)

            @block.scalar
            def _(E):
                for f in ops["act"]:
                    f(E)

            @block.vector
            def _(E):
                for f in ops["dve"]:
                    f(E)

            @block.gpsimd
            def _(E):
                for f in ops["pool"]:
                    f(E)

            @block.sync
            def _(E):
                for f in ops["sp"]:
                    f(E)
        self.sb.close()
        self.in_phase = False

    def _wait1(self, eng, key, val):
        if self.seen[eng].get(key, 0) >= val:
            return
        self.seen[eng][key] = val
        if isinstance(key, DSem):
            self.ops[eng].append(lambda E, s=key.h, v=val: E.wait_ge(s, v))
        else:
            self.ops[eng].append(lambda E, s=self.esem[key], v=val: E.wait_ge(s, v))

    def _waits(self, eng, r, w, after):
        for b in r:
            for k, v in b.w.items():
                self._wait1(eng, k, v)
            if b.excl:
                for k, v in b.r.items():
                    if k != eng:
                        self._wait1(eng, k, v)
        for b in w:
            for k, v in b.w.items():
                if eng == "pe" and k == "pe":
                    continue
                self._wait1(eng, k, v)
            for k, v in b.r.items():
                self._wait1(eng, k, v)
        for ev in after:
            if ev is not None:
                self._wait1(eng, ev[0], ev[1])

    @staticmethod
    def _bufs(xs):
        return [x.b if isinstance(x, Tile) else x for x in xs]

    def _record(self, ev, r, w):
        k, v = ev
        for b in r:
            if b.r.get(k, 0) < v:
                b.r[k] = v
        for b in w:
            b.w = {k: v}
            b.r = {}

    def op(self, eng, fn, r=(), w=(), after=(), sig=True):
        r = self._bufs(r)
        w = self._bufs(w)
        self._waits(eng, r, w, after)
        if sig:
            self.cnt[eng] += 1
            ev = (eng, self.cnt[eng])
            self.ops[eng].append(lambda E, fn=fn, s=self.esem[eng]: fn(E).then_inc(s, 1))
        else:
            ev = (eng, self.cnt[eng] + 1)
            self.ops[eng].append(lambda E, fn=fn: fn(E))
        self._record(ev, r, w)
        return ev

    def dma(self, q, out, in_, ds, r=(), w=(), after=()):
        r = self._bufs(r)
        w = self._bufs(w)
        self._waits(q, r, w, after)
        assert (q != "pool") or ds.persist
        ds.n += 16
        ev = (ds, ds.n)
        self.ops[q].append(lambda E, o=out, i=in_, s=ds.h: E.dma_start(out=o, in_=i).then_inc(s, 16))
        self._record(ev, r, w)
        return ev

    def ag_sem(self, name):
        d = self.pds(name)
        d.nobarrier = True
        return d

    def wait_all(self, eng, ds):
        if ds.n > 0:
            self._wait1(eng, ds, ds.n)

    def allgather(self, in_ap, out_ap, ds, after=()):
        self._waits("pool", [], [], after)
        assert ds.persist
        ds.n += 1
        ev = (ds, ds.n)
        self.ops["pool"].append(
            lambda E, i=in_ap, o=out_ap, s=ds.h: E.collective_compute(
                "AllGather", ALU.bypass, replica_groups=GROUPS, ins=[i], outs=[o]
            ).then_inc(s, 1)
        )
        return ev


class MK:
    def __init__(self, L, ext_in=(), ext_out=(), use_cc=True):
        self.L = L
        self.NCH = L // 512
        self.P = Prog(L)
        self.nc = self.P.nc
        self.ext_in = set(ext_in)
        self.ext_out = set(ext_out)
        self.use_cc = use_cc
        self.dr = {}
        self.xstack = contextlib.ExitStack()
        self.Wg = None
        self.Wo = None

    def dram(self, name, shape, dt, kind=None):
        if name in self.dr:
            return self.dr[name]
        if kind is None:
            if name in self.ext_in:
                kind = "ExternalInput"
            elif name in self.ext_out:
                kind = "ExternalOutput"
            else:
                kind = "Internal"
        t = self.nc.dram_tensor(name, list(shape), dt, kind=kind)
        self.dr[name] = t
        return t

    def consts_dram(self):
        self.c_mats = self.dram("c_mats", [128, 4 * 128], F32, "ExternalInput")
        self.c_mask = self.dram("c_mask", [128, 4 * 512], F32, "ExternalInput")
        self.c_invc = self.dram("c_invc", [4, self.L], F32, "ExternalInput")

    def ph_norm_stats(self, xt, stl, stg):
        P, L, NCH = self.P, self.L, self.NCH
        P.begin()
        ones = P.tile("ones", [128, 128], BF16)
        xs = [P.tile(f"x{i}", [128, L], F32) for i in range(2)]
        sq = [P.tile(f"sq{i}", [128, L], BF16) for i in range(2)]
        srow = P.tile("srow", [1, L], F32)
        for t in xs + [srow]:
            t.ds = P.new_ds()
        ones.ds = P.pds("ones")
        P.dma("pool", ones[:, :], self.c_mats[:, 256:384], ones.ds, w=[ones])
        import os
        CUT = int(os.environ.get("PH1_CUT", "9"))
        for i in range(8):
            x = xs[i % 2]
            s = sq[i % 2]
            P.dma("sp", x[:, :], xt[i * 128:(i + 1) * 128, :], x.ds, w=[x])
            P.op("act", lambda E, s=s, x=x: E.activation(out=s[:, :], in_=x[:, :], func=AF.Square), r=[x], w=[s])
            if CUT < 2:
                continue
            for ch in range(NCH):
                ps = P.psum[ch]
                P.op("pe", lambda E, ps=ps, s=s, ch=ch, i=i: E.matmul(
                    ps[:, :], lhsT=ones[:, :], rhs=s[:, ch * 512:(ch + 1) * 512], start=(i == 0), stop=(i == 7)),
                    r=[ones, s], w=[ps], sig=(ch == NCH - 1))
        for ch in range(NCH):
            if CUT < 3:
                continue
            ps = P.psum[ch]
            P.op("dve", lambda E, ps=ps, ch=ch: E.tensor_copy(out=srow[0:1, ch * 512:(ch + 1) * 512], in_=ps[0:1, :]),
                 r=[ps], w=[srow])
        ev = None
        if CUT >= 4:
            ev = P.dma("sp", stl[0:1, :], srow[0:1, :], srow.ds, r=[srow])
        if self.use_cc:
            P.allgather(stl.ap(), stg.ap(), P.pds("ag"), after=[ev])
        P.end()

    def ph_norm_apply(self, xt, g, stg, htl, htg, final_out=None):
        P, L, NCH = self.P, self.L, self.NCH
        P.begin()
        ones4 = P.tile("ones4", [4, 128], F32)
        st4 = P.tile("st4", [4, L], F32)
        gt = P.tile("gt", [128, 8], F32)
        R = P.tile("R", [128, L], F32)
        xs = [P.tile(f"x{i}", [128, L], F32) for i in range(2)]
        odt = F32 if final_out is not None else BF16
        hs = [P.tile(f"h{i}", [128, L], odt) for i in range(2)]
        for t in xs + hs + [ones4, st4, gt]:
            t.ds = P.new_ds()
        P.dma("sp", ones4[:, :], self.c_mats[0:4, 256:384], ones4.ds, w=[ones4])
        P.dma("sp", st4[:, :], stg[:, :], st4.ds, w=[st4])
        P.dma("sp", gt[:, :], g[:, :], gt.ds, w=[gt])
        for ch in range(NCH):
            ps = P.psum[ch]
            sl = slice(ch * 512, (ch + 1) * 512)
            P.op("pe", lambda E, ps=ps, sl=sl: E.matmul(ps[:, :], lhsT=ones4[:, :], rhs=st4[:, sl], start=True, stop=True),
                 r=[ones4, st4], w=[ps])
            P.op("dve", lambda E, ps=ps, sl=sl: E.tensor_scalar(out=R[:, sl], in0=ps[:, :], scalar1=1.0 / D, scalar2=EPS,
                                                                 op0=ALU.mult, op1=ALU.add), r=[ps], w=[R])
        P.op("act", lambda E: E.activation(out=R[:, :], in_=R[:, :], func=AF.Sqrt), r=[R], w=[R])
        P.op("dve", lambda E: E.reciprocal(out=R[:, :], in_=R[:, :]), r=[R], w=[R])
        agds = P.pds("ag")
        for i in range(8):
            x = xs[i % 2]
            h = hs[i % 2]
            P.dma("sp", x[:, :], xt[i * 128:(i + 1) * 128, :], x.ds, w=[x])
            P.op("dve", lambda E, x=x, h=h, i=i: E.scalar_tensor_tensor(
                out=h[:, :], in0=x[:, :], scalar=gt[:, i:i + 1], in1=R[:, :], op0=ALU.mult, op1=ALU.mult),
                r=[x, gt, R], w=[h])
            if final_out is not None:
                P.dma("sp", final_out[i * 128:(i + 1) * 128, :], h[:, :], h.ds, r=[h])
            else:
                ev = P.dma("sp", htl[i][:, :], h[:, :], h.ds, r=[h])
                if self.use_cc:
                    P.allgather(htl[i].ap(), htg[i].ap(), agds, after=[ev])
        P.end()

    def dump(self, src, name, rows=None):
        P = self.P
        shape = list(src.shape)
        if rows is not None:
            shape[0] = rows
        dst = self.dram(name, shape, src.dtype, "ExternalOutput")
        P.begin()
        ds = P.new_ds()
        rows = shape[0]
        step = 128 if rows >= 128 else rows
        for r0 in range(0, rows, step):
            P.dma("sp", dst[r0:r0 + step, :], src[r0:r0 + step, :], ds)
        P.end()
        return dst

    def scratch(self):
        L = self.L
        S = {}
        S["PX"] = self.dram("PX", [256, L], F32)
        S["PG"] = self.dram("PG", [256, L], BF16)
        S["QT"] = self.dram("QT", [512, L], BF16)
        S["KT"] = self.dram("KT", [512, L], BF16)
        S["V"] = self.dram("V", [L, 512], BF16)
        S["AG"] = self.dram("AGs", [512, L], BF16)
        S["UT"] = self.dram("UT", [256, L], BF16)
        S["ygl"] = [self.dram(f"ygl{a}", [128, L], BF16) for a in range(10)]
        S["ygg"] = [self.dram(f"ygg{a}", [512, L], BF16) for a in range(10)]
        S["bsl"] = self.dram("bsl", [2, L], F32)
        S["bsg"] = self.dram("bsg", [8, L], F32)
        self.S = S
        return S

    def ph_inproj(self, htg, win):
        P, L, NCH, S = self.P, self.L, self.NCH, self.S
        P.begin()
        Ws = [P.tile(f"W{i}", [128, 4, 8, 512], BF16) for i in range(2)]
        HTs = [P.tile(f"HT{i}", [128, 4, 8, 512], BF16) for i in range(2)]
        NO = 6
        Of = [P.tile(f"Of{i}", [128, 512], F32) for i in range(NO)]
        Ob = [P.tile(f"Ob{i}", [128, 512], BF16) for i in range(NO)]
        for i, t in enumerate(Ws):
            t.ds = P.pds(f"w{i}")
        for t in HTs + Of + Ob:
            t.ds = P.new_ds()
        steps = [(g, ch) for g in range(6) for ch in range(NCH)]

        def load_w(g):
            W = Ws[g % 2]
            for r in range(4):
                src = win[1024 * r:1024 * (r + 1), g * 512:(g + 1) * 512].rearrange("(i p) f -> p i f", p=128)
                P.dma("pool", W[:, r, :, :], src, W.ds, w=[W])

        def load_h(si):
            g, ch = steps[si]
            HT = HTs[si % 2]
            for i in range(8):
                src = htg[i][:, ch * 512:(ch + 1) * 512].rearrange("(r p) t -> p r t", p=128)
                P.dma("sp", HT[:, :, i, :], src, HT.ds, w=[HT])

        load_w(0)
        load_h(0)
        oi = 0
        for si, (g, ch) in enumerate(steps):
            if ch == 0 and g + 1 < 6:
                load_w(g + 1)
            if si + 1 < len(steps):
                load_h(si + 1)
            W = Ws[g % 2]
            HT = HTs[si % 2]
            pb = (si % 2) * 4
            for r in range(4):
                for i in range(8):
                    kt = r * 8 + i
                    for f in range(4):
                        ps = P.psum[pb + f]
                        if g == 3:
                            fn = lambda E, ps=ps, r=r, i=i, f=f, W=W, HT=HT, kt=kt: E.matmul(
                                ps[:, :], lhsT=HT[:, r, i, f * 128:(f + 1) * 128], rhs=W[:, r, i, :],
                                start=(kt == 0), stop=(kt == 31))
                        else:
                            fn = lambda E, ps=ps, r=r, i=i, f=f, W=W, HT=HT, kt=kt: E.matmul(
                                ps[:, :], lhsT=W[:, r, i, f * 128:(f + 1) * 128], rhs=HT[:, r, i, :],
                                start=(kt == 0), stop=(kt == 31))
                        P.op("pe", fn, r=[W, HT], w=[ps], sig=(kt == 31))
            csl = slice(ch * 512, (ch + 1) * 512)
            for f in range(4):
                ps = P.psum[pb + f]
                eng = "act" if (f % 2 == 0) else "dve"
                rs = slice((f % 2) * 128, (f % 2) * 128 + 128)
                fs = slice(f * 128, f * 128 + 128)
                kind = "copy"
                if g == 0:
                    if f < 2:
                        dst, odt = S["PX"][rs, csl], F32
                    else:
                        dst, odt, kind = S["PG"][rs, csl], BF16, "silu"
                elif g == 1:
                    dst, odt, kind = S["QT"][fs, csl], BF16, "scale"
                elif g == 2:
                    dst, odt = S["KT"][fs, csl], BF16
                elif g == 3:
                    dst, odt = S["V"][ch * 512 + f * 128: ch * 512 + f * 128 + 128, :], BF16
                elif g == 4:
                    dst, odt, kind = S["AG"][fs, csl], BF16, "silu"
                else:
                    if f < 2:
                        dst, odt = S["UT"][rs, csl], BF16
                    else:
                        dst, odt, kind = S["ygl"][8 + f - 2][:, csl], BF16, "silu"
                O = (Of if odt == F32 else Ob)[oi % NO]
                oi += 1
                if kind == "silu":
                    eng = "act"
                    P.op("act", lambda E, O=O, ps=ps: E.activation(out=O[:, :], in_=ps[:, :], func=AF.Silu), r=[ps], w=[O])
                elif kind == "scale":
                    sc = 128.0 ** -0.5
                    if eng == "act":
                        P.op("act", lambda E, O=O, ps=ps: E.activation(out=O[:, :], in_=ps[:, :], func=AF.Copy, scale=sc), r=[ps], w=[O])
                    else:
                        P.op("dve", lambda E, O=O, ps=ps: E.tensor_scalar(out=O[:, :], in0=ps[:, :], scalar1=sc, scalar2=None, op0=ALU.mult), r=[ps], w=[O])
                else:
                    if eng == "act":
                        P.op("act", lambda E, O=O, ps=ps: E.activation(out=O[:, :], in_=ps[:, :], func=AF.Copy), r=[ps], w=[O])
                    else:
                        P.op("dve", lambda E, O=O, ps=ps: E.tensor_copy(out=O[:, :], in_=ps[:, :]), r=[ps], w=[O])
                P.dma("sp", dst, O[:, :], O.ds, r=[O])
        if self.use_cc:
            evs = [(t.ds, t.ds.n) for t in Of + Ob if t.ds.n > 0]
            for a in (8, 9):
                P.allgather(S["ygl"][a].ap(), S["ygg"][a].ap(), P.ag_sem("ag_a"), after=evs)
        P.end()

    def ph_pool(self, wpool, pscale, gpool, poolsel, poolselT):
        P, L, NCH, S = self.P, self.L, self.NCH, self.S
        P.begin()
        ones = P.tile("ones", [128, 128], BF16)
        ones.ds = P.pds("ones")
        P.dma("pool", ones[:, :], self.c_mats[:, 256:384], ones.ds, w=[ones])
        wp = P.tile("wp", [128, 2, 256], BF16)
        wp.ds = P.pds("w0")
        P.dma("pool", wp[:, :, :], wpool[:, :].rearrange("(j p) d -> p j d", p=128), wp.ds, w=[wp])
        PB = [P.tile(f"PB{j}", [128, L], BF16) for j in range(2)]
        small = P.tile("small", [128, 8], F32)
        small.ds = P.new_ds()
        P.dma("sp", small[:, 0:2], pscale[:, :], small.ds, w=[small])
        P.dma("sp", small[:, 2:4], gpool[:, :], small.ds, w=[small])
        P.dma("sp", small[:, 4:8], poolsel[:, :], small.ds, w=[small])
        selT = P.tile("selT", [4, 128], BF16)
        selT.ds = P.pds("w1")
        P.dma("pool", selT[:, :], poolselT[:, :], selT.ds, w=[selT])
        inv4 = [P.tile(f"inv4_{i}", [4, 512], F32) for i in range(2)]
        ihi = [P.tile(f"ihi{i}", [4, 512], BF16) for i in range(2)]
        ilo = [P.tile(f"ilo{i}", [4, 512], BF16) for i in range(2)]
        for t in inv4:
            t.ds = P.new_ds()
        IC = P.tile("IC", [128, L], F32)
        for ch in range(NCH):
            ps = P.psum[6 + ch % 2]
            i4, hi, lo = inv4[ch % 2], ihi[ch % 2], ilo[ch % 2]
            P.dma("sp", i4[:, :], self.c_invc[:, ch * 512:(ch + 1) * 512], i4.ds, w=[i4])
            P.op("dve", lambda E, i4=i4, hi=hi: E.tensor_copy(out=hi[:, :], in_=i4[:, :]), r=[i4], w=[hi])
            P.op("dve", lambda E, i4=i4, hi=hi, lo=lo: E.tensor_tensor(out=lo[:, :], in0=i4[:, :], in1=hi[:, :], op=ALU.subtract), r=[i4, hi], w=[lo])
            P.op("pe", lambda E, ps=ps, hi=hi: E.matmul(ps[:, :], lhsT=selT[:, :], rhs=hi[:, :], start=True, stop=False),
                 r=[selT, hi], w=[ps], sig=False)
            P.op("pe", lambda E, ps=ps, lo=lo: E.matmul(ps[:, :], lhsT=selT[:, :], rhs=lo[:, :], start=False, stop=True),
                 r=[selT, lo], w=[ps])
            P.op("act", lambda E, ps=ps, ch=ch: E.activation(out=IC[:, ch * 512:(ch + 1) * 512], in_=ps[:, :], func=AF.Copy), r=[ps], w=[IC])
        import os
        PCUT = int(os.environ.get("POOL_CUT", "9"))
        if PCUT < 1:
            P.end()
            return
        X = P.tile("X", [128, L], F32)
        A = P.tile("A", [128, L], F32)
        B = P.tile("B", [128, L], F32)
        Sx = P.tile("Sx", [128, L], F32)
        PGt = [P.tile(f"PG{j}", [128, L], BF16) for j in range(2)]
        srow = P.tile("srow", [1, L], F32)
        for t in [X, srow] + PGt:
            t.ds = P.new_ds()
        for j in range(2):
            P.dma("sp", PGt[j][:, :], S["PG"][j * 128:(j + 1) * 128, :], PGt[j].ds, w=[PGt[j]])
        for j in range(2):
            P.dma("sp", X[:, :], S["PX"][j * 128:(j + 1) * 128, :], X.ds, w=[X])
            src = X
            bufs = [A, B]
            for wi, k in enumerate((1, 2, 4, 8)):
                dst = bufs[wi % 2]
                P.op("dve", lambda E, dst=dst, src=src, k=k: E.tensor_tensor(out=dst[:, k:L], in0=src[:, k:L], in1=src[:, 0:L - k], op=ALU.add),
                     r=[src], w=[dst])
                P.op("dve", lambda E, dst=dst, src=src, k=k: E.tensor_copy(out=dst[:, 0:k], in_=src[:, 0:k]), r=[src], w=[dst])
                if wi == 0:
                    P.op("dve", lambda E, dst=dst: E.tensor_scalar(out=Sx[:, :], in0=dst[:, :], scalar1=small[:, 4:5], scalar2=None, op0=ALU.mult),
                         r=[dst, small], w=[Sx])
                else:
                    P.op("dve", lambda E, dst=dst, wi=wi: E.scalar_tensor_tensor(out=Sx[:, :], in0=dst[:, :], scalar=small[:, 4 + wi:5 + wi],
                                                                               in1=Sx[:, :], op0=ALU.mult, op1=ALU.add), r=[dst, small, Sx], w=[Sx])
                src = dst
            P.op("dve", lambda E: E.tensor_tensor(out=Sx[:, :], in0=Sx[:, :], in1=IC[:, :], op=ALU.mult), r=[Sx, IC], w=[Sx])
            P.op("dve", lambda E, j=j: E.tensor_tensor(out=Sx[:, :], in0=Sx[:, :], in1=X[:, :], op=ALU.subtract), r=[Sx, X], w=[Sx])
            if os.environ.get("PB_FROM_PG"):
                P.op("act", lambda E, j=j: E.activation(out=PB[j][:, :], in_=PGt[j][:, :], func=AF.Copy), r=[Sx, PGt[j]], w=[PB[j]])
            else:
                P.op("act", lambda E, j=j: E.activation(out=PB[j][:, :], in_=Sx[:, :], func=AF.Copy), r=[Sx], w=[PB[j]])
        if PCUT < 2:
            for j in range(2):
                PB[j].ds = P.new_ds()
                P.dma("sp", S["ygl"][j][:, :], PB[j][:, :], PB[j].ds, r=[PB[j]])
            P.end()
            return
        NO = 4
        Yt = [P.tile(f"Y{i}", [128, 512], F32) for i in range(NO)]
        Qt = [P.tile(f"Q{i}", [128, 512], BF16) for i in range(NO)]
        Gt = [P.tile(f"G{i}", [128, 512], BF16) for i in range(NO)]
        for t in Gt:
            t.ds = P.new_ds()
        oi = 0
        for ch in range(NCH):
            csl = slice(ch * 512, (ch + 1) * 512)
            pst = P.psum[4 + ch % 2]
            for dt in range(2):
                ps = P.psum[oi % 4]
                Y, Q, G = Yt[oi % NO], Qt[oi % NO], Gt[oi % NO]
                oi += 1
                for j in range(2 if PCUT != 19 else 0):
                    P.op("pe", lambda E, ps=ps, j=j, dt=dt, csl=csl: E.matmul(ps[:, :], lhsT=wp[:, j, dt * 128:(dt + 1) * 128], rhs=(PGt if os.environ.get("USE_PG") else PB)[j][:, csl],
                                                                          start=(j == 0), stop=(j == 1)), r=[wp, PB[j], PGt[j]], w=[ps], sig=(j == 1))
                P.op("dve", lambda E, ps=ps, Y=Y, dt=dt: E.tensor_scalar(out=Y[:, :], in0=ps[:, :], scalar1=small[:, dt:dt + 1], scalar2=None, op0=ALU.mult),
                     r=[ps, small], w=[Y])
                P.op("act", lambda E, Y=Y, Q=Q: E.activation(out=Q[:, :], in_=Y[:, :], func=AF.Square), r=[Y], w=[Q])
                if PCUT >= 3 and PCUT < 20:
                    P.op("pe", lambda E, pst=pst, Q=Q, dt=dt: E.matmul(pst[:, :], lhsT=ones[:, :], rhs=Q[:, :], start=(dt == 0), stop=(dt == 1)),
                     r=[ones, Q], w=[pst], sig=True)
                if PCUT >= 4 and PCUT < 20:
                    P.op("dve", lambda E, Y=Y, G=G, dt=dt, csl=csl: E.scalar_tensor_tensor(out=G[:, :], in0=Y[:, :], scalar=small[:, 2 + dt:3 + dt],
                                                                                   in1=PGt[dt][:, csl], op0=ALU.mult, op1=ALU.mult),
                     r=[Y, small, PGt[dt]], w=[G])
                if PCUT >= 5 and PCUT < 20:
                    P.dma("sp", S["ygl"][dt][:, csl], G[:, :], G.ds, r=[G])
            if PCUT >= 6 and PCUT < 20:
                P.op("dve", lambda E, pst=pst, csl=csl: E.tensor_copy(out=srow[0:1, csl], in_=pst[0:1, :]), r=[pst], w=[srow])
        if PCUT >= 6 and PCUT < 20:
            P.dma("sp", S["bsl"][0:1, :], srow[0:1, :], srow.ds, r=[srow])
        if self.use_cc:
            evs = [(t.ds, t.ds.n) for t in Gt if t.ds.n > 0]
            for a in (0, 1):
                P.allgather(S["ygl"][a].ap(), S["ygg"][a].ap(), P.ag_sem("ag_b"), after=evs)
        P.end()

    def ph_attn(self, gattn, wglu_next=None):
        P, L, NCH, S = self.P, self.L, self.NCH, self.S
        NBLK = L // 128
        P.begin()
        if wglu_next is not None:
            self.prefetch_wg(wglu_next)
        mats = P.tile("mats", [128, 3, 128], BF16)
        mats.ds = P.pds("ones")
        P.dma("pool", mats[:, :, :], self.c_mats[:, 0:384].rearrange("p (a b) -> p a b", a=3), mats.ds, w=[mats])
        mask = P.tile("mask", [128, 4, 512], BF16)
        mask.ds = P.pds("w0")
        P.dma("pool", mask[:, :, :], self.c_mask[:, :].rearrange("p (a b) -> p a b", a=4), mask.ds, w=[mask])
        ga = P.tile("ga", [128, 4], F32)
        ga.ds = P.new_ds()
        P.dma("sp", ga[:, :], gattn[:, :], ga.ds, w=[ga])
        Qh = [P.tile(f"Qh{i}", [128, L], BF16) for i in range(2)]
        Kh = [P.tile(f"Kh{i}", [128, L], BF16) for i in range(2)]
        Vh = [P.tile(f"Vh{i}", [128, NBLK, 128], BF16) for i in range(2)]
        Gh = [P.tile(f"Gh{i}", [128, L], BF16) for i in range(2)]
        for t in Qh + Kh + Vh + Gh:
            t.ds = P.new_ds()
        NR = 4
        Eb = [P.tile(f"E{i}", [128, 512], F32) for i in range(NR)]
        Lb = [P.tile(f"Lb{i}", [128, 512], BF16) for i in range(NR)]
        Db = [P.tile(f"D{i}", [128, 512], F32) for i in range(NR)]
        Wb = [P.tile(f"Wt{i}", [128, 512], BF16) for i in range(NR)]
        Carry = P.tile("Carry", [128, 512], F32)
        Yc = [P.tile(f"Yc{i}", [128, 512], F32) for i in range(2)]
        Yg = [P.tile(f"Yg{i}", [128, 512], BF16) for i in range(2)]
        Sq = [P.tile(f"Sq{i}", [128, 512], F32) for i in range(2)]
        SQacc = P.tile("SQacc", [128, L], F32)
        SQb = P.tile("SQb", [128, L], BF16)
        srow = P.tile("srow", [1, L], F32)
        for t in Yg + [srow]:
            t.ds = P.new_ds()

        def load_head(h):
            s = h % 2
            hs = slice(h * 128, (h + 1) * 128)
            P.dma("sp", Qh[s][:, :], S["QT"][hs, :], Qh[s].ds, w=[Qh[s]])
            P.dma("sp", Kh[s][:, :], S["KT"][hs, :], Kh[s].ds, w=[Kh[s]])
            P.dma("sp", Vh[s][:, :, :], S["V"][:, hs].rearrange("(n p) d -> p n d", p=128), Vh[s].ds, w=[Vh[s]])
            P.dma("sp", Gh[s][:, :], S["AG"][hs, :], Gh[s].ds, w=[Gh[s]])

        tiles = []
        for h in range(4):
            for qc in range(NCH):
                nb = 4 * (qc + 1)
                for bi, j in enumerate(range(nb - 1, -1, -1)):
                    tiles.append((h, qc, j, bi == 0, bi == nb - 1))
        T = len(tiles)
        ident, negtri, ones = mats[:, 0, :], mats[:, 1, :], mats[:, 2, :]
        strm = {}
        sc = 0
        for t in tiles:
            if t[3]:
                strm[(t[0], t[1])] = sc
                sc += 1

        def zmm(ps, h, qc, j, last_stop):
            s = h % 2
            r = j - 4 * qc
            qsl = slice(qc * 512, (qc + 1) * 512)
            ksl = slice(j * 128, (j + 1) * 128)
            diag = r >= 0
            P.op("pe", lambda E: E.matmul(ps[:, :], lhsT=Kh[s][:, ksl], rhs=Qh[s][:, qsl], start=True, stop=(last_stop and not diag)),
                 r=[Kh[s], Qh[s]], w=[ps], sig=(last_stop and not diag))
            if diag:
                P.op("pe", lambda E: E.matmul(ps[:, :], lhsT=ident, rhs=mask[:, r, :], start=False, stop=last_stop),
                     r=[mats, mask], w=[ps], sig=last_stop)

        def stage_a(ti):
            h, qc, j, first, last = tiles[ti]
            psz = P.psum[ti % 2]
            zmm(psz, h, qc, j, True)
            E_, L_ = Eb[ti % NR], Lb[ti % NR]
            P.op("act", lambda E: E.activation(out=E_[:, :], in_=psz[:, :], func=AF.Exp), r=[psz], w=[E_])
            P.op("act", lambda E: E.activation(out=L_[:, :], in_=E_[:, :], func=AF.Ln, bias=1.0), r=[E_], w=[L_])

        def stage_b(ti):
            h, qc, j, first, last = tiles[ti]
            pse = P.psum[2 + ti % 2]
            pst = P.psum[4 + ti % 2]
            L_, D_ = Lb[ti % NR], Db[ti % NR]
            zmm(pse, h, qc, j, False)
            P.op("pe", lambda E: E.matmul(pse[:, :], lhsT=negtri, rhs=L_[:, :], start=False, stop=True), r=[mats, L_], w=[pse])
            P.op("pe", lambda E: E.matmul(pst[:, :], lhsT=ones, rhs=L_[:, :], start=True, stop=True), r=[mats, L_], w=[pst])
            if first:
                P.op("dve", lambda E: E.tensor_copy(out=D_[:, :], in_=pse[:, :]), r=[pse], w=[D_])
                P.op("dve", lambda E: E.tensor_copy(out=Carry[:, :], in_=pst[:, :]), r=[pst], w=[Carry])
            else:
                P.op("dve", lambda E: E.tensor_tensor(out=D_[:, :], in0=pse[:, :], in1=Carry[:, :], op=ALU.subtract), r=[pse, Carry], w=[D_])
                if not last:
                    P.op("dve", lambda E: E.tensor_tensor(out=Carry[:, :], in0=pst[:, :], in1=Carry[:, :], op=ALU.add), r=[pst, Carry], w=[Carry])

        def stage_c1(ti):
            D_, W_ = Db[ti % NR], Wb[ti % NR]
            P.op("act", lambda E: E.activation(out=W_[:, :], in_=D_[:, :], func=AF.Exp), r=[D_], w=[W_])

        def stage_c2(ti):
            h, qc, j, first, last = tiles[ti]
            s = h % 2
            si = strm[(h, qc)]
            pso = P.psum[6 + si % 2]
            W_ = Wb[ti % NR]
            P.op("pe", lambda E: E.matmul(pso[:, :], lhsT=Vh[s][:, j, :], rhs=W_[:, :], start=first, stop=last), r=[Vh[s], W_], w=[pso], sig=last)
            if last:
                qsl = slice(qc * 512, (qc + 1) * 512)
                yc, yg, sq = Yc[si % 2], Yg[si % 2], Sq[si % 2]
                P.op("dve", lambda E: E.tensor_copy(out=yc[:, :], in_=pso[:, :]), r=[pso], w=[yc])
                P.op("dve", lambda E: E.scalar_tensor_tensor(out=yg[:, :], in0=yc[:, :], scalar=ga[:, h:h + 1], in1=Gh[s][:, qsl],
                                                             op0=ALU.mult, op1=ALU.mult), r=[yc, ga, Gh[s]], w=[yg])
                P.dma("sp", S["ygl"][2 + h][:, qsl], yg[:, :], yg.ds, r=[yg])
                if h == 0:
                    P.op("pool", lambda E: E.tensor_tensor(out=SQacc[:, qsl], in0=yc[:, :], in1=yc[:, :], op=ALU.mult), r=[yc], w=[SQacc])
                else:
                    P.op("pool", lambda E: E.tensor_tensor(out=sq[:, :], in0=yc[:, :], in1=yc[:, :], op=ALU.mult), r=[yc], w=[sq])
                    dst = SQb if h == 3 else SQacc
                    P.op("pool", lambda E: E.tensor_tensor(out=dst[:, qsl], in0=sq[:, :], in1=SQacc[:, qsl], op=ALU.add), r=[sq, SQacc], w=[dst])

        load_head(0)
        load_head(1)
        for ti in range(T + 3):
            if 0 <= ti - 3 < T:
                stage_c2(ti - 3)
                hh, qq, jj, ff, ll = tiles[ti - 3]
                if ff and qq == 0 and 1 <= hh <= 2:
                    load_head(hh + 1)
            if ti < T:
                stage_a(ti)
            if 0 <= ti - 1 < T:
                stage_b(ti - 1)
            if 0 <= ti - 2 < T:
                stage_c1(ti - 2)
        for ch in range(NCH):
            ps = P.psum[ch % 2]
            csl = slice(ch * 512, (ch + 1) * 512)
            P.op("pe", lambda E, ps=ps, csl=csl: E.matmul(ps[:, :], lhsT=ones, rhs=SQb[:, csl], start=True, stop=True), r=[mats, SQb], w=[ps])
            P.op("dve", lambda E, ps=ps, csl=csl: E.tensor_copy(out=srow[0:1, csl], in_=ps[0:1, :]), r=[ps], w=[srow])
        ev = P.dma("sp", S["bsl"][1:2, :], srow[0:1, :], srow.ds, r=[srow])
        if self.use_cc:
            evs = [(t.ds, t.ds.n) for t in Yg if t.ds.n > 0] + [ev]
            for a in (2, 3, 4, 5):
                P.allgather(S["ygl"][a].ap(), S["ygg"][a].ap(), P.ag_sem("ag_b"), after=evs)
            P.allgather(S["bsl"].ap(), S["bsg"].ap(), P.ag_sem("ag_b"), after=evs)
        P.end()

    def ph_ssm(self, lam, ldt, BA, BAs, C1, C2, dskip):
        P, L, NCH, S = self.P, self.L, self.NCH, self.S
        NLEV = 10
        P.begin()
        TWO_PI = 2.0 * math.pi
        MAG = 12582912.0
        CW1 = 6.28125
        CW2 = float(TWO_PI - 6.28125)
        lamt = P.tile("lamt", [128, 32], F32)
        ldtt = P.tile("ldtt", [128, 16], F32)
        dsk = P.tile("dsk", [128, 2], F32)
        for t in (lamt, ldtt, dsk):
            t.ds = P.new_ds()
        P.dma("sp", lamt[:, :], lam[:, :], lamt.ds, w=[lamt])
        P.dma("sp", ldtt[:, :], ldt[:, :], ldtt.ds, w=[ldtt])
        P.dma("sp", dsk[:, :], dskip[:, :], dsk.ds, w=[dsk])
        sg = P.tile("sg", [128, 2], F32)
        P.op("dve", lambda E: E.memset(sg[0:64, 0:1], 1.0), w=[sg])
        P.op("dve", lambda E: E.memset(sg[64:128, 0:1], -1.0), w=[sg])
        P.op("dve", lambda E: E.memset(sg[0:64, 1:2], -1.0), w=[sg])
        P.op("dve", lambda E: E.memset(sg[64:128, 1:2], 1.0), w=[sg])
        names = ["dt", "are", "th", "r", "k", "phs", "phc", "c1", "s1", "nr", "ni", "den", "inv", "t1", "t2", "cre", "cim", "a2", "b2"]
        q = {n: P.tile("q_" + n, [128, 16], F32) for n in names}
        CK = [P.tile(f"CK{k}", [128, 16], F32) for k in range(NLEV)]
        SK = [P.tile(f"SK{k}", [128, 16], F32) for k in range(NLEV)]
        NSK = [P.tile(f"NSK{k}", [128, 16], F32) for k in range(NLEV)]
        lre, lim = lamt[:, 0:16], lamt[:, 16:32]

        def dve(fn, r, w):
            P.op("dve", fn, r=r, w=w)

        def act(fn, r, w):
            P.op("act", fn, r=r, w=w)

        import os
        SCUT = int(os.environ.get("SSM_CUT", "9"))
        if SCUT < 1:
            P.end()
            return
        act(lambda E: E.activation(out=q["dt"][:, :], in_=ldtt[:, :], func=AF.Exp), [ldtt], [q["dt"]])
        dve(lambda E: E.tensor_tensor(out=q["are"][:, :], in0=lre, in1=q["dt"][:, :], op=ALU.mult), [lamt, q["dt"]], [q["are"]])
        dve(lambda E: E.tensor_tensor(out=q["th"][:, :], in0=lim, in1=q["dt"][:, :], op=ALU.mult), [lamt, q["dt"]], [q["th"]])
        act(lambda E: E.activation(out=q["r"][:, :], in_=q["are"][:, :], func=AF.Exp), [q["are"]], [q["r"]])

        def reduce_angle(dst, shift):
            dve(lambda E: E.tensor_scalar(out=q["t1"][:, :], in0=q["th"][:, :], scalar1=shift, scalar2=None, op0=ALU.add), [q["th"]], [q["t1"]])
            dve(lambda E: E.tensor_scalar(out=q["k"][:, :], in0=q["t1"][:, :], scalar1=float(1.0 / TWO_PI), scalar2=MAG, op0=ALU.mult, op1=ALU.add),
                [q["t1"]], [q["k"]])
            dve(lambda E: E.tensor_single_scalar(out=q["k"][:, :], in_=q["k"][:, :], scalar=-MAG, op=ALU.add), [q["k"]], [q["k"]])
            dve(lambda E: E.scalar_tensor_tensor(out=q["t1"][:, :], in0=q["k"][:, :], scalar=-CW1, in1=q["t1"][:, :], op0=ALU.mult, op1=ALU.add),
                [q["k"], q["t1"]], [q["t1"]])
            dve(lambda E: E.scalar_tensor_tensor(out=dst[:, :], in0=q["k"][:, :], scalar=-CW2, in1=q["t1"][:, :], op0=ALU.mult, op1=ALU.add),
                [q["k"], q["t1"]], [dst])

        reduce_angle(q["phs"], 0.0)
        reduce_angle(q["phc"], float(math.pi / 2))
        act(lambda E: E.activation(out=q["s1"][:, :], in_=q["phs"][:, :], func=AF.Sin), [q["phs"]], [q["s1"]])
        act(lambda E: E.activation(out=q["c1"][:, :], in_=q["phc"][:, :], func=AF.Sin), [q["phc"]], [q["c1"]])
        dve(lambda E: E.tensor_tensor(out=q["nr"][:, :], in0=q["r"][:, :], in1=q["c1"][:, :], op=ALU.mult), [q["r"], q["c1"]], [q["nr"]])
        dve(lambda E: E.tensor_single_scalar(out=q["nr"][:, :], in_=q["nr"][:, :], scalar=-1.0, op=ALU.add), [q["nr"]], [q["nr"]])
        dve(lambda E: E.tensor_tensor(out=q["ni"][:, :], in0=q["r"][:, :], in1=q["s1"][:, :], op=ALU.mult), [q["r"], q["s1"]], [q["ni"]])
        dve(lambda E: E.tensor_tensor(out=q["den"][:, :], in0=lre, in1=lre, op=ALU.mult), [lamt], [q["den"]])
        dve(lambda E: E.tensor_tensor(out=q["t1"][:, :], in0=lim, in1=lim, op=ALU.mult), [lamt], [q["t1"]])
        dve(lambda E: E.tensor_tensor(out=q["den"][:, :], in0=q["den"][:, :], in1=q["t1"][:, :], op=ALU.add), [q["den"], q["t1"]], [q["den"]])
        dve(lambda E: E.reciprocal(out=q["inv"][:, :], in_=q["den"][:, :]), [q["den"]], [q["inv"]])
        dve(lambda E: E.tensor_tensor(out=q["t1"][:, :], in0=q["nr"][:, :], in1=lre, op=ALU.mult), [q["nr"], lamt], [q["t1"]])
        dve(lambda E: E.tensor_tensor(out=q["t2"][:, :], in0=q["ni"][:, :], in1=lim, op=ALU.mult), [q["ni"], lamt], [q["t2"]])
        dve(lambda E: E.tensor_tensor(out=q["t1"][:, :], in0=q["t1"][:, :], in1=q["t2"][:, :], op=ALU.add), [q["t1"], q["t2"]], [q["t1"]])
        dve(lambda E: E.tensor_tensor(out=q["cre"][:, :], in0=q["t1"][:, :], in1=q["inv"][:, :], op=ALU.mult), [q["t1"], q["inv"]], [q["cre"]])
        dve(lambda E: E.tensor_tensor(out=q["t1"][:, :], in0=q["ni"][:, :], in1=lre, op=ALU.mult), [q["ni"], lamt], [q["t1"]])
        dve(lambda E: E.tensor_tensor(out=q["t2"][:, :], in0=q["nr"][:, :], in1=lim, op=ALU.mult), [q["nr"], lamt], [q["t2"]])
        dve(lambda E: E.tensor_tensor(out=q["t1"][:, :], in0=q["t1"][:, :], in1=q["t2"][:, :], op=ALU.subtract), [q["t1"], q["t2"]], [q["t1"]])
        dve(lambda E: E.tensor_tensor(out=q["cim"][:, :], in0=q["t1"][:, :], in1=q["inv"][:, :], op=ALU.mult), [q["t1"], q["inv"]], [q["cim"]])
        dve(lambda E: E.tensor_scalar(out=q["a2"][:, :], in0=q["cim"][:, :], scalar1=sg[:, 1:2], scalar2=None, op0=ALU.mult), [q["cim"], sg], [q["a2"]])
        dve(lambda E: E.tensor_scalar(out=q["b2"][:, :], in0=q["cre"][:, :], scalar1=sg[:, 0:1], scalar2=None, op0=ALU.mult), [q["cre"], sg], [q["b2"]])
        dve(lambda E: E.tensor_copy(out=CK[0][:, :], in_=q["c1"][:, :]), [q["c1"]], [CK[0]])
        dve(lambda E: E.tensor_copy(out=SK[0][:, :], in_=q["s1"][:, :]), [q["s1"]], [SK[0]])
        for k in range(NLEV):
            dve(lambda E, k=k: E.tensor_single_scalar(out=NSK[k][:, :], in_=SK[k][:, :], scalar=-1.0, op=ALU.mult), [SK[k]], [NSK[k]])
            if k + 1 < NLEV:
                dve(lambda E, k=k: E.tensor_tensor(out=q["t1"][:, :], in0=CK[k][:, :], in1=CK[k][:, :], op=ALU.mult), [CK[k]], [q["t1"]])
                dve(lambda E, k=k: E.tensor_tensor(out=q["t2"][:, :], in0=SK[k][:, :], in1=SK[k][:, :], op=ALU.mult), [SK[k]], [q["t2"]])
                dve(lambda E, k=k: E.tensor_tensor(out=CK[k + 1][:, :], in0=q["t1"][:, :], in1=q["t2"][:, :], op=ALU.subtract), [q["t1"], q["t2"]], [CK[k + 1]])
                dve(lambda E, k=k: E.tensor_tensor(out=q["t1"][:, :], in0=CK[k][:, :], in1=SK[k][:, :], op=ALU.mult), [CK[k], SK[k]], [q["t1"]])
                dve(lambda E, k=k: E.tensor_single_scalar(out=SK[k + 1][:, :], in_=q["t1"][:, :], scalar=2.0, op=ALU.mult), [q["t1"]], [SK[k + 1]])
        TC = 512
        NL = 9
        isw = P.tile("isw", [128, 2, 128], F32)
        isw.ds = P.new_ds()
        P.dma("sp", isw[:, 0, :], self.c_mats[:, 0:128], isw.ds, w=[isw])
        P.dma("sp", isw[:, 1, :], self.c_mats[:, 384:512], isw.ds, w=[isw])
        s9s = P.tile("s9s", [128, 16], F32)
        dve(lambda E: E.tensor_scalar(out=s9s[:, :], in0=SK[NL][:, :], scalar1=sg[:, 0:1], scalar2=None, op0=ALU.mult), [SK[NL], sg], [s9s])
        TB = [P.tile(f"TB{i}", [128, 4, TC], F32) for i in range(8)]
        Rts = [P.tile(f"Rt{i}", [128, TC], F32) for i in range(8)]
        RotF = P.tile("RotF", [128, 128], F32)
        RotH = [P.tile(f"RotH{i}", [128, 128], BF16) for i in range(8)]
        RotL = [P.tile(f"RotL{i}", [128, 128], BF16) for i in range(8)]
        winit = [P.tile(f"winit{i}", [128, 1], F32) for i in range(8)]
        whl = [P.tile(f"whl{i}", [128, 2], BF16) for i in range(4)]
        onesf = P.tile("onesf", [128, TC], F32)
        P.op("dve", lambda E: E.memset(onesf[:, :], 1.0), w=[onesf])
        U = P.tile("U", [128, L], BF16)
        U.ds = P.new_ds()
        mats = {n: P.tile("m_" + n, [128, 8, 128], BF16) for n in ("BA", "BAs", "C1", "C2")}
        for n, t in mats.items():
            t.ds = P.pds("m_" + n)
            P.op("dve", lambda E, t=t: E.memset(t[:, :, :], 0.0), w=[t])
        NR = 4
        v1 = [P.tile(f"v1_{i}", [128, TC], F32) for i in range(NR)]
        v2 = [P.tile(f"v2_{i}", [128, TC], F32) for i in range(NR)]
        Vt = [P.tile(f"V_{i}", [128, TC], F32) for i in range(NR)]
        Wc = [P.tile(f"Wc_{i}", [128, TC], F32) for i in range(NR)]
        P1 = [P.tile(f"P1_{i}", [128, TC], BF16) for i in range(NR)]
        P2 = [P.tile(f"P2_{i}", [128, TC], BF16) for i in range(NR)]
        yb = [P.tile(f"yb_{i}", [128, TC], F32) for i in range(2)]
        hb = [P.tile(f"hb_{i}", [128, TC], BF16) for i in range(2)]
        for t in hb:
            t.ds = P.new_ds()
        psR = P.psum[6]
        it = 0
        for jt in range(2):
            P.dma("sp", U[:, :], S["UT"][jt * 128:(jt + 1) * 128, :], U.ds, w=[U])
            for n, srcs in (("BA", BA), ("BAs", BAs), ("C1", C1), ("C2", C2)):
                t = mats[n]
                P._waits("pool", [], [t.b], [])
                for gi in range(8):
                    g = jt * 8 + gi
                    if n in ("BA", "BAs"):
                        P.dma("pool", t[16 * gi:16 * gi + 16, gi, :], srcs[g], t.ds)
                    else:
                        P.dma("pool", t[:, gi, 16 * gi:16 * gi + 16], srcs[g], t.ds)
                t.b.w = {t.ds: t.ds.n}
                t.b.r = {}
            for gi in range(8):
                g = jt * 8 + gi
                gs = slice(g, g + 1)
                tb = TB[gi]
                cb, sb = Buf("cb"), Buf("sb")
                dve(lambda E, tb=tb: E.memset(tb[:, 0, 0:1], 1.0), [], [tb])
                dve(lambda E, tb=tb: E.memset(tb[:, 1, 0:1], 0.0), [], [tb])
                for k in range(NL):
                    n = 1 << k
                    dve(lambda E, k=k, n=n, gs=gs, tb=tb: E.tensor_scalar(out=tb[:, 0, n:2 * n], in0=tb[:, 0, 0:n], scalar1=CK[k][:, gs], scalar2=None, op0=ALU.mult),
                        [tb, CK[k]], [cb])
                    dve(lambda E, k=k, n=n, gs=gs, tb=tb: E.tensor_scalar(out=tb[:, 1, n:2 * n], in0=tb[:, 1, 0:n], scalar1=CK[k][:, gs], scalar2=None, op0=ALU.mult),
                        [tb, CK[k]], [sb])
                    dve(lambda E, k=k, n=n, gs=gs, tb=tb: E.scalar_tensor_tensor(out=tb[:, 0, n:2 * n], in0=tb[:, 1, 0:n], scalar=NSK[k][:, gs], in1=tb[:, 0, n:2 * n],
                                                                               op0=ALU.mult, op1=ALU.add), [tb, NSK[k], cb], [cb])
                    dve(lambda E, k=k, n=n, gs=gs, tb=tb: E.scalar_tensor_tensor(out=tb[:, 1, n:2 * n], in0=tb[:, 0, 0:n], scalar=SK[k][:, gs], in1=tb[:, 1, n:2 * n],
                                                                               op0=ALU.mult, op1=ALU.add), [tb, SK[k], sb], [sb])
                    tb.b.w = dict(cb.w)
                    tb.b.w.update(sb.w)
                    tb.b.r = {}
                dve(lambda E, gs=gs, tb=tb: E.tensor_scalar(out=tb[:, 2, :], in0=tb[:, 0, :], scalar1=q["cre"][:, gs], scalar2=None, op0=ALU.mult), [tb, q["cre"]], [tb])
                dve(lambda E, gs=gs, tb=tb: E.scalar_tensor_tensor(out=tb[:, 2, :], in0=tb[:, 1, :], scalar=q["cim"][:, gs], in1=tb[:, 2, :], op0=ALU.mult, op1=ALU.add),
                    [tb, q["cim"]], [tb])
                dve(lambda E, gs=gs, tb=tb: E.tensor_scalar(out=tb[:, 3, :], in0=tb[:, 0, :], scalar1=q["a2"][:, gs], scalar2=None, op0=ALU.mult), [tb, q["a2"]], [tb])
                dve(lambda E, gs=gs, tb=tb: E.scalar_tensor_tensor(out=tb[:, 3, :], in0=tb[:, 1, :], scalar=q["b2"][:, gs], in1=tb[:, 3, :], op0=ALU.mult, op1=ALU.add),
                    [tb, q["b2"]], [tb])
                dve(lambda E, tb=tb: E.tensor_scalar(out=tb[:, 0, :], in0=tb[:, 0, :], scalar1=sg[:, 0:1], scalar2=None, op0=ALU.mult), [tb, sg], [tb])
                dve(lambda E, tb=tb: E.tensor_single_scalar(out=tb[:, 1, :], in_=tb[:, 1, :], scalar=-1.0, op=ALU.mult), [tb], [tb])
                dve(lambda E, gs=gs, gi=gi: E.tensor_scalar(out=Rts[gi][:, :], in0=onesf[:, :], scalar1=q["r"][:, gs], scalar2=None, op0=ALU.mult), [onesf, q["r"]], [Rts[gi]])
                dve(lambda E, gs=gs: E.tensor_scalar(out=RotF[:, :], in0=isw[:, 0, :], scalar1=CK[NL][:, gs], scalar2=None, op0=ALU.mult), [isw, CK[NL]], [RotF])
                dve(lambda E, gs=gs: E.scalar_tensor_tensor(out=RotF[:, :], in0=isw[:, 1, :], scalar=s9s[:, gs], in1=RotF[:, :], op0=ALU.mult, op1=ALU.add),
                    [isw, s9s, RotF], [RotF])
                dve(lambda E, gi=gi: E.tensor_copy(out=RotH[gi][:, :], in_=RotF[:, :]), [RotF], [RotH[gi]])
                dve(lambda E, gi=gi: E.tensor_tensor(out=RotL[gi][:, :], in0=RotF[:, :], in1=RotH[gi][:, :], op=ALU.subtract), [RotF, RotH[gi]], [RotL[gi]])
            units = []
            for ch in range(NCH):
                for gi in range(8):
                    units.append((ch, gi, it))
                    it += 1

            def s1(ch, gi, itx):
                csl = slice(ch * TC, (ch + 1) * TC)
                psA, psA2 = P.psum[itx % 2], P.psum[2 + itx % 2]
                a, b, V_ = v1[itx % NR], v2[itx % NR], Vt[itx % NR]
                tb = TB[gi]
                P.op("pe", lambda E: E.matmul(psA[:, :], lhsT=mats["BA"][:, gi, :], rhs=U[:, csl], start=True, stop=True),
                     r=[mats["BA"], U], w=[psA])
                P.op("pe", lambda E: E.matmul(psA2[:, :], lhsT=mats["BAs"][:, gi, :], rhs=U[:, csl], start=True, stop=True),
                     r=[mats["BAs"], U], w=[psA2])
                dve(lambda E: E.tensor_tensor(out=a[:, :], in0=psA[:, :], in1=tb[:, 2, :], op=ALU.mult), [psA, tb], [a])
                dve(lambda E: E.tensor_tensor(out=b[:, :], in0=psA2[:, :], in1=tb[:, 3, :], op=ALU.mult), [psA2, tb], [b])
                P.op("pool", lambda E: E.tensor_tensor(out=V_[:, :], in0=a[:, :], in1=b[:, :], op=ALU.add), r=[a, b], w=[V_])

            def s2(ch, gi, itx, jt=jt):
                psY = P.psum[4 + ch % 2]
                V_, W_, p1, p2 = Vt[itx % NR], Wc[itx % NR], P1[itx % NR], P2[itx % NR]
                tb = TB[gi]
                wi = winit[gi]
                if ch == 0:
                    dve(lambda E: E.tensor_tensor_scan(out=W_[:, :], data0=Rts[gi][:, :], data1=V_[:, :], initial=0.0,
                                                       op0=ALU.mult, op1=ALU.add), [Rts[gi], V_], [W_])
                else:
                    dve(lambda E: E.tensor_tensor_scan(out=W_[:, :], data0=Rts[gi][:, :], data1=V_[:, :], initial=wi[:, 0:1],
                                                       op0=ALU.mult, op1=ALU.add), [Rts[gi], V_, wi], [W_])
                P.op("pool", lambda E: E.tensor_tensor(out=p1[:, :], in0=W_[:, :], in1=tb[:, 0, :], op=ALU.mult), r=[W_, tb], w=[p1])
                P.op("pool", lambda E: E.tensor_tensor(out=p2[:, :], in0=W_[:, :], in1=tb[:, 1, :], op=ALU.mult), r=[W_, tb], w=[p2])
                P.op("pe", lambda E: E.matmul(psY[:, :], lhsT=mats["C1"][:, gi, :], rhs=p1[:, :], start=(gi == 0), stop=False),
                     r=[mats["C1"], p1], w=[psY], sig=False)
                P.op("pe", lambda E: E.matmul(psY[:, :], lhsT=mats["C2"][:, gi, :], rhs=p2[:, :], start=False, stop=(gi == 7)),
                     r=[mats["C2"], p2], w=[psY], sig=True)
                if ch + 1 < NCH:
                    hl = whl[itx % 4]
                    dve(lambda E: E.tensor_copy(out=hl[:, 0:1], in_=W_[:, TC - 1:TC]), [W_], [hl])
                    dve(lambda E: E.tensor_tensor(out=hl[:, 1:2], in0=W_[:, TC - 1:TC], in1=hl[:, 0:1], op=ALU.subtract), [W_, hl], [hl])
                    P.op("pe", lambda E: E.matmul(psR[:, gi:gi + 1], lhsT=RotH[gi][:, :], rhs=hl[:, 0:1], start=True, stop=False),
                         r=[RotH[gi], hl], w=[psR], sig=False)
                    P.op("pe", lambda E: E.matmul(psR[:, gi:gi + 1], lhsT=RotH[gi][:, :], rhs=hl[:, 1:2], start=False, stop=False),
                         r=[RotH[gi], hl], w=[psR], sig=False)
                    P.op("pe", lambda E: E.matmul(psR[:, gi:gi + 1], lhsT=RotL[gi][:, :], rhs=hl[:, 0:1], start=False, stop=True),
                         r=[RotL[gi], hl], w=[psR], sig=True)
                    dve(lambda E: E.tensor_copy(out=wi[:, 0:1], in_=psR[:, gi:gi + 1]), [psR], [wi])
                if gi == 7:
                    csl = slice(ch * TC, (ch + 1) * TC)
                    y_, h_ = yb[ch % 2], hb[ch % 2]
                    dve(lambda E: E.tensor_copy(out=y_[:, :], in_=psY[:, :]), [psY], [y_])
                    dve(lambda E: E.scalar_tensor_tensor(out=y_[:, :], in0=U[:, csl], scalar=dsk[:, jt:jt + 1], in1=y_[:, :],
                                                         op0=ALU.mult, op1=ALU.add), [U, dsk, y_], [y_])
                    act(lambda E: E.activation(out=h_[:, :], in_=y_[:, :], func=AF.Gelu_apprx_tanh), [y_], [h_])
                    P.dma("sp", S["ygl"][6 + jt][:, csl], h_[:, :], h_.ds, r=[h_])

            nu = len(units)
            for k in range(nu + 1):
                if k < nu:
                    s1(*units[k])
                if 0 <= k - 1 < nu:
                    s2(*units[k - 1])
        if self.use_cc:
            evs = [(t.ds, t.ds.n) for t in hb if t.ds.n > 0]
            for a in (6, 7):
                P.allgather(S["ygl"][a].ap(), S["ygg"][a].ap(), P.ag_sem("ag_a"), after=evs)
        P.end()

    def ph_exchange(self):
        P, S = self.P, self.S
        if not self.use_cc:
            return
        P.begin()
        ds = P.pds("ag")
        for a in range(10):
            P.allgather(S["ygl"][a].ap(), S["ygg"][a].ap(), ds)
        P.allgather(S["bsl"].ap(), S["bsg"].ap(), ds)
        P.end()

    def prefetch_wg(self, wglu):
        P = self.P
        t = self.xstack.enter_context(self.nc.sbuf_tensor(f"Wg_{P.phase_no}", [128, 8, 2048], BF16))
        Wg = Tile(t, "Wg")
        Wg.ds = P.pds("wg")
        for r in range(4):
            for i in range(2):
                row0 = 256 * r + 128 * i
                P.dma("pool", Wg[:, r * 2 + i, :], wglu[row0:row0 + 128, :], Wg.ds)
        self.Wg = Wg

    def prefetch_wo_alloc(self):
        P = self.P
        t = self.xstack.enter_context(self.nc.sbuf_tensor(f"Wo_{P.phase_no}", [128, 32, 1024], BF16))
        self.Wo_t = Tile(t, "Wo")

    def prefetch_wo(self, wout):
        P = self.P
        if getattr(self, "Wo_t", None) is None:
            self.prefetch_wo_alloc()
        Wo = self.Wo_t
        self.Wo_t = None
        Wo.ds = P.pds("wo")
        kt = 0
        for a in range(2):
            for r in range(4):
                row0 = 256 * r + 128 * a
                P.dma("pool", Wo[:, kt, :], wout[row0:row0 + 128, :], Wo.ds)
                kt += 1
        for i in range(4):
            for r in range(4):
                row0 = 1024 + 512 * r + 128 * i
                P.dma("pool", Wo[:, kt, :], wout[row0:row0 + 128, :], Wo.ds)
                kt += 1
        for nt in range(8):
            row0 = 3072 + 128 * nt
            P.dma("pool", Wo[:, kt, :], wout[row0:row0 + 128, :], Wo.ds)
            kt += 1
        self.Wo = Wo

    def ph_glu(self, wglu, bglu, gssm, YS, wout_next=None):
        P, L, NCH, S = self.P, self.L, self.NCH, self.S
        P.begin()
        if wout_next is not None:
            self.prefetch_wo_alloc()
        ones = P.tile("ones", [128, 128], BF16)
        ones.ds = P.pds("ones")
        P.dma("pool", ones[:, :], self.c_mats[:, 256:384], ones.ds, w=[ones])
        Wg = P.tile("Wg", [128, 8, 2048], BF16)
        Wg.ds = P.pds("wg")
        for r in range(4):
            for i in range(2):
                row0 = 256 * r + 128 * i
                P.dma("pool", Wg[:, r * 2 + i, :], wglu[row0:row0 + 128, :], Wg.ds)
        Wg.b.w = {Wg.ds: Wg.ds.n}
        if wout_next is not None:
            self.prefetch_wo(wout_next)
        sm = P.tile("sm", [128, 24], F32)
        sm.ds = P.new_ds()
        P.dma("sp", sm[:, 0:16], bglu[:, :], sm.ds, w=[sm])
        P.dma("sp", sm[:, 16:24], gssm[:, :], sm.ds, w=[sm])
        HGs = [P.tile(f"HG{i}", [128, 4, 2, 512], BF16) for i in range(2)]
        SGs = [P.tile(f"SG{i}", [128, 4, 2, 512], BF16) for i in range(2)]
        ys = [P.tile(f"ys{i}", [128, 8, 512], F32) for i in range(2)]
        sgm = [P.tile(f"sgm{i}", [128, 512], F32) for i in range(2)]
        sqb = [P.tile(f"sqb{i}", [128, 512], BF16) for i in range(2)]
        Rs = [P.tile(f"Rs{i}", [128, 512], F32) for i in range(2)]
        tmp = [P.tile(f"tmp{i}", [128, 512], F32) for i in range(2)]
        ob = [P.tile(f"ob{i}", [128, 512], BF16) for i in range(4)]
        for t in HGs + SGs + ob:
            t.ds = P.new_ds()

        if self.use_cc:
            P.wait_all("sp", P.ag_sem("ag_a"))

        def load(ch):
            csl = slice(ch * 512, (ch + 1) * 512)
            for i in range(2):
                P.dma("sp", HGs[ch % 2][:, :, i, :], S["ygg"][6 + i][:, csl].rearrange("(r p) t -> p r t", p=128), HGs[ch % 2].ds, w=[HGs[ch % 2]])
                P.dma("sp", SGs[ch % 2][:, :, i, :], S["ygg"][8 + i][:, csl].rearrange("(r p) t -> p r t", p=128), SGs[ch % 2].ds, w=[SGs[ch % 2]])

        load(0)
        it = 0
        oi = 0
        for ch in range(NCH):
            if ch + 1 < NCH:
                load(ch + 1)
            csl = slice(ch * 512, (ch + 1) * 512)
            HG, SG, Y, R = HGs[ch % 2], SGs[ch % 2], ys[ch % 2], Rs[ch % 2]
            pss = P.psum[4 + ch % 2]
            for nt in range(8):
                psv, psg = P.psum[it % 2], P.psum[2 + it % 2]
                g_, q_ = sgm[it % 2], sqb[it % 2]
                it += 1
                for kt in range(8):
                    r, i = kt // 2, kt % 2
                    P.op("pe", lambda E, psv=psv, kt=kt, nt=nt, r=r, i=i, HG=HG: E.matmul(
                        psv[:, :], lhsT=Wg[:, kt, nt * 128:(nt + 1) * 128], rhs=HG[:, r, i, :], start=(kt == 0), stop=(kt == 7)),
                        r=[Wg, HG], w=[psv], sig=(kt == 7))
                for kt in range(8):
                    r, i = kt // 2, kt % 2
                    P.op("pe", lambda E, psg=psg, kt=kt, nt=nt, r=r, i=i, HG=HG: E.matmul(
                        psg[:, :], lhsT=Wg[:, kt, 1024 + nt * 128:1024 + (nt + 1) * 128], rhs=HG[:, r, i, :], start=(kt == 0), stop=(kt == 7)),
                        r=[Wg, HG], w=[psg], sig=(kt == 7))
                P.op("act", lambda E, g_=g_, psg=psg, nt=nt: E.activation(out=g_[:, :], in_=psg[:, :], func=AF.Sigmoid, bias=sm[:, 8 + nt:9 + nt]),
                     r=[psg, sm], w=[g_])
                P.op("dve", lambda E, Y=Y, nt=nt, psv=psv, g_=g_: E.scalar_tensor_tensor(out=Y[:, nt, :], in0=psv[:, :], scalar=sm[:, nt:nt + 1], in1=g_[:, :],
                                                                                     op0=ALU.add, op1=ALU.mult), r=[psv, sm, g_], w=[Y])
                P.op("act", lambda E, q_=q_, Y=Y, nt=nt: E.activation(out=q_[:, :], in_=Y[:, nt, :], func=AF.Square), r=[Y], w=[q_])
                P.op("pe", lambda E, pss=pss, q_=q_, nt=nt: E.matmul(pss[:, :], lhsT=ones[:, :], rhs=q_[:, :], start=(nt == 0), stop=(nt == 7)),
                     r=[ones, q_], w=[pss], sig=True)
            P.op("dve", lambda E, R=R, pss=pss: E.tensor_scalar(out=R[:, :], in0=pss[:, :], scalar1=1.0 / 1024, scalar2=EPS, op0=ALU.mult, op1=ALU.add),
                 r=[pss], w=[R])
            P.op("act", lambda E, R=R: E.activation(out=R[:, :], in_=R[:, :], func=AF.Sqrt), r=[R], w=[R])
            P.op("dve", lambda E, R=R: E.reciprocal(out=R[:, :], in_=R[:, :]), r=[R], w=[R])
            for nt in range(8):
                r, i = nt // 2, nt % 2
                t_ = tmp[nt % 2]
                o_ = ob[oi % 4]
                oi += 1
                P.op("dve", lambda E, t_=t_, Y=Y, nt=nt, SG=SG, r=r, i=i: E.scalar_tensor_tensor(
                    out=t_[:, :], in0=Y[:, nt, :], scalar=sm[:, 16 + nt:17 + nt], in1=SG[:, r, i, :], op0=ALU.mult, op1=ALU.mult),
                    r=[Y, sm, SG], w=[t_])
                P.op("pool", lambda E, o_=o_, t_=t_, R=R: E.tensor_tensor(out=o_[:, :], in0=t_[:, :], in1=R[:, :], op=ALU.mult), r=[t_, R], w=[o_])
                P.dma("sp", YS[nt * 128:(nt + 1) * 128, csl], o_[:, :], o_.ds, r=[o_])
        P.end()

    def ph_outproj(self, wout, YS, xin, xout):
        P, L, NCH, S = self.P, self.L, self.NCH, self.S
        P.begin()
        if getattr(self, "Wo", None) is None:
            self.prefetch_wo(wout)
        Wo = self.Wo
        self.Wo = None
        Wo.b = Buf("Wo")
        Wo.b.w = {Wo.ds: Wo.ds.n}
        sel = P.tile("sel", [8, 256], BF16)
        sel.ds = P.pds("w0")
        P.dma("pool", sel[:, :], self.c_sel[:, :], sel.ds, w=[sel])
        st8 = [P.tile(f"st8_{i}", [8, 512], F32) for i in range(2)]
        shi = [P.tile(f"shi{i}", [8, 512], BF16) for i in range(2)]
        slo = [P.tile(f"slo{i}", [8, 512], BF16) for i in range(2)]
        for t in st8:
            t.ds = P.new_ds()
        A24 = [P.tile(f"A24_{i}", [128, 6, 4, 512], BF16) for i in range(2)]
        YSc = [P.tile(f"YSc{i}", [128, 8, 512], BF16) for i in range(2)]
        Rb = [[P.tile(f"Rb{b}_{i}", [128, 512], F32) for i in range(2)] for b in range(2)]
        xt = [P.tile(f"xt{i}", [128, 512], F32) for i in range(4)]
        xo = [P.tile(f"xo{i}", [128, 512], F32) for i in range(4)]
        for t in A24 + YSc + xt + xo:
            t.ds = P.new_ds()

        if self.use_cc:
            P.wait_all("sp", P.ag_sem("ag_a"))
            P.wait_all("sp", P.ag_sem("ag_b"))

        def load(ch):
            csl = slice(ch * 512, (ch + 1) * 512)
            A = A24[ch % 2]
            for a in range(6):
                P.dma("sp", A[:, a, :, :], S["ygg"][a][:, csl].rearrange("(r p) t -> p r t", p=128), A.ds, w=[A])
            Y = YSc[ch % 2]
            P.dma("sp", Y[:, :, :], YS[:, csl].rearrange("(n p) t -> p n t", p=128), Y.ds, w=[Y])

        load(0)
        xi = 0
        for ch in range(NCH):
            if ch + 1 < NCH:
                load(ch + 1)
            csl = slice(ch * 512, (ch + 1) * 512)
            A, Y = A24[ch % 2], YSc[ch % 2]
            s8, hi8, lo8 = st8[ch % 2], shi[ch % 2], slo[ch % 2]
            P.dma("sp", s8[:, :], S["bsg"][:, csl], s8.ds, w=[s8])
            P.op("dve", lambda E, s8=s8, hi8=hi8: E.tensor_copy(out=hi8[:, :], in_=s8[:, :]), r=[s8], w=[hi8])
            P.op("dve", lambda E, s8=s8, hi8=hi8, lo8=lo8: E.tensor_tensor(out=lo8[:, :], in0=s8[:, :], in1=hi8[:, :], op=ALU.subtract), r=[s8, hi8], w=[lo8])
            for b in range(2):
                ps = P.psum[6 + b]
                R = Rb[b][ch % 2]
                n_b = 1024.0 if b == 0 else 2048.0
                P.op("pe", lambda E, ps=ps, b=b, hi8=hi8: E.matmul(ps[:, :], lhsT=sel[:, b * 128:(b + 1) * 128], rhs=hi8[:, :], start=True, stop=False),
                     r=[sel, hi8], w=[ps], sig=False)
                P.op("pe", lambda E, ps=ps, b=b, lo8=lo8: E.matmul(ps[:, :], lhsT=sel[:, b * 128:(b + 1) * 128], rhs=lo8[:, :], start=False, stop=True),
                     r=[sel, lo8], w=[ps])
                P.op("dve", lambda E, R=R, ps=ps, n_b=n_b: E.tensor_scalar(out=R[:, :], in0=ps[:, :], scalar1=1.0 / n_b, scalar2=EPS, op0=ALU.mult, op1=ALU.add),
                     r=[ps], w=[R])
                P.op("act", lambda E, R=R: E.activation(out=R[:, :], in_=R[:, :], func=AF.Sqrt), r=[R], w=[R])
                P.op("dve", lambda E, R=R: E.reciprocal(out=R[:, :], in_=R[:, :]), r=[R], w=[R])
            cnt = 0
            for a in range(6):
                R = Rb[0 if a < 2 else 1][ch % 2]
                for r in range(4):
                    eng = "dve" if cnt % 2 == 0 else "pool"
                    cnt += 1
                    P.op(eng, lambda E, A=A, a=a, r=r, R=R: E.tensor_tensor(out=A[:, a, r, :], in0=A[:, a, r, :], in1=R[:, :], op=ALU.mult), r=[A, R], w=[A])
            for no in range(8):
                ps = P.psum[no % 4]
                x_, o_ = xt[xi % 4], xo[xi % 4]
                xi += 1
                P.dma("sp", x_[:, :], xin[no * 128:(no + 1) * 128, csl], x_.ds, w=[x_])
                for kt in range(32):
                    if kt < 24:
                        rhs_t, rhs = A, A[:, kt // 4, kt % 4, :]
                    else:
                        rhs_t, rhs = Y, Y[:, kt - 24, :]
                    P.op("pe", lambda E, ps=ps, kt=kt, no=no, rhs=rhs: E.matmul(ps[:, :], lhsT=Wo[:, kt, no * 128:(no + 1) * 128], rhs=rhs,
                                                                            start=(kt == 0), stop=(kt == 31)), r=[Wo, rhs_t], w=[ps], sig=(kt == 31))
                P.op("dve", lambda E, ps=ps, x_=x_, o_=o_: E.tensor_tensor(out=o_[:, :], in0=ps[:, :], in1=x_[:, :], op=ALU.add), r=[ps, x_], w=[o_])
                P.dma("sp", xout[no * 128:(no + 1) * 128, csl], o_[:, :], o_.ds, r=[o_])
        P.end()
        self.xstack.close()
        self.xstack = contextlib.ExitStack()


def build_full(L, depth=DEPTH, use_cc=True):
    m = MK(L, use_cc=use_cc)
    m.consts_dram()
    m.c_sel = m.dram("c_sel", [8, 256], F32, "ExternalInput")
    S = m.scratch()
    ext = lambda n, shp: m.dram(n, shp, F32, "ExternalInput")
    xT = ext("xT", [1024, L])
    poolsel = ext("poolsel", [128, 4])
    poolselT = ext("poolselT", [4, 128])
    fing = ext("fing", [128, 8])
    outT = m.dram("outT", [1024, L], F32, "ExternalOutput")
    stl = m.dram("stl", [1, L], F32)
    stg = m.dram("stg", [4, L], F32)
    htl = [m.dram(f"htl{i}", [128, L], BF16) for i in range(8)]
    htg = [m.dram(f"htg{i}", [512, L], BF16) for i in range(8)]
    YS = m.dram("YS", [1024, L], BF16)
    XT = [m.dram(f"XT{l}", [1024, L], F32) for l in range(depth)]
    xin = xT
    for l in range(depth):
        lng = ext(f"lng{l}", [128, 8])
        win = ext(f"win{l}", [4096, 3072])
        wpool = ext(f"wpool{l}", [256, 256])
        pscale = ext(f"pscale{l}", [128, 2])
        gpool = ext(f"gpool{l}", [128, 2])
        gattn = ext(f"gattn{l}", [128, 4])
        gssm = ext(f"gssm{l}", [128, 8])
        lam = ext(f"ssm_lam{l}", [128, 32])
        ldt = ext(f"ssm_ldt{l}", [128, 16])
        BA = ext(f"ssm_BA{l}", [16, 16, 128])
        BAs = ext(f"ssm_BAs{l}", [16, 16, 128])
        C1 = ext(f"ssm_C1{l}", [16, 128, 16])
        C2 = ext(f"ssm_C2{l}", [16, 128, 16])
        dsk = ext(f"dskip{l}", [128, 2])
        wglu = ext(f"wglu{l}", [1024, 2048])
        bglu = ext(f"bglu{l}", [128, 16])
        wout = ext(f"wout{l}", [4096, 1024])
        m.ph_norm_stats(xin, stl, stg)
        m.ph_norm_apply(xin, lng, stg, htl, htg)
        m.ph_inproj(htg, win)
        m.ph_ssm(lam, ldt, BA, BAs, C1, C2, dsk)
        m.ph_pool(wpool, pscale, gpool, poolsel, poolselT)
        m.ph_attn(gattn)
        m.ph_glu(wglu, bglu, gssm, YS, wout_next=wout)
        m.ph_outproj(wout, YS, xin, XT[l])
        xin = XT[l]
    m.ph_norm_stats(xin, stl, stg)
    m.ph_norm_apply(xin, fing, stg, htl, htg, final_out=outT)
    return m


def host_consts(L):
    ident = np.eye(128, dtype=np.float32)
    negtri = -(np.arange(128)[:, None] >= np.arange(128)[None, :]).astype(np.float32)
    ones = np.ones((128, 128), np.float32)
    swap = np.zeros((128, 128), np.float32)
    swap[np.arange(128), (np.arange(128) + 64) % 128] = 1.0
    mats = np.concatenate([ident, negtri, ones, swap], axis=1)
    mask = np.zeros((128, 4, 512), np.float32)
    for r in range(4):
        mask[:, r, :] = np.where((128 * r + np.arange(128))[:, None] >= np.arange(512)[None, :], NEG, 0.0)
    pos = np.arange(1, L + 1, dtype=np.float32)
    invc = np.stack([1.0 / np.minimum(pos, float(w)) for w in (2, 4, 8, 16)]).astype(np.float32)
    sel = np.zeros((8, 256), np.float32)
    sel[0::2, 0:128] = 1.0
    sel[1::2, 128:256] = 1.0
    return {"c_mats": mats, "c_mask": mask.reshape(128, 2048), "c_invc": invc, "c_sel": sel}


def _pt(v, n):
    return np.ascontiguousarray(np.asarray(v, np.float32).reshape(n, 128).T)


def host_inputs(inp, L, depth=DEPTH):
    cs = host_consts(L)
    maps = []
    for core in range(8):
        b, c = core // 4, core % 4
        d = dict(cs)
        d["xT"] = np.ascontiguousarray(inp["x"][b][:, c * 1024:(c + 1) * 1024].T)
        sel = np.zeros((128, 4), np.float32)
        sel[:, c] = 1.0
        d["poolsel"] = sel
        d["poolselT"] = np.ascontiguousarray(sel.T)
        d["fing"] = _pt(inp["final_g"][c * 1024:(c + 1) * 1024], 8)
        for l in range(depth):
            w_in = inp["w_in"][l]
            cols = np.concatenate([
                np.arange(256 * c, 256 * c + 256), 1024 + np.arange(256 * c, 256 * c + 256),
                2048 + np.arange(512 * c, 512 * c + 512), 4096 + np.arange(512 * c, 512 * c + 512),
                6144 + np.arange(512 * c, 512 * c + 512), 8192 + np.arange(512 * c, 512 * c + 512),
                10240 + np.arange(256 * c, 256 * c + 256), 11264 + np.arange(256 * c, 256 * c + 256)])
            d[f"lng{l}"] = _pt(inp["ln_g"][l][c * 1024:(c + 1) * 1024], 8)
            d[f"win{l}"] = np.ascontiguousarray(w_in[:, cols])
            d[f"wpool{l}"] = np.ascontiguousarray(inp["w_pool"][l][c])
            d[f"pscale{l}"] = _pt(inp["pool_scale"][l][256 * c:256 * c + 256], 2)
            bg = inp["branch_g"][l]
            d[f"gpool{l}"] = _pt(bg[256 * c:256 * c + 256], 2)
            d[f"gattn{l}"] = _pt(bg[1024 + 512 * c:1024 + 512 * c + 512], 4)
            d[f"gssm{l}"] = _pt(bg[3072:4096], 8)
            gs = slice(16 * c, 16 * c + 16)
            lre = inp["lam_re"][l][gs].T
            lim = inp["lam_im"][l][gs].T
            d[f"ssm_lam{l}"] = np.ascontiguousarray(np.concatenate(
                [np.concatenate([lre, lre], 0), np.concatenate([lim, lim], 0)], axis=1).astype(np.float32))
            d[f"ssm_ldt{l}"] = np.ascontiguousarray(np.broadcast_to(inp["log_dt"][l][gs][None, :], (128, 16)).astype(np.float32))
            bre = inp["b_re"][l][gs].transpose(0, 2, 1)
            bim = inp["b_im"][l][gs].transpose(0, 2, 1)
            d[f"ssm_BA{l}"] = np.ascontiguousarray(np.concatenate([bre, bim], axis=2))
            d[f"ssm_BAs{l}"] = np.ascontiguousarray(np.concatenate([bim, bre], axis=2))
            cre = inp["c_re"][l][gs].transpose(0, 2, 1)
            cim = inp["c_im"][l][gs].transpose(0, 2, 1)
            d[f"ssm_C1{l}"] = np.ascontiguousarray(np.concatenate([cre, cim], axis=1))
            d[f"ssm_C2{l}"] = np.ascontiguousarray(np.concatenate([cim, cre], axis=1))
            d[f"dskip{l}"] = _pt(inp["d_skip"][l][256 * c:256 * c + 256], 2)
            d[f"wglu{l}"] = np.ascontiguousarray(inp["w_glu"][l])
            d[f"bglu{l}"] = _pt(inp["b_glu"][l], 16)
            d[f"wout{l}"] = np.ascontiguousarray(inp["w_out"][l][:, c * 1024:(c + 1) * 1024])
        maps.append(d)
    return maps


def kernel(**inputs):
    inp = {k: np.asarray(v) for k, v in inputs.items()}
    L = inp["x"].shape[1]
    m = build_full(L)
    maps = host_inputs(inp, L)
    res = run_bass_kernel_spmd(m.nc, maps, core_ids=list(range(8)))
    out = np.empty((NB, L, D), np.float32)
    for core in range(8):
        b, c = core // 4, core % 4
        out[b][:, c * 1024:(c + 1) * 1024] = res.results[core]["outT"].T
    return out
```

```python
import math
import contextlib
import numpy as np
import ml_dtypes
import concourse.bass as bass
import concourse.mybir as mybir
from concourse.bass_utils import run_bass_kernel_spmd

F32 = mybir.dt.float32
BF16 = mybir.dt.bfloat16
AF = mybir.ActivationFunctionType
ALU = mybir.AluOpType

D = 4096
NB = 2
DEPTH = 2
EPS = 1e-6
GROUPS = [[0, 1, 2, 3], [4, 5, 6, 7]]
NEG = -30000.0
ENGS = ("pe", "act", "dve", "pool", "sp")
CENGS = ("pe", "act", "dve", "pool")


class DSem:
    def __init__(self, h):
        self.h = h
        self.n = 0
        self.persist = False


class Buf:
    def __init__(self, name=""):
        self.name = name
        self.w = {}
        self.r = {}
        self.excl = False


class Tile:
    def __init__(self, t, name):
        self.t = t
        self.b = Buf(name)
        self.ds = None

    def __getitem__(self, k):
        return self.t[k]


class Prog:
    def __init__(self, L):
        self.L = L
        self.nc = bass.Bass("TRN2", target_bir_lowering=False)
        nc = self.nc
        self.glob = contextlib.ExitStack()
        self.esem = {e: nc.alloc_semaphore(name=f"e_{e}") for e in CENGS}
        self.ph_a = nc.alloc_semaphore(name="ph_a")
        self.ph_b = nc.alloc_semaphore(name="ph_b")
        self.dsems = [DSem(nc.alloc_semaphore(name=f"d{i}")) for i in range(90)]
        self.psum = []
        for i in range(8):
            t = self.glob.enter_context(nc.psum_tensor(f"ps{i}", [128, 512], F32))
            self.psum.append(Tile(t, f"ps{i}"))
            self.psum[-1].b.excl = True
        self.phase_no = 0
        self.in_phase = False
        self.persist_lo = len(self.dsems)
        self.pnamed = {}

    def begin(self):
        assert not self.in_phase
        self.in_phase = True
        self.ops = {e: [] for e in ENGS}
        self.cnt = {e: 0 for e in CENGS}
        self.seen = {e: {} for e in ENGS}
        self.ds_next = 0
        for d in self.dsems[: self.persist_lo]:
            d.n = 0
        for p in self.psum:
            p.b.w = {}
            p.b.r = {}
        self.sb = contextlib.ExitStack()

    def pds(self, name):
        if name not in self.pnamed:
            self.pnamed[name] = self.new_ds(persist=True)
        return self.pnamed[name]

    def new_ds(self, persist=False):
        if persist:
            self.persist_lo -= 1
            assert self.persist_lo >= self.ds_next
            self.dsems[self.persist_lo].persist = True
            return self.dsems[self.persist_lo]
        d = self.dsems[self.ds_next]
        self.ds_next += 1
        assert self.ds_next <= self.persist_lo
        return d

    def tile(self, name, shape, dt):
        t = self.sb.enter_context(self.nc.sbuf_tensor(f"{name}_{self.phase_no}", list(shape), dt))
        return Tile(t, name)

    def end(self):
        used = self.dsems[: self.ds_next]
        pers = [d for d in self.dsems[self.persist_lo:] if not getattr(d, "nobarrier", False)]
        for e in ENGS:
            for p in CENGS:
                if self.cnt[p] > 0:
                    self._wait1(e, p, self.cnt[p])
            for d in used + pers:
                if d.n > 0:
                    self._wait1(e, d, d.n)
        self.phase_no += 1
        k = self.phase_no
        for e in ("pe", "act", "dve", "sp"):
            self.ops[e].append(lambda E: E.sem_inc(self.ph_a, 1))
        self.ops["pool"].append(lambda E: E.wait_ge(self.ph_a, 4 * k))
        for p in CENGS:
            self.ops["pool"].append(lambda E, s=self.esem[p]: E.sem_clear(s))
        for d in used:
            self.ops["pool"].append(lambda E, s=d.h: E.sem_clear(s))
        self.ops["pool"].append(lambda E: E.sem_inc(self.ph_b, 1))
        for e in ENGS:
            self.ops[e].append(lambda E: E.wait_ge(self.ph_b, k))
        ops = self.ops
        with self.nc.Block() as block:
            @block.tensor
            def _(E):
                for f in ops["pe"]:
                    f(E)

            @block.scalar
            def _(E):
                for f in ops["act"]:
                    f(E)

            @block.vector
            def _(E):
                for f in ops["dve"]:
                    f(E)

            @block.gpsimd
            def _(E):
                for f in ops["pool"]:
                    f(E)

            @block.sync
            def _(E):
                for f in ops["sp"]:
                    f(E)
        self.sb.close()
        self.in_phase = False

    def _wait1(self, eng, key, val):
        if self.seen[eng].get(key, 0) >= val:
            return
        self.seen[eng][key] = val
        if isinstance(key, DSem):
            self.ops[eng].append(lambda E, s=key.h, v=val: E.wait_ge(s, v))
        else:
            self.ops[eng].append(lambda E, s=self.esem[key], v=val: E.wait_ge(s, v))

    def _waits(self, eng, r, w, after):
        for b in r:
            for k, v in b.w.items():
                self._wait1(eng, k, v)
            if b.excl:
                for k, v in b.r.items():
                    if k != eng:
                        self._wait1(eng, k, v)
        for b in w:
            for k, v in b.w.items():
                if eng == "pe" and k == "pe":
                    continue
                self._wait1(eng, k, v)
            for k, v in b.r.items():
                self._wait1(eng, k, v)
        for ev in after:
            if ev is not None:
                self._wait1(eng, ev[0], ev[1])

    @staticmethod
    def _bufs(xs):
        return [x.b if isinstance(x, Tile) else x for x in xs]

    def _record(self, ev, r, w):
        k, v = ev
        for b in r:
            if b.r.get(k, 0) < v:
                b.r[k] = v
        for b in w:
            b.w = {k: v}
            b.r = {}

    def op(self, eng, fn, r=(), w=(), after=(), sig=True):
        r = self._bufs(r)
        w = self._bufs(w)
        self._waits(eng, r, w, after)
        if sig:
            self.cnt[eng] += 1
            ev = (eng, self.cnt[eng])
            self.ops[eng].append(lambda E, fn=fn, s=self.esem[eng]: fn(E).then_inc(s, 1))
        else:
            ev = (eng, self.cnt[eng] + 1)
            self.ops[eng].append(lambda E, fn=fn: fn(E))
        self._record(ev, r, w)
        return ev

    def dma(self, q, out, in_, ds, r=(), w=(), after=()):
        r = self._bufs(r)
        w = self._bufs(w)
        self._waits(q, r, w, after)
        assert (q != "pool") or ds.persist
        ds.n += 16
        ev = (ds, ds.n)
        self.ops[q].append(lambda E, o=out, i=in_, s=ds.h: E.dma_start(out=o, in_=i).then_inc(s, 16))
        self._record(ev, r, w)
        return ev

    def ag_sem(self, name):
        d = self.pds(name)
        d.nobarrier = True
        return d

    def wait_all(self, eng, ds):
        if ds.n > 0:
            self._wait1(eng, ds, ds.n)

    def allgather(self, in_ap, out_ap, ds, after=()):
        self._waits("pool", [], [], after)
        assert ds.persist
        ds.n += 1
        ev = (ds, ds.n)
        self.ops["pool"].append(
            lambda E, i=in_ap, o=out_ap, s=ds.h: E.collective_compute(
                "AllGather", ALU.bypass, replica_groups=GROUPS, ins=[i], outs=[o]
            ).then_inc(s, 1)
        )
        return ev


class MK:
    def __init__(self, L, ext_in=(), ext_out=(), use_cc=True):
        self.L = L
        self.NCH = L // 512
        self.P = Prog(L)
        self.nc = self.P.nc
        self.ext_in = set(ext_in)
        self.ext_out = set(ext_out)
        self.use_cc = use_cc
        self.dr = {}
        self.xstack = contextlib.ExitStack()
        self.Wg = None
        self.Wo = None

    def dram(self, name, shape, dt, kind=None):
        if name in self.dr:
            return self.dr[name]
        if kind is None:
            if name in self.ext_in:
                kind = "ExternalInput"
            elif name in self.ext_out:
                kind = "ExternalOutput"
            else:
                kind = "Internal"
        t = self.nc.dram_tensor(name, list(shape), dt, kind=kind)
        self.dr[name] = t
        return t

    def consts_dram(self):
        self.c_mats = self.dram("c_mats", [128, 4 * 128], F32, "ExternalInput")
        self.c_mask = self.dram("c_mask", [128, 4 * 512], F32, "ExternalInput")
        self.c_invc = self.dram("c_invc", [4, self.L], F32, "ExternalInput")

    def ph_norm_stats(self, xt, stl, stg):
        P, L, NCH = self.P, self.L, self.NCH
        P.begin()
        ones = P.tile("ones", [128, 128], BF16)
        xs = [P.tile(f"x{i}", [128, L], F32) for i in range(2)]
        sq = [P.tile(f"sq{i}", [128, L], BF16) for i in range(2)]
        srow = P.tile("srow", [1, L], F32)
        for t in xs + [srow]:
            t.ds = P.new_ds()
        ones.ds = P.pds("ones")
        P.dma("pool", ones[:, :], self.c_mats[:, 256:384], ones.ds, w=[ones])
        import os
        CUT = int(os.environ.get("PH1_CUT", "9"))
        for i in range(8):
            x = xs[i % 2]
            s = sq[i % 2]
            P.dma("sp", x[:, :], xt[i * 128:(i + 1) * 128, :], x.ds, w=[x])
            P.op("act", lambda E, s=s, x=x: E.activation(out=s[:, :], in_=x[:, :], func=AF.Square), r=[x], w=[s])
            if CUT < 2:
                continue
            for ch in range(NCH):
                ps = P.psum[ch]
                P.op("pe", lambda E, ps=ps, s=s, ch=ch, i=i: E.matmul(
                    ps[:, :], lhsT=ones[:, :], rhs=s[:, ch * 512:(ch + 1) * 512], start=(i == 0), stop=(i == 7)),
                    r=[ones, s], w=[ps], sig=(ch == NCH - 1))
        for ch in range(NCH):
            if CUT < 3:
                continue
            ps = P.psum[ch]
            P.op("dve", lambda E, ps=ps, ch=ch: E.tensor_copy(out=srow[0:1, ch * 512:(ch + 1) * 512], in_=ps[0:1, :]),
                 r=[ps], w=[srow])
        ev = None
        if CUT >= 4:
            ev = P.dma("sp", stl[0:1, :], srow[0:1, :], srow.ds, r=[srow])
        if self.use_cc:
            P.allgather(stl.ap(), stg.ap(), P.pds("ag"), after=[ev])
        P.end()

    def ph_norm_apply(self, xt, g, stg, htl, htg, final_out=None):
        P, L, NCH = self.P, self.L, self.NCH
        P.begin()
        ones4 = P.tile("ones4", [4, 128], F32)
        st4 = P.tile("st4", [4, L], F32)
        gt = P.tile("gt", [128, 8], F32)
        R = P.tile("R", [128, L], F32)
        xs = [P.tile(f"x{i}", [128, L], F32) for i in range(2)]
        odt = F32 if final_out is not None else BF16
        hs = [P.tile(f"h{i}", [128, L], odt) for i in range(2)]
        for t in xs + hs + [ones4, st4, gt]:
            t.ds = P.new_ds()
        P.dma("sp", ones4[:, :], self.c_mats[0:4, 256:384], ones4.ds, w=[ones4])
        P.dma("sp", st4[:, :], stg[:, :], st4.ds, w=[st4])
        P.dma("sp", gt[:, :], g[:, :], gt.ds, w=[gt])
        for ch in range(NCH):
            ps = P.psum[ch]
            sl = slice(ch * 512, (ch + 1) * 512)
            P.op("pe", lambda E, ps=ps, sl=sl: E.matmul(ps[:, :], lhsT=ones4[:, :], rhs=st4[:, sl], start=True, stop=True),
                 r=[ones4, st4], w=[ps])
            P.op("dve", lambda E, ps=ps, sl=sl: E.tensor_scalar(out=R[:, sl], in0=ps[:, :], scalar1=1.0 / D, scalar2=EPS,
                                                                 op0=ALU.mult, op1=ALU.add), r=[ps], w=[R])
        P.op("act", lambda E: E.activation(out=R[:, :], in_=R[:, :], func=AF.Sqrt), r=[R], w=[R])
        P.op("dve", lambda E: E.reciprocal(out=R[:, :], in_=R[:, :]), r=[R], w=[R])
        agds = P.pds("ag")
        for i in range(8):
            x = xs[i % 2]
            h = hs[i % 2]
            P.dma("sp", x[:, :], xt[i * 128:(i + 1) * 128, :], x.ds, w=[x])
            P.op("dve", lambda E, x=x, h=h, i=i: E.scalar_tensor_tensor(
                out=h[:, :], in0=x[:, :], scalar=gt[:, i:i + 1], in1=R[:, :], op0=ALU.mult, op1=ALU.mult),
                r=[x, gt, R], w=[h])
            if final_out is not None:
                P.dma("sp", final_out[i * 128:(i + 1) * 128, :], h[:, :], h.ds, r=[h])
            else:
                ev = P.dma("sp", htl[i][:, :], h[:, :], h.ds, r=[h])
                if self.use_cc:
                    P.allgather(htl[i].ap(), htg[i].ap(), agds, after=[ev])
        P.end()

    def dump(self, src, name, rows=None):
        P = self.P
        shape = list(src.shape)
        if rows is not None:
            shape[0] = rows
        dst = self.dram(name, shape, src.dtype, "ExternalOutput")
        P.begin()
        ds = P.new_ds()
        rows = shape[0]
        step = 128 if rows >= 128 else rows
        for r0 in range(0, rows, step):
            P.dma("sp", dst[r0:r0 + step, :], src[r0:r0 + step, :], ds)
        P.end()
        return dst

    def scratch(self):
        L = self.L
        S = {}
        S["PX"] = self.dram("PX", [256, L], F32)
        S["PG"] = self.dram("PG", [256, L], BF16)
        S["QT"] = self.dram("QT", [512, L], BF16)
        S["KT"] = self.dram("KT", [512, L], BF16)
        S["V"] = self.dram("V", [L, 512], BF16)
        S["AG"] = self.dram("AGs", [512, L], BF16)
        S["UT"] = self.dram("UT", [256, L], BF16)
        S["ygl"] = [self.dram(f"ygl{a}", [128, L], BF16) for a in range(10)]
        S["ygg"] = [self.dram(f"ygg{a}", [512, L], BF16) for a in range(10)]
        S["bsl"] = self.dram("bsl", [2, L], F32)
        S["bsg"] = self.dram("bsg", [8, L], F32)
        self.S = S
        return S

    def ph_inproj(self, htg, win):
        P, L, NCH, S = self.P, self.L, self.NCH, self.S
        P.begin()
        Ws = [P.tile(f"W{i}", [128, 4, 8, 512], BF16) for i in range(2)]
        HTs = [P.tile(f"HT{i}", [128, 4, 8, 512], BF16) for i in range(2)]
        NO = 6
        Of = [P.tile(f"Of{i}", [128, 512], F32) for i in range(NO)]
        Ob = [P.tile(f"Ob{i}", [128, 512], BF16) for i in range(NO)]
        for i, t in enumerate(Ws):
            t.ds = P.pds(f"w{i}")
        for t in HTs + Of + Ob:
            t.ds = P.new_ds()
        steps = [(g, ch) for g in range(6) for ch in range(NCH)]

        def load_w(g):
            W = Ws[g % 2]
            for r in range(4):
                src = win[1024 * r:1024 * (r + 1), g * 512:(g + 1) * 512].rearrange("(i p) f -> p i f", p=128)
                P.dma("pool", W[:, r, :, :], src, W.ds, w=[W])

        def load_h(si):
            g, ch = steps[si]
            HT = HTs[si % 2]
            for i in range(8):
                src = htg[i][:, ch * 512:(ch + 1) * 512].rearrange("(r p) t -> p r t", p=128)
                P.dma("sp", HT[:, :, i, :], src, HT.ds, w=[HT])

        load_w(0)
        load_h(0)
        oi = 0
        for si, (g, ch) in enumerate(steps):
            if ch == 0 and g + 1 < 6:
                load_w(g + 1)
            if si + 1 < len(steps):
                load_h(si + 1)
            W = Ws[g % 2]
            HT = HTs[si % 2]
            pb = (si % 2) * 4
            for r in range(4):
                for i in range(8):
                    kt = r * 8 + i
                    for f in range(4):
                        ps = P.psum[pb + f]
                        if g == 3:
                            fn = lambda E, ps=ps, r=r, i=i, f=f, W=W, HT=HT, kt=kt: E.matmul(
                                ps[:, :], lhsT=HT[:, r, i, f * 128:(f + 1) * 128], rhs=W[:, r, i, :],
                                start=(kt == 0), stop=(kt == 31))
                        else:
                            fn = lambda E, ps=ps, r=r, i=i, f=f, W=W, HT=HT, kt=kt: E.matmul(
                                ps[:, :], lhsT=W[:, r, i, f * 128:(f + 1) * 128], rhs=HT[:, r, i, :],
                                start=(kt == 0), stop=(kt == 31))
                        P.op("pe", fn, r=[W, HT], w=[ps], sig=(kt == 31))
            csl = slice(ch * 512, (ch + 1) * 512)
            for f in range(4):
                ps = P.psum[pb + f]
                eng = "act" if (f % 2 == 0) else "dve"
                rs = slice((f % 2) * 128, (f % 2) * 128 + 128)
                fs = slice(f * 128, f * 128 + 128)
                kind = "copy"
                if g == 0:
                    if f < 2:
                        dst, odt = S["PX"][rs, csl], F32
                    else:
                        dst, odt, kind = S["PG"][rs, csl], BF16, "silu"
                elif g == 1:
                    dst, odt, kind = S["QT"][fs, csl], BF16, "scale"
                elif g == 2:
                    dst, odt = S["KT"][fs, csl], BF16
                elif g == 3:
                    dst, odt = S["V"][ch * 512 + f * 128: ch * 512 + f * 128 + 128, :], BF16
                elif g == 4:
                    dst, odt, kind = S["AG"][fs, csl], BF16, "silu"
                else:
                    if f < 2:
                        dst, odt = S["UT"][rs, csl], BF16
                    else:
                        dst, odt, kind = S["ygl"][8 + f - 2][:, csl], BF16, "silu"
                O = (Of if odt == F32 else Ob)[oi % NO]
                oi += 1
                if kind == "silu":
                    eng = "act"
                    P.op("act", lambda E, O=O, ps=ps: E.activation(out=O[:, :], in_=ps[:, :], func=AF.Silu), r=[ps], w=[O])
                elif kind == "scale":
                    sc = 128.0 ** -0.5
                    if eng == "act":
                        P.op("act", lambda E, O=O, ps=ps: E.activation(out=O[:, :], in_=ps[:, :], func=AF.Copy, scale=sc), r=[ps], w=[O])
                    else:
                        P.op("dve", lambda E, O=O, ps=ps: E.tensor_scalar(out=O[:, :], in0=ps[:, :], scalar1=sc, scalar2=None, op0=ALU.mult), r=[ps], w=[O])
                else:
                    if eng == "act":
                        P.op("act", lambda E, O=O, ps=ps: E.activation(out=O[:, :], in_=ps[:, :], func=AF.Copy), r=[ps], w=[O])
                    else:
                        P.op("dve", lambda E, O=O, ps=ps: E.tensor_copy(out=O[:, :], in_=ps[:, :]), r=[ps], w=[O])
                P.dma("sp", dst, O[:, :], O.ds, r=[O])
        if self.use_cc:
            evs = [(t.ds, t.ds.n) for t in Of + Ob if t.ds.n > 0]
            for a in (8, 9):
                P.allgather(S["ygl"][a].ap(), S["ygg"][a].ap(), P.ag_sem("ag_a"), after=evs)
        P.end()

    def ph_pool(self, wpool, pscale, gpool, poolsel, poolselT):
        P, L, NCH, S = self.P, self.L, self.NCH, self.S
        P.begin()
        ones = P.tile("ones", [128, 128], BF16)
        ones.ds = P.pds("ones")
        P.dma("pool", ones[:, :], self.c_mats[:, 256:384], ones.ds, w=[ones])
        wp = P.tile("wp", [128, 2, 256], BF16)
        wp.ds = P.pds("w0")
        P.dma("pool", wp[:, :, :], wpool[:, :].rearrange("(j p) d -> p j d", p=128), wp.ds, w=[wp])
        PB = [P.tile(f"PB{j}", [128, L], BF16) for j in range(2)]
        small = P.tile("small", [128, 8], F32)
        small.ds = P.new_ds()
        P.dma("sp", small[:, 0:2], pscale[:, :], small.ds, w=[small])
        P.dma("sp", small[:, 2:4], gpool[:, :], small.ds, w=[small])
        P.dma("sp", small[:, 4:8], poolsel[:, :], small.ds, w=[small])
        selT = P.tile("selT", [4, 128], BF16)
        selT.ds = P.pds("w1")
        P.dma("pool", selT[:, :], poolselT[:, :], selT.ds, w=[selT])
        inv4 = [P.tile(f"inv4_{i}", [4, 512], F32) for i in range(2)]
        ihi = [P.tile(f"ihi{i}", [4, 512], BF16) for i in range(2)]
        ilo = [P.tile(f"ilo{i}", [4, 512], BF16) for i in range(2)]
        for t in inv4:
            t.ds = P.new_ds()
        IC = P.tile("IC", [128, L], F32)
        for ch in range(NCH):
            ps = P.psum[6 + ch % 2]
            i4, hi, lo = inv4[ch % 2], ihi[ch % 2], ilo[ch % 2]
            P.dma("sp", i4[:, :], self.c_invc[:, ch * 512:(ch + 1) * 512], i4.ds, w=[i4])
            P.op("dve", lambda E, i4=i4, hi=hi: E.tensor_copy(out=hi[:, :], in_=i4[:, :]), r=[i4], w=[hi])
            P.op("dve", lambda E, i4=i4, hi=hi, lo=lo: E.tensor_tensor(out=lo[:, :], in0=i4[:, :], in1=hi[:, :], op=ALU.subtract), r=[i4, hi], w=[lo])
            P.op("pe", lambda E, ps=ps, hi=hi: E.matmul(ps[:, :], lhsT=selT[:, :], rhs=hi[:, :], start=True, stop=False),
                 r=[selT, hi], w=[ps], sig=False)
            P.op("pe", lambda E, ps=ps, lo=lo: E.matmul(ps[:, :], lhsT=selT[:, :], rhs=lo[:, :], start=False, stop=True),
                 r=[selT, lo], w=[ps])
            P.op("act", lambda E, ps=ps, ch=ch: E.activation(out=IC[:, ch * 512:(ch + 1) * 512], in_=ps[:, :], func=AF.Copy), r=[ps], w=[IC])
        import os
        PCUT = int(os.environ.get("POOL_CUT", "9"))
        if PCUT < 1:
            P.end()
            return
        X = P.tile("X", [128, L], F32)
        A = P.tile("A", [128, L], F32)
        B = P.tile("B", [128, L], F32)
        Sx = P.tile("Sx", [128, L], F32)
        PGt = [P.tile(f"PG{j}", [128, L], BF16) for j in range(2)]
        srow = P.tile("srow", [1, L], F32)
        for t in [X, srow] + PGt:
            t.ds = P.new_ds()
        for j in range(2):
            P.dma("sp", PGt[j][:, :], S["PG"][j * 128:(j + 1) * 128, :], PGt[j].ds, w=[PGt[j]])
        for j in range(2):
            P.dma("sp", X[:, :], S["PX"][j * 128:(j + 1) * 128, :], X.ds, w=[X])
            src = X
            bufs = [A, B]
            for wi, k in enumerate((1, 2, 4, 8)):
                dst = bufs[wi % 2]
                P.op("dve", lambda E, dst=dst, src=src, k=k: E.tensor_tensor(out=dst[:, k:L], in0=src[:, k:L], in1=src[:, 0:L - k], op=ALU.add),
                     r=[src], w=[dst])
                P.op("dve", lambda E, dst=dst, src=src, k=k: E.tensor_copy(out=dst[:, 0:k], in_=src[:, 0:k]), r=[src], w=[dst])
                if wi == 0:
                    P.op("dve", lambda E, dst=dst: E.tensor_scalar(out=Sx[:, :], in0=dst[:, :], scalar1=small[:, 4:5], scalar2=None, op0=ALU.mult),
                         r=[dst, small], w=[Sx])
                else:
                    P.op("dve", lambda E, dst=dst, wi=wi: E.scalar_tensor_tensor(out=Sx[:, :], in0=dst[:, :], scalar=small[:, 4 + wi:5 + wi],
                                                                               in1=Sx[:, :], op0=ALU.mult, op1=ALU.add), r=[dst, small, Sx], w=[Sx])
                src = dst
            P.op("dve", lambda E: E.tensor_tensor(out=Sx[:, :], in0=Sx[:, :], in1=IC[:, :], op=ALU.mult), r=[Sx, IC], w=[Sx])
            P.op("dve", lambda E, j=j: E.tensor_tensor(out=Sx[:, :], in0=Sx[:, :], in1=X[:, :], op=ALU.subtract), r=[Sx, X], w=[Sx])
            if os.environ.get("PB_FROM_PG"):
                P.op("act", lambda E, j=j: E.activation(out=PB[j][:, :], in_=PGt[j][:, :], func=AF.Copy), r=[Sx, PGt[j]], w=[PB[j]])
            else:
                P.op("act", lambda E, j=j: E.activation(out=PB[j][:, :], in_=Sx[:, :], func=AF.Copy), r=[Sx], w=[PB[j]])
        if PCUT < 2:
            for j in range(2):
                PB[j].ds = P.new_ds()
                P.dma("sp", S["ygl"][j][:, :], PB[j][:, :], PB[j].ds, r=[PB[j]])
            P.end()
            return
        NO = 4
        Yt = [P.tile(f"Y{i}", [128, 512], F32) for i in range(NO)]
        Qt = [P.tile(f"Q{i}", [128, 512], BF16) for i in range(NO)]
        Gt = [P.tile(f"G{i}", [128, 512], BF16) for i in range(NO)]
        for t in Gt:
            t.ds = P.new_ds()
        oi = 0
        for ch in range(NCH):
            csl = slice(ch * 512, (ch + 1) * 512)
            pst = P.psum[4 + ch % 2]
            for dt in range(2):
                ps = P.psum[oi % 4]
                Y, Q, G = Yt[oi % NO], Qt[oi % NO], Gt[oi % NO]
                oi += 1
                for j in range(2 if PCUT != 19 else 0):
                    P.op("pe", lambda E, ps=ps, j=j, dt=dt, csl=csl: E.matmul(ps[:, :], lhsT=wp[:, j, dt * 128:(dt + 1) * 128], rhs=(PGt if os.environ.get("USE_PG") else PB)[j][:, csl],
                                                                          start=(j == 0), stop=(j == 1)), r=[wp, PB[j], PGt[j]], w=[ps], sig=(j == 1))
                P.op("dve", lambda E, ps=ps, Y=Y, dt=dt: E.tensor_scalar(out=Y[:, :], in0=ps[:, :], scalar1=small[:, dt:dt + 1], scalar2=None, op0=ALU.mult),
                     r=[ps, small], w=[Y])
                P.op("act", lambda E, Y=Y, Q=Q: E.activation(out=Q[:, :], in_=Y[:, :], func=AF.Square), r=[Y], w=[Q])
                if PCUT >= 3 and PCUT < 20:
                    P.op("pe", lambda E, pst=pst, Q=Q, dt=dt: E.matmul(pst[:, :], lhsT=ones[:, :], rhs=Q[:, :], start=(dt == 0), stop=(dt == 1)),
                     r=[ones, Q], w=[pst], sig=True)
                if PCUT >= 4 and PCUT < 20:
                    P.op("dve", lambda E, Y=Y, G=G, dt=dt, csl=csl: E.scalar_tensor_tensor(out=G[:, :], in0=Y[:, :], scalar=small[:, 2 + dt:3 + dt],
                                                                                   in1=PGt[dt][:, csl], op0=ALU.mult, op1=ALU.mult),
                     r=[Y, small, PGt[dt]], w=[G])
                if PCUT >= 5 and PCUT < 20:
                    P.dma("sp", S["ygl"][dt][:, csl], G[:, :], G.ds, r=[G])
            if PCUT >= 6 and PCUT < 20:
                P.op("dve", lambda E, pst=pst, csl=csl: E.tensor_copy(out=srow[0:1, csl], in_=pst[0:1, :]), r=[pst], w=[srow])
        if PCUT >= 6 and PCUT < 20:
            P.dma("sp", S["bsl"][0:1, :], srow[0:1, :], srow.ds, r=[srow])
        if self.use_cc:
            evs = [(t.ds, t.ds.n) for t in Gt if t.ds.n > 0]
            for a in (0, 1):
                P.allgather(S["ygl"][a].ap(), S["ygg"][a].ap(), P.ag_sem("ag_b"), after=evs)
        P.end()

    def ph_attn(self, gattn, wglu_next=None):
        P, L, NCH, S = self.P, self.L, self.NCH, self.S
        NBLK = L // 128
        P.begin()
        mats = P.tile("mats", [128, 3, 128], BF16)
        mats.ds = P.pds("ones")
        P.dma("pool", mats[:, :, :], self.c_mats[:, 0:384].rearrange("p (a b) -> p a b", a=3), mats.ds, w=[mats])
        mask = P.tile("mask", [128, 4, 512], BF16)
        mask.ds = P.pds("w0")
        P.dma("pool", mask[:, :, :], self.c_mask[:, :].rearrange("p (a b) -> p a b", a=4), mask.ds, w=[mask])
        ga = P.tile("ga", [128, 4], F32)
        ga.ds = P.new_ds()
        P.dma("sp", ga[:, :], gattn[:, :], ga.ds, w=[ga])
        Qh = [P.tile(f"Qh{i}", [128, L], BF16) for i in range(2)]
        Kh = [P.tile(f"Kh{i}", [128, L], BF16) for i in range(2)]
        Vh = [P.tile(f"Vh{i}", [128, NBLK, 128], BF16) for i in range(2)]
        Gh = [P.tile(f"Gh{i}", [128, L], BF16) for i in range(2)]
        for t in Qh + Kh + Vh + Gh:
            t.ds = P.new_ds()
        NR = 4
        Eb = [P.tile(f"E{i}", [128, 512], F32) for i in range(NR)]
        Lb = [P.tile(f"Lb{i}", [128, 512], BF16) for i in range(NR)]
        Db = [P.tile(f"D{i}", [128, 512], F32) for i in range(NR)]
        Wb = [P.tile(f"Wt{i}", [128, 512], BF16) for i in range(NR)]
        Carry = P.tile("Carry", [128, 512], F32)
        Yc = [P.tile(f"Yc{i}", [128, 512], F32) for i in range(2)]
        Yg = [P.tile(f"Yg{i}", [128, 512], BF16) for i in range(2)]
        Sq = [P.tile(f"Sq{i}", [128, 512], F32) for i in range(2)]
        SQacc = P.tile("SQacc", [128, L], F32)
        SQb = P.tile("SQb", [128, L], BF16)
        srow = P.tile("srow", [1, L], F32)
        for t in Yg + [srow]:
            t.ds = P.new_ds()

        def load_head(h):
            s = h % 2
            hs = slice(h * 128, (h + 1) * 128)
            P.dma("sp", Qh[s][:, :], S["QT"][hs, :], Qh[s].ds, w=[Qh[s]])
            P.dma("sp", Kh[s][:, :], S["KT"][hs, :], Kh[s].ds, w=[Kh[s]])
            P.dma("sp", Vh[s][:, :, :], S["V"][:, hs].rearrange("(n p) d -> p n d", p=128), Vh[s].ds, w=[Vh[s]])
            P.dma("sp", Gh[s][:, :], S["AG"][hs, :], Gh[s].ds, w=[Gh[s]])

        tiles = []
        for h in range(4):
            for qc in range(NCH):
                nb = 4 * (qc + 1)
                for bi, j in enumerate(range(nb - 1, -1, -1)):
                    tiles.append((h, qc, j, bi == 0, bi == nb - 1))
        T = len(tiles)
        ident, negtri, ones = mats[:, 0, :], mats[:, 1, :], mats[:, 2, :]
        strm = {}
        sc = 0
        for t in tiles:
            if t[3]:
                strm[(t[0], t[1])] = sc
                sc += 1

        def zmm(ps, h, qc, j, last_stop):
            s = h % 2
            r = j - 4 * qc
            qsl = slice(qc * 512, (qc + 1) * 512)
            ksl = slice(j * 128, (j + 1) * 128)
            diag = r >= 0
            P.op("pe", lambda E: E.matmul(ps[:, :], lhsT=Kh[s][:, ksl], rhs=Qh[s][:, qsl], start=True, stop=(last_stop and not diag)),
                 r=[Kh[s], Qh[s]], w=[ps], sig=(last_stop and not diag))
            if diag:
                P.op("pe", lambda E: E.matmul(ps[:, :], lhsT=ident, rhs=mask[:, r, :], start=False, stop=last_stop),
                     r=[mats, mask], w=[ps], sig=last_stop)

        def stage_a(ti):
            h, qc, j, first, last = tiles[ti]
            psz = P.psum[ti % 2]
            zmm(psz, h, qc, j, True)
            E_, L_ = Eb[ti % NR], Lb[ti % NR]
            P.op("act", lambda E: E.activation(out=E_[:, :], in_=psz[:, :], func=AF.Exp), r=[psz], w=[E_])
            P.op("act", lambda E: E.activation(out=L_[:, :], in_=E_[:, :], func=AF.Ln, bias=1.0), r=[E_], w=[L_])

        def stage_b(ti):
            h, qc, j, first, last = tiles[ti]
            pse = P.psum[2 + ti % 2]
            pst = P.psum[4 + ti % 2]
            L_, D_ = Lb[ti % NR], Db[ti % NR]
            zmm(pse, h, qc, j, False)
            P.op("pe", lambda E: E.matmul(pse[:, :], lhsT=negtri, rhs=L_[:, :], start=False, stop=True), r=[mats, L_], w=[pse])
            P.op("pe", lambda E: E.matmul(pst[:, :], lhsT=ones, rhs=L_[:, :], start=True, stop=True), r=[mats, L_], w=[pst])
            if first:
                P.op("dve", lambda E: E.tensor_copy(out=D_[:, :], in_=pse[:, :]), r=[pse], w=[D_])
                P.op("dve", lambda E: E.tensor_copy(out=Carry[:, :], in_=pst[:, :]), r=[pst], w=[Carry])
            else:
                P.op("dve", lambda E: E.tensor_tensor(out=D_[:, :], in0=pse[:, :], in1=Carry[:, :], op=ALU.subtract), r=[pse, Carry], w=[D_])
                if not last:
                    P.op("dve", lambda E: E.tensor_tensor(out=Carry[:, :], in0=pst[:, :], in1=Carry[:, :], op=ALU.add), r=[pst, Carry], w=[Carry])

        def stage_c1(ti):
            D_, W_ = Db[ti % NR], Wb[ti % NR]
            P.op("act", lambda E: E.activation(out=W_[:, :], in_=D_[:, :], func=AF.Exp), r=[D_], w=[W_])

        def stage_c2(ti):
            h, qc, j, first, last = tiles[ti]
            s = h % 2
            si = strm[(h, qc)]
            pso = P.psum[6 + si % 2]
            W_ = Wb[ti % NR]
            P.op("pe", lambda E: E.matmul(pso[:, :], lhsT=Vh[s][:, j, :], rhs=W_[:, :], start=first, stop=last), r=[Vh[s], W_], w=[pso], sig=last)
            if last:
                qsl = slice(qc * 512, (qc + 1) * 512)
                yc, yg, sq = Yc[si % 2], Yg[si % 2], Sq[si % 2]
                P.op("dve", lambda E: E.tensor_copy(out=yc[:, :], in_=pso[:, :]), r=[pso], w=[yc])
                P.op("dve", lambda E: E.scalar_tensor_tensor(out=yg[:, :], in0=yc[:, :], scalar=ga[:, h:h + 1], in1=Gh[s][:, qsl],
                                                             op0=ALU.mult, op1=ALU.mult), r=[yc, ga, Gh[s]], w=[yg])
                P.dma("sp", S["ygl"][2 + h][:, qsl], yg[:, :], yg.ds, r=[yg])
                if h == 0:
                    P.op("pool", lambda E: E.tensor_tensor(out=SQacc[:, qsl], in0=yc[:, :], in1=yc[:, :], op=ALU.mult), r=[yc], w=[SQacc])
                else:
                    P.op("pool", lambda E: E.tensor_tensor(out=sq[:, :], in0=yc[:, :], in1=yc[:, :], op=ALU.mult), r=[yc], w=[sq])
                    dst = SQb if h == 3 else SQacc
                    P.op("pool", lambda E: E.tensor_tensor(out=dst[:, qsl], in0=sq[:, :], in1=SQacc[:, qsl], op=ALU.add), r=[sq, SQacc], w=[dst])

        load_head(0)
        load_head(1)
        for ti in range(T + 3):
            if ti < T:
                stage_a(ti)
            if 0 <= ti - 1 < T:
                stage_b(ti - 1)
            if 0 <= ti - 2 < T:
                stage_c1(ti - 2)
            if 0 <= ti - 3 < T:
                stage_c2(ti - 3)
                hh, qq, jj, ff, ll = tiles[ti - 3]
                if ff and qq == 0 and 1 <= hh <= 2:
                    load_head(hh + 1)
        for ch in range(NCH):
            ps = P.psum[ch % 2]
            csl = slice(ch * 512, (ch + 1) * 512)
            P.op("pe", lambda E, ps=ps, csl=csl: E.matmul(ps[:, :], lhsT=ones, rhs=SQb[:, csl], start=True, stop=True), r=[mats, SQb], w=[ps])
            P.op("dve", lambda E, ps=ps, csl=csl: E.tensor_copy(out=srow[0:1, csl], in_=ps[0:1, :]), r=[ps], w=[srow])
        ev = P.dma("sp", S["bsl"][1:2, :], srow[0:1, :], srow.ds, r=[srow])
        if self.use_cc:
            evs = [(t.ds, t.ds.n) for t in Yg if t.ds.n > 0] + [ev]
            for a in (2, 3, 4, 5):
                P.allgather(S["ygl"][a].ap(), S["ygg"][a].ap(), P.ag_sem("ag_b"), after=evs)
            P.allgather(S["bsl"].ap(), S["bsg"].ap(), P.ag_sem("ag_b"), after=evs)
        P.end()

    def ph_ssm(self, lam, ldt, BA, BAs, C1, C2, dskip):
        P, L, NCH, S = self.P, self.L, self.NCH, self.S
        NLEV = 10
        P.begin()
        TWO_PI = 2.0 * math.pi
        MAG = 12582912.0
        CW1 = 6.28125
        CW2 = float(TWO_PI - 6.28125)
        lamt = P.tile("lamt", [128, 32], F32)
        ldtt = P.tile("ldtt", [128, 16], F32)
        dsk = P.tile("dsk", [128, 2], F32)
        for t in (lamt, ldtt, dsk):
            t.ds = P.new_ds()
        P.dma("sp", lamt[:, :], lam[:, :], lamt.ds, w=[lamt])
        P.dma("sp", ldtt[:, :], ldt[:, :], ldtt.ds, w=[ldtt])
        P.dma("sp", dsk[:, :], dskip[:, :], dsk.ds, w=[dsk])
        sg = P.tile("sg", [128, 2], F32)
        P.op("dve", lambda E: E.memset(sg[0:64, 0:1], 1.0), w=[sg])
        P.op("dve", lambda E: E.memset(sg[64:128, 0:1], -1.0), w=[sg])
        P.op("dve", lambda E: E.memset(sg[0:64, 1:2], -1.0), w=[sg])
        P.op("dve", lambda E: E.memset(sg[64:128, 1:2], 1.0), w=[sg])
        names = ["dt", "are", "th", "r", "k", "phs", "phc", "c1", "s1", "nr", "ni", "den", "inv", "t1", "t2", "cre", "cim", "a2", "b2"]
        q = {n: P.tile("q_" + n, [128, 16], F32) for n in names}
        CK = [P.tile(f"CK{k}", [128, 16], F32) for k in range(NLEV)]
        SK = [P.tile(f"SK{k}", [128, 16], F32) for k in range(NLEV)]
        NSK = [P.tile(f"NSK{k}", [128, 16], F32) for k in range(NLEV)]
        lre, lim = lamt[:, 0:16], lamt[:, 16:32]

        def dve(fn, r, w):
            P.op("dve", fn, r=r, w=w)

        def act(fn, r, w):
            P.op("act", fn, r=r, w=w)

        import os
        SCUT = int(os.environ.get("SSM_CUT", "9"))
        if SCUT < 1:
            P.end()
            return
        act(lambda E: E.activation(out=q["dt"][:, :], in_=ldtt[:, :], func=AF.Exp), [ldtt], [q["dt"]])
        dve(lambda E: E.tensor_tensor(out=q["are"][:, :], in0=lre, in1=q["dt"][:, :], op=ALU.mult), [lamt, q["dt"]], [q["are"]])
        dve(lambda E: E.tensor_tensor(out=q["th"][:, :], in0=lim, in1=q["dt"][:, :], op=ALU.mult), [lamt, q["dt"]], [q["th"]])
        act(lambda E: E.activation(out=q["r"][:, :], in_=q["are"][:, :], func=AF.Exp), [q["are"]], [q["r"]])

        def reduce_angle(dst, shift):
            dve(lambda E: E.tensor_scalar(out=q["t1"][:, :], in0=q["th"][:, :], scalar1=shift, scalar2=None, op0=ALU.add), [q["th"]], [q["t1"]])
            dve(lambda E: E.tensor_scalar(out=q["k"][:, :], in0=q["t1"][:, :], scalar1=float(1.0 / TWO_PI), scalar2=MAG, op0=ALU.mult, op1=ALU.add),
                [q["t1"]], [q["k"]])
            dve(lambda E: E.tensor_single_scalar(out=q["k"][:, :], in_=q["k"][:, :], scalar=-MAG, op=ALU.add), [q["k"]], [q["k"]])
            dve(lambda E: E.scalar_tensor_tensor(out=q["t1"][:, :], in0=q["k"][:, :], scalar=-CW1, in1=q["t1"][:, :], op0=ALU.mult, op1=ALU.add),
                [q["k"], q["t1"]], [q["t1"]])
            dve(lambda E: E.scalar_tensor_tensor(out=dst[:, :], in0=q["k"][:, :], scalar=-CW2, in1=q["t1"][:, :], op0=ALU.mult, op1=ALU.add),
                [q["k"], q["t1"]], [dst])

        reduce_angle(q["phs"], 0.0)
        reduce_angle(q["phc"], float(math.pi / 2))
        act(lambda E: E.activation(out=q["s1"][:, :], in_=q["phs"][:, :], func=AF.Sin), [q["phs"]], [q["s1"]])
        act(lambda E: E.activation(out=q["c1"][:, :], in_=q["phc"][:, :], func=AF.Sin), [q["phc"]], [q["c1"]])
        dve(lambda E: E.tensor_tensor(out=q["nr"][:, :], in0=q["r"][:, :], in1=q["c1"][:, :], op=ALU.mult), [q["r"], q["c1"]], [q["nr"]])
        dve(lambda E: E.tensor_single_scalar(out=q["nr"][:, :], in_=q["nr"][:, :], scalar=-1.0, op=ALU.add), [q["nr"]], [q["nr"]])
        dve(lambda E: E.tensor_tensor(out=q["ni"][:, :], in0=q["r"][:, :], in1=q["s1"][:, :], op=ALU.mult), [q["r"], q["s1"]], [q["ni"]])
        dve(lambda E: E.tensor_tensor(out=q["den"][:, :], in0=lre, in1=lre, op=ALU.mult), [lamt], [q["den"]])
        dve(lambda E: E.tensor_tensor(out=q["t1"][:, :], in0=lim, in1=lim, op=ALU.mult), [lamt], [q["t1"]])
        dve(lambda E: E.tensor_tensor(out=q["den"][:, :], in0=q["den"][:, :], in1=q["t1"][:, :], op=ALU.add), [q["den"], q["t1"]], [q["den"]])
        dve(lambda E: E.reciprocal(out=q["inv"][:, :], in_=q["den"][:, :]), [q["den"]], [q["inv"]])
        dve(lambda E: E.tensor_tensor(out=q["t1"][:, :], in0=q["nr"][:, :], in1=lre, op=ALU.mult), [q["nr"], lamt], [q["t1"]])
        dve(lambda E: E.tensor_tensor(out=q["t2"][:, :], in0=q["ni"][:, :], in1=lim, op=ALU.mult), [q["ni"], lamt], [q["t2"]])
        dve(lambda E: E.tensor_tensor(out=q["t1"][:, :], in0=q["t1"][:, :], in1=q["t2"][:, :], op=ALU.add), [q["t1"], q["t2"]], [q["t1"]])
        dve(lambda E: E.tensor_tensor(out=q["cre"][:, :], in0=q["t1"][:, :], in1=q["inv"][:, :], op=ALU.mult), [q["t1"], q["inv"]], [q["cre"]])
        dve(lambda E: E.tensor_tensor(out=q["t1"][:, :], in0=q["ni"][:, :], in1=lre, op=ALU.mult), [q["ni"], lamt], [q["t1"]])
        dve(lambda E: E.tensor_tensor(out=q["t2"][:, :], in0=q["nr"][:, :], in1=lim, op=ALU.mult), [q["nr"], lamt], [q["t2"]])
        dve(lambda E: E.tensor_tensor(out=q["t1"][:, :], in0=q["t1"][:, :], in1=q["t2"][:, :], op=ALU.subtract), [q["t1"], q["t2"]], [q["t1"]])
        dve(lambda E: E.tensor_tensor(out=q["cim"][:, :], in0=q["t1"][:, :], in1=q["inv"][:, :], op=ALU.mult), [q["t1"], q["inv"]], [q["cim"]])
        dve(lambda E: E.tensor_scalar(out=q["a2"][:, :], in0=q["cim"][:, :], scalar1=sg[:, 1:2], scalar2=None, op0=ALU.mult), [q["cim"], sg], [q["a2"]])
        dve(lambda E: E.tensor_scalar(out=q["b2"][:, :], in0=q["cre"][:, :], scalar1=sg[:, 0:1], scalar2=None, op0=ALU.mult), [q["cre"], sg], [q["b2"]])
        dve(lambda E: E.tensor_copy(out=CK[0][:, :], in_=q["c1"][:, :]), [q["c1"]], [CK[0]])
        dve(lambda E: E.tensor_copy(out=SK[0][:, :], in_=q["s1"][:, :]), [q["s1"]], [SK[0]])
        for k in range(NLEV):
            dve(lambda E, k=k: E.tensor_single_scalar(out=NSK[k][:, :], in_=SK[k][:, :], scalar=-1.0, op=ALU.mult), [SK[k]], [NSK[k]])
            if k + 1 < NLEV:
                dve(lambda E, k=k: E.tensor_tensor(out=q["t1"][:, :], in0=CK[k][:, :], in1=CK[k][:, :], op=ALU.mult), [CK[k]], [q["t1"]])
                dve(lambda E, k=k: E.tensor_tensor(out=q["t2"][:, :], in0=SK[k][:, :], in1=SK[k][:, :], op=ALU.mult), [SK[k]], [q["t2"]])
                dve(lambda E, k=k: E.tensor_tensor(out=CK[k + 1][:, :], in0=q["t1"][:, :], in1=q["t2"][:, :], op=ALU.subtract), [q["t1"], q["t2"]], [CK[k + 1]])
                dve(lambda E, k=k: E.tensor_tensor(out=q["t1"][:, :], in0=CK[k][:, :], in1=SK[k][:, :], op=ALU.mult), [CK[k], SK[k]], [q["t1"]])
                dve(lambda E, k=k: E.tensor_single_scalar(out=SK[k + 1][:, :], in_=q["t1"][:, :], scalar=2.0, op=ALU.mult), [q["t1"]], [SK[k + 1]])
        TC = 512
        NL = 9
        isw = P.tile("isw", [128, 2, 128], F32)
        isw.ds = P.new_ds()
        P.dma("sp", isw[:, 0, :], self.c_mats[:, 0:128], isw.ds, w=[isw])
        P.dma("sp", isw[:, 1, :], self.c_mats[:, 384:512], isw.ds, w=[isw])
        s9s = P.tile("s9s", [128, 16], F32)
        dve(lambda E: E.tensor_scalar(out=s9s[:, :], in0=SK[NL][:, :], scalar1=sg[:, 0:1], scalar2=None, op0=ALU.mult), [SK[NL], sg], [s9s])
        TB = [P.tile(f"TB{i}", [128, 4, TC], F32) for i in range(8)]
        Rts = [P.tile(f"Rt{i}", [128, TC], F32) for i in range(8)]
        RotF = P.tile("RotF", [128, 128], F32)
        RotH = [P.tile(f"RotH{i}", [128, 128], BF16) for i in range(8)]
        RotL = [P.tile(f"RotL{i}", [128, 128], BF16) for i in range(8)]
        winit = [P.tile(f"winit{i}", [128, 1], F32) for i in range(8)]
        whl = [P.tile(f"whl{i}", [128, 2], BF16) for i in range(4)]
        onesf = P.tile("onesf", [128, TC], F32)
        P.op("dve", lambda E: E.memset(onesf[:, :], 1.0), w=[onesf])
        U = P.tile("U", [128, L], BF16)
        U.ds = P.new_ds()
        mats = {n: P.tile("m_" + n, [128, 8, 128], BF16) for n in ("BA", "BAs", "C1", "C2")}
        for n, t in mats.items():
            t.ds = P.pds("m_" + n)
            P.op("dve", lambda E, t=t: E.memset(t[:, :, :], 0.0), w=[t])
        NR = 4
        v1 = [P.tile(f"v1_{i}", [128, TC], F32) for i in range(NR)]
        v2 = [P.tile(f"v2_{i}", [128, TC], F32) for i in range(NR)]
        Vt = [P.tile(f"V_{i}", [128, TC], F32) for i in range(NR)]
        Wc = [P.tile(f"Wc_{i}", [128, TC], F32) for i in range(NR)]
        P1 = [P.tile(f"P1_{i}", [128, TC], BF16) for i in range(NR)]
        P2 = [P.tile(f"P2_{i}", [128, TC], BF16) for i in range(NR)]
        yb = [P.tile(f"yb_{i}", [128, TC], F32) for i in range(2)]
        hb = [P.tile(f"hb_{i}", [128, TC], BF16) for i in range(2)]
        for t in hb:
            t.ds = P.new_ds()
        psR = P.psum[6]
        it = 0
        for jt in range(2):
            P.dma("sp", U[:, :], S["UT"][jt * 128:(jt + 1) * 128, :], U.ds, w=[U])
            for n, srcs in (("BA", BA), ("BAs", BAs), ("C1", C1), ("C2", C2)):
                t = mats[n]
                P._waits("pool", [], [t.b], [])
                for gi in range(8):
                    g = jt * 8 + gi
                    if n in ("BA", "BAs"):
                        P.dma("pool", t[16 * gi:16 * gi + 16, gi, :], srcs[g], t.ds)
                    else:
                        P.dma("pool", t[:, gi, 16 * gi:16 * gi + 16], srcs[g], t.ds)
                t.b.w = {t.ds: t.ds.n}
                t.b.r = {}
            for gi in range(8):
                g = jt * 8 + gi
                gs = slice(g, g + 1)
                tb = TB[gi]
                cb, sb = Buf("cb"), Buf("sb")
                dve(lambda E, tb=tb: E.memset(tb[:, 0, 0:1], 1.0), [], [tb])
                dve(lambda E, tb=tb: E.memset(tb[:, 1, 0:1], 0.0), [], [tb])
                for k in range(NL):
                    n = 1 << k
                    dve(lambda E, k=k, n=n, gs=gs, tb=tb: E.tensor_scalar(out=tb[:, 0, n:2 * n], in0=tb[:, 0, 0:n], scalar1=CK[k][:, gs], scalar2=None, op0=ALU.mult),
                        [tb, CK[k]], [cb])
                    dve(lambda E, k=k, n=n, gs=gs, tb=tb: E.tensor_scalar(out=tb[:, 1, n:2 * n], in0=tb[:, 1, 0:n], scalar1=CK[k][:, gs], scalar2=None, op0=ALU.mult),
                        [tb, CK[k]], [sb])
                    dve(lambda E, k=k, n=n, gs=gs, tb=tb: E.scalar_tensor_tensor(out=tb[:, 0, n:2 * n], in0=tb[:, 1, 0:n], scalar=NSK[k][:, gs], in1=tb[:, 0, n:2 * n],
                                                                               op0=ALU.mult, op1=ALU.add), [tb, NSK[k], cb], [cb])
                    dve(lambda E, k=k, n=n, gs=gs, tb=tb: E.scalar_tensor_tensor(out=tb[:, 1, n:2 * n], in0=tb[:, 0, 0:n], scalar=SK[k][:, gs], in1=tb[:, 1, n:2 * n],
                                                                               op0=ALU.mult, op1=ALU.add), [tb, SK[k], sb], [sb])
                    tb.b.w = dict(cb.w)
                    tb.b.w.update(sb.w)
                    tb.b.r = {}
                dve(lambda E, gs=gs, tb=tb: E.tensor_scalar(out=tb[:, 2, :], in0=tb[:, 0, :], scalar1=q["cre"][:, gs], scalar2=None, op0=ALU.mult), [tb, q["cre"]], [tb])
                dve(lambda E, gs=gs, tb=tb: E.scalar_tensor_tensor(out=tb[:, 2, :], in0=tb[:, 1, :], scalar=q["cim"][:, gs], in1=tb[:, 2, :], op0=ALU.mult, op1=ALU.add),
                    [tb, q["cim"]], [tb])
                dve(lambda E, gs=gs, tb=tb: E.tensor_scalar(out=tb[:, 3, :], in0=tb[:, 0, :], scalar1=q["a2"][:, gs], scalar2=None, op0=ALU.mult), [tb, q["a2"]], [tb])
                dve(lambda E, gs=gs, tb=tb: E.scalar_tensor_tensor(out=tb[:, 3, :], in0=tb[:, 1, :], scalar=q["b2"][:, gs], in1=tb[:, 3, :], op0=ALU.mult, op1=ALU.add),
                    [tb, q["b2"]], [tb])
                dve(lambda E, tb=tb: E.tensor_scalar(out=tb[:, 0, :], in0=tb[:, 0, :], scalar1=sg[:, 0:1], scalar2=None, op0=ALU.mult), [tb, sg], [tb])
                dve(lambda E, tb=tb: E.tensor_single_scalar(out=tb[:, 1, :], in_=tb[:, 1, :], scalar=-1.0, op=ALU.mult), [tb], [tb])
                dve(lambda E, gs=gs, gi=gi: E.tensor_scalar(out=Rts[gi][:, :], in0=onesf[:, :], scalar1=q["r"][:, gs], scalar2=None, op0=ALU.mult), [onesf, q["r"]], [Rts[gi]])
                dve(lambda E, gs=gs: E.tensor_scalar(out=RotF[:, :], in0=isw[:, 0, :], scalar1=CK[NL][:, gs], scalar2=None, op0=ALU.mult), [isw, CK[NL]], [RotF])
                dve(lambda E, gs=gs: E.scalar_tensor_tensor(out=RotF[:, :], in0=isw[:, 1, :], scalar=s9s[:, gs], in1=RotF[:, :], op0=ALU.mult, op1=ALU.add),
                    [isw, s9s, RotF], [RotF])
                dve(lambda E, gi=gi: E.tensor_copy(out=RotH[gi][:, :], in_=RotF[:, :]), [RotF], [RotH[gi]])
                dve(lambda E, gi=gi: E.tensor_tensor(out=RotL[gi][:, :], in0=RotF[:, :], in1=RotH[gi][:, :], op=ALU.subtract), [RotF, RotH[gi]], [RotL[gi]])
            units = []
            for ch in range(NCH):
                for gi in range(8):
                    units.append((ch, gi, it))
                    it += 1

            def s1(ch, gi, itx):
                csl = slice(ch * TC, (ch + 1) * TC)
                psA, psA2 = P.psum[itx % 2], P.psum[2 + itx % 2]
                a, b, V_ = v1[itx % NR], v2[itx % NR], Vt[itx % NR]
                tb = TB[gi]
                P.op("pe", lambda E: E.matmul(psA[:, :], lhsT=mats["BA"][:, gi, :], rhs=U[:, csl], start=True, stop=True),
                     r=[mats["BA"], U], w=[psA])
                P.op("pe", lambda E: E.matmul(psA2[:, :], lhsT=mats["BAs"][:, gi, :], rhs=U[:, csl], start=True, stop=True),
                     r=[mats["BAs"], U], w=[psA2])
                dve(lambda E: E.tensor_tensor(out=a[:, :], in0=psA[:, :], in1=tb[:, 2, :], op=ALU.mult), [psA, tb], [a])
                dve(lambda E: E.tensor_tensor(out=b[:, :], in0=psA2[:, :], in1=tb[:, 3, :], op=ALU.mult), [psA2, tb], [b])
                P.op("pool", lambda E: E.tensor_tensor(out=V_[:, :], in0=a[:, :], in1=b[:, :], op=ALU.add), r=[a, b], w=[V_])

            def s2(ch, gi, itx, jt=jt):
                psY = P.psum[4 + ch % 2]
                V_, W_, p1, p2 = Vt[itx % NR], Wc[itx % NR], P1[itx % NR], P2[itx % NR]
                tb = TB[gi]
                wi = winit[gi]
                if ch == 0:
                    dve(lambda E: E.tensor_tensor_scan(out=W_[:, :], data0=Rts[gi][:, :], data1=V_[:, :], initial=0.0,
                                                       op0=ALU.mult, op1=ALU.add), [Rts[gi], V_], [W_])
                else:
                    dve(lambda E: E.tensor_tensor_scan(out=W_[:, :], data0=Rts[gi][:, :], data1=V_[:, :], initial=wi[:, 0:1],
                                                       op0=ALU.mult, op1=ALU.add), [Rts[gi], V_, wi], [W_])
                P.op("pool", lambda E: E.tensor_tensor(out=p1[:, :], in0=W_[:, :], in1=tb[:, 0, :], op=ALU.mult), r=[W_, tb], w=[p1])
                P.op("pool", lambda E: E.tensor_tensor(out=p2[:, :], in0=W_[:, :], in1=tb[:, 1, :], op=ALU.mult), r=[W_, tb], w=[p2])
                if ch + 1 < NCH:
                    hl = whl[itx % 4]
                    dve(lambda E: E.tensor_copy(out=hl[:, 0:1], in_=W_[:, TC - 1:TC]), [W_], [hl])
                    dve(lambda E: E.tensor_tensor(out=hl[:, 1:2], in0=W_[:, TC - 1:TC], in1=hl[:, 0:1], op=ALU.subtract), [W_, hl], [hl])

            def s3(ch, gi, itx, jt=jt):
                psY = P.psum[4 + ch % 2]
                p1, p2 = P1[itx % NR], P2[itx % NR]
                wi = winit[gi]
                P.op("pe", lambda E: E.matmul(psY[:, :], lhsT=mats["C1"][:, gi, :], rhs=p1[:, :], start=(gi == 0), stop=False),
                     r=[mats["C1"], p1], w=[psY], sig=False)
                P.op("pe", lambda E: E.matmul(psY[:, :], lhsT=mats["C2"][:, gi, :], rhs=p2[:, :], start=False, stop=(gi == 7)),
                     r=[mats["C2"], p2], w=[psY], sig=True)
                if ch + 1 < NCH:
                    hl = whl[itx % 4]
                    P.op("pe", lambda E: E.matmul(psR[:, gi:gi + 1], lhsT=RotH[gi][:, :], rhs=hl[:, 0:1], start=True, stop=False),
                         r=[RotH[gi], hl], w=[psR], sig=False)
                    P.op("pe", lambda E: E.matmul(psR[:, gi:gi + 1], lhsT=RotH[gi][:, :], rhs=hl[:, 1:2], start=False, stop=False),
                         r=[RotH[gi], hl], w=[psR], sig=False)
                    P.op("pe", lambda E: E.matmul(psR[:, gi:gi + 1], lhsT=RotL[gi][:, :], rhs=hl[:, 0:1], start=False, stop=True),
                         r=[RotL[gi], hl], w=[psR], sig=True)
                    dve(lambda E: E.tensor_copy(out=wi[:, 0:1], in_=psR[:, gi:gi + 1]), [psR], [wi])
                if gi == 7:
                    csl = slice(ch * TC, (ch + 1) * TC)
                    y_, h_ = yb[ch % 2], hb[ch % 2]
                    dve(lambda E: E.tensor_copy(out=y_[:, :], in_=psY[:, :]), [psY], [y_])
                    dve(lambda E: E.scalar_tensor_tensor(out=y_[:, :], in0=U[:, csl], scalar=dsk[:, jt:jt + 1], in1=y_[:, :],
                                                         op0=ALU.mult, op1=ALU.add), [U, dsk, y_], [y_])
                    act(lambda E: E.activation(out=h_[:, :], in_=y_[:, :], func=AF.Gelu_apprx_tanh), [y_], [h_])
                    P.dma("sp", S["ygl"][6 + jt][:, csl], h_[:, :], h_.ds, r=[h_])

            nu = len(units)
            for k in range(nu + 2):
                if k < nu:
                    s1(*units[k])
                if 0 <= k - 1 < nu:
                    s2(*units[k - 1])
                if 0 <= k - 2 < nu:
                    s3(*units[k - 2])
        if self.use_cc:
            evs = [(t.ds, t.ds.n) for t in hb if t.ds.n > 0]
            for a in (6, 7):
                P.allgather(S["ygl"][a].ap(), S["ygg"][a].ap(), P.ag_sem("ag_a"), after=evs)
        P.end()

    def ph_exchange(self):
        P, S = self.P, self.S
        if not self.use_cc:
            return
        P.begin()
        ds = P.pds("ag")
        for a in range(10):
            P.allgather(S["ygl"][a].ap(), S["ygg"][a].ap(), ds)
        P.allgather(S["bsl"].ap(), S["bsg"].ap(), ds)
        P.end()

    def prefetch_wg(self, wglu):
        P = self.P
        t = self.xstack.enter_context(self.nc.sbuf_tensor(f"Wg_{P.phase_no}", [128, 8, 2048], BF16))
        Wg = Tile(t, "Wg")
        Wg.ds = P.pds("wg")
        for r in range(4):
            for i in range(2):
                row0 = 256 * r + 128 * i
                P.dma("pool", Wg[:, r * 2 + i, :], wglu[row0:row0 + 128, :], Wg.ds)
        self.Wg = Wg

    def prefetch_wo_alloc(self):
        P = self.P
        t = self.xstack.enter_context(self.nc.sbuf_tensor(f"Wo_{P.phase_no}", [128, 32, 1024], BF16))
        self.Wo_t = Tile(t, "Wo")

    def prefetch_wo(self, wout):
        P = self.P
        if getattr(self, "Wo_t", None) is None:
            self.prefetch_wo_alloc()
        Wo = self.Wo_t
        self.Wo_t = None
        Wo.ds = P.pds("wo")
        kt = 0
        for a in range(2):
            for r in range(4):
                row0 = 256 * r + 128 * a
                P.dma("pool", Wo[:, kt, :], wout[row0:row0 + 128, :], Wo.ds)
                kt += 1
        for i in range(4):
            for r in range(4):
                row0 = 1024 + 512 * r + 128 * i
                P.dma("pool", Wo[:, kt, :], wout[row0:row0 + 128, :], Wo.ds)
                kt += 1
        for nt in range(8):
            row0 = 3072 + 128 * nt
            P.dma("pool", Wo[:, kt, :], wout[row0:row0 + 128, :], Wo.ds)
            kt += 1
        self.Wo = Wo

    def ph_glu(self, wglu, bglu, gssm, YS, wout_next=None):
        P, L, NCH, S = self.P, self.L, self.NCH, self.S
        P.begin()
        if wout_next is not None:
            self.prefetch_wo_alloc()
        ones = P.tile("ones", [128, 128], BF16)
        ones.ds = P.pds("ones")
        P.dma("pool", ones[:, :], self.c_mats[:, 256:384], ones.ds, w=[ones])
        Wg = P.tile("Wg", [128, 8, 2048], BF16)
        Wg.ds = P.pds("wg")
        for r in range(4):
            for i in range(2):
                row0 = 256 * r + 128 * i
                P.dma("pool", Wg[:, r * 2 + i, :], wglu[row0:row0 + 128, :], Wg.ds)
        Wg.b.w = {Wg.ds: Wg.ds.n}
        if wout_next is not None:
            self.prefetch_wo(wout_next)
        sm = P.tile("sm", [128, 24], F32)
        sm.ds = P.new_ds()
        P.dma("sp", sm[:, 0:16], bglu[:, :], sm.ds, w=[sm])
        P.dma("sp", sm[:, 16:24], gssm[:, :], sm.ds, w=[sm])
        HGs = [P.tile(f"HG{i}", [128, 4, 2, 512], BF16) for i in range(2)]
        SGs = [P.tile(f"SG{i}", [128, 4, 2, 512], BF16) for i in range(2)]
        ys = [P.tile(f"ys{i}", [128, 8, 512], F32) for i in range(2)]
        sgm = [P.tile(f"sgm{i}", [128, 512], F32) for i in range(2)]
        sqb = [P.tile(f"sqb{i}", [128, 512], BF16) for i in range(2)]
        Rs = [P.tile(f"Rs{i}", [128, 512], F32) for i in range(2)]
        tmp = [P.tile(f"tmp{i}", [128, 512], F32) for i in range(2)]
        ob = [P.tile(f"ob{i}", [128, 512], BF16) for i in range(4)]
        for t in HGs + SGs + ob:
            t.ds = P.new_ds()

        if self.use_cc:
            P.wait_all("sp", P.ag_sem("ag_a"))

        def load(ch):
            csl = slice(ch * 512, (ch + 1) * 512)
            for i in range(2):
                P.dma("sp", HGs[ch % 2][:, :, i, :], S["ygg"][6 + i][:, csl].rearrange("(r p) t -> p r t", p=128), HGs[ch % 2].ds, w=[HGs[ch % 2]])
                P.dma("sp", SGs[ch % 2][:, :, i, :], S["ygg"][8 + i][:, csl].rearrange("(r p) t -> p r t", p=128), SGs[ch % 2].ds, w=[SGs[ch % 2]])

        load(0)
        it = 0
        oi = 0
        for ch in range(NCH):
            if ch + 1 < NCH:
                load(ch + 1)
            csl = slice(ch * 512, (ch + 1) * 512)
            HG, SG, Y, R = HGs[ch % 2], SGs[ch % 2], ys[ch % 2], Rs[ch % 2]
            pss = P.psum[4 + ch % 2]
            for nt in range(8):
                psv, psg = P.psum[it % 2], P.psum[2 + it % 2]
                g_, q_ = sgm[it % 2], sqb[it % 2]
                it += 1
                for kt in range(8):
                    r, i = kt // 2, kt % 2
                    P.op("pe", lambda E, psv=psv, kt=kt, nt=nt, r=r, i=i, HG=HG: E.matmul(
                        psv[:, :], lhsT=Wg[:, kt, nt * 128:(nt + 1) * 128], rhs=HG[:, r, i, :], start=(kt == 0), stop=(kt == 7)),
                        r=[Wg, HG], w=[psv], sig=(kt == 7))
                for kt in range(8):
                    r, i = kt // 2, kt % 2
                    P.op("pe", lambda E, psg=psg, kt=kt, nt=nt, r=r, i=i, HG=HG: E.matmul(
                        psg[:, :], lhsT=Wg[:, kt, 1024 + nt * 128:1024 + (nt + 1) * 128], rhs=HG[:, r, i, :], start=(kt == 0), stop=(kt == 7)),
                        r=[Wg, HG], w=[psg], sig=(kt == 7))
                P.op("act", lambda E, g_=g_, psg=psg, nt=nt: E.activation(out=g_[:, :], in_=psg[:, :], func=AF.Sigmoid, bias=sm[:, 8 + nt:9 + nt]),
                     r=[psg, sm], w=[g_])
                P.op("dve", lambda E, Y=Y, nt=nt, psv=psv, g_=g_: E.scalar_tensor_tensor(out=Y[:, nt, :], in0=psv[:, :], scalar=sm[:, nt:nt + 1], in1=g_[:, :],
                                                                                     op0=ALU.add, op1=ALU.mult), r=[psv, sm, g_], w=[Y])
                P.op("act", lambda E, q_=q_, Y=Y, nt=nt: E.activation(out=q_[:, :], in_=Y[:, nt, :], func=AF.Square), r=[Y], w=[q_])
                P.op("pe", lambda E, pss=pss, q_=q_, nt=nt: E.matmul(pss[:, :], lhsT=ones[:, :], rhs=q_[:, :], start=(nt == 0), stop=(nt == 7)),
                     r=[ones, q_], w=[pss], sig=True)
            P.op("dve", lambda E, R=R, pss=pss: E.tensor_scalar(out=R[:, :], in0=pss[:, :], scalar1=1.0 / 1024, scalar2=EPS, op0=ALU.mult, op1=ALU.add),
                 r=[pss], w=[R])
            P.op("act", lambda E, R=R: E.activation(out=R[:, :], in_=R[:, :], func=AF.Sqrt), r=[R], w=[R])
            P.op("dve", lambda E, R=R: E.reciprocal(out=R[:, :], in_=R[:, :]), r=[R], w=[R])
            for nt in range(8):
                r, i = nt // 2, nt % 2
                t_ = tmp[nt % 2]
                o_ = ob[oi % 4]
                oi += 1
                P.op("dve", lambda E, t_=t_, Y=Y, nt=nt, SG=SG, r=r, i=i: E.scalar_tensor_tensor(
                    out=t_[:, :], in0=Y[:, nt, :], scalar=sm[:, 16 + nt:17 + nt], in1=SG[:, r, i, :], op0=ALU.mult, op1=ALU.mult),
                    r=[Y, sm, SG], w=[t_])
                P.op("pool", lambda E, o_=o_, t_=t_, R=R: E.tensor_tensor(out=o_[:, :], in0=t_[:, :], in1=R[:, :], op=ALU.mult), r=[t_, R], w=[o_])
                P.dma("sp", YS[nt * 128:(nt + 1) * 128, csl], o_[:, :], o_.ds, r=[o_])
        P.end()

    def ph_outproj(self, wout, YS, xin, xout):
        P, L, NCH, S = self.P, self.L, self.NCH, self.S
        P.begin()
        if getattr(self, "Wo", None) is None:
            self.prefetch_wo(wout)
        Wo = self.Wo
        self.Wo = None
        Wo.b = Buf("Wo")
        Wo.b.w = {Wo.ds: Wo.ds.n}
        sel = P.tile("sel", [8, 256], BF16)
        sel.ds = P.pds("w0")
        P.dma("pool", sel[:, :], self.c_sel[:, :], sel.ds, w=[sel])
        st8 = [P.tile(f"st8_{i}", [8, 512], F32) for i in range(2)]
        shi = [P.tile(f"shi{i}", [8, 512], BF16) for i in range(2)]
        slo = [P.tile(f"slo{i}", [8, 512], BF16) for i in range(2)]
        for t in st8:
            t.ds = P.new_ds()
        A24 = [P.tile(f"A24_{i}", [128, 6, 4, 512], BF16) for i in range(2)]
        YSc = [P.tile(f"YSc{i}", [128, 8, 512], BF16) for i in range(2)]
        Rb = [[P.tile(f"Rb{b}_{i}", [128, 512], F32) for i in range(2)] for b in range(2)]
        xt = [P.tile(f"xt{i}", [128, 512], F32) for i in range(4)]
        xo = [P.tile(f"xo{i}", [128, 512], F32) for i in range(4)]
        for t in A24 + YSc + xt + xo:
            t.ds = P.new_ds()

        if self.use_cc:
            P.wait_all("sp", P.ag_sem("ag_a"))
            P.wait_all("sp", P.ag_sem("ag_b"))

        def load(ch):
            csl = slice(ch * 512, (ch + 1) * 512)
            A = A24[ch % 2]
            for a in range(6):
                P.dma("sp", A[:, a, :, :], S["ygg"][a][:, csl].rearrange("(r p) t -> p r t", p=128), A.ds, w=[A])
            Y = YSc[ch % 2]
            P.dma("sp", Y[:, :, :], YS[:, csl].rearrange("(n p) t -> p n t", p=128), Y.ds, w=[Y])

        load(0)
        xi = 0
        for ch in range(NCH):
            if ch + 1 < NCH:
                load(ch + 1)
            csl = slice(ch * 512, (ch + 1) * 512)
            A, Y = A24[ch % 2], YSc[ch % 2]
            s8, hi8, lo8 = st8[ch % 2], shi[ch % 2], slo[ch % 2]
            P.dma("sp", s8[:, :], S["bsg"][:, csl], s8.ds, w=[s8])
            P.op("dve", lambda E, s8=s8, hi8=hi8: E.tensor_copy(out=hi8[:, :], in_=s8[:, :]), r=[s8], w=[hi8])
            P.op("dve", lambda E, s8=s8, hi8=hi8, lo8=lo8: E.tensor_tensor(out=lo8[:, :], in0=s8[:, :], in1=hi8[:, :], op=ALU.subtract), r=[s8, hi8], w=[lo8])
            for b in range(2):
                ps = P.psum[6 + b]
                R = Rb[b][ch % 2]
                n_b = 1024.0 if b == 0 else 2048.0
                P.op("pe", lambda E, ps=ps, b=b, hi8=hi8: E.matmul(ps[:, :], lhsT=sel[:, b * 128:(b + 1) * 128], rhs=hi8[:, :], start=True, stop=False),
                     r=[sel, hi8], w=[ps], sig=False)
                P.op("pe", lambda E, ps=ps, b=b, lo8=lo8: E.matmul(ps[:, :], lhsT=sel[:, b * 128:(b + 1) * 128], rhs=lo8[:, :], start=False, stop=True),
                     r=[sel, lo8], w=[ps])
                P.op("dve", lambda E, R=R, ps=ps, n_b=n_b: E.tensor_scalar(out=R[:, :], in0=ps[:, :], scalar1=1.0 / n_b, scalar2=EPS, op0=ALU.mult, op1=ALU.add),
                     r=[ps], w=[R])
                P.op("act", lambda E, R=R: E.activation(out=R[:, :], in_=R[:, :], func=AF.Sqrt), r=[R], w=[R])
                P.op("dve", lambda E, R=R: E.reciprocal(out=R[:, :], in_=R[:, :]), r=[R], w=[R])
            cnt = 0
            for a in range(6):
                R = Rb[0 if a < 2 else 1][ch % 2]
                for r in range(4):
                    eng = "dve" if cnt % 2 == 0 else "pool"
                    cnt += 1
                    P.op(eng, lambda E, A=A, a=a, r=r, R=R: E.tensor_tensor(out=A[:, a, r, :], in0=A[:, a, r, :], in1=R[:, :], op=ALU.mult), r=[A, R], w=[A])
            for no in range(8):
                ps = P.psum[no % 4]
                x_, o_ = xt[xi % 4], xo[xi % 4]
                xi += 1
                P.dma("sp", x_[:, :], xin[no * 128:(no + 1) * 128, csl], x_.ds, w=[x_])
                for kt in range(32):
                    if kt < 24:
                        rhs_t, rhs = A, A[:, kt // 4, kt % 4, :]
                    else:
                        rhs_t, rhs = Y, Y[:, kt - 24, :]
                    P.op("pe", lambda E, ps=ps, kt=kt, no=no, rhs=rhs: E.matmul(ps[:, :], lhsT=Wo[:, kt, no * 128:(no + 1) * 128], rhs=rhs,
                                                                            start=(kt == 0), stop=(kt == 31)), r=[Wo, rhs_t], w=[ps], sig=(kt == 31))
                P.op("dve", lambda E, ps=ps, x_=x_, o_=o_: E.tensor_tensor(out=o_[:, :], in0=ps[:, :], in1=x_[:, :], op=ALU.add), r=[ps, x_], w=[o_])
                P.dma("sp", xout[no * 128:(no + 1) * 128, csl], o_[:, :], o_.ds, r=[o_])
        P.end()
        self.xstack.close()
        self.xstack = contextlib.ExitStack()


def build_full(L, depth=DEPTH, use_cc=True):
    m = MK(L, use_cc=use_cc)
    m.consts_dram()
    m.c_sel = m.dram("c_sel", [8, 256], F32, "ExternalInput")
    S = m.scratch()
    ext = lambda n, shp: m.dram(n, shp, F32, "ExternalInput")
    xT = ext("xT", [1024, L])
    poolsel = ext("poolsel", [128, 4])
    poolselT = ext("poolselT", [4, 128])
    fing = ext("fing", [128, 8])
    outT = m.dram("outT", [1024, L], F32, "ExternalOutput")
    stl = m.dram("stl", [1, L], F32)
    stg = m.dram("stg", [4, L], F32)
    htl = [m.dram(f"htl{i}", [128, L], BF16) for i in range(8)]
    htg = [m.dram(f"htg{i}", [512, L], BF16) for i in range(8)]
    YS = m.dram("YS", [1024, L], BF16)
    XT = [m.dram(f"XT{l}", [1024, L], F32) for l in range(depth)]
    xin = xT
    for l in range(depth):
        lng = ext(f"lng{l}", [128, 8])
        win = ext(f"win{l}", [4096, 3072])
        wpool = ext(f"wpool{l}", [256, 256])
        pscale = ext(f"pscale{l}", [128, 2])
        gpool = ext(f"gpool{l}", [128, 2])
        gattn = ext(f"gattn{l}", [128, 4])
        gssm = ext(f"gssm{l}", [128, 8])
        lam = ext(f"ssm_lam{l}", [128, 32])
        ldt = ext(f"ssm_ldt{l}", [128, 16])
        BA = ext(f"ssm_BA{l}", [16, 16, 128])
        BAs = ext(f"ssm_BAs{l}", [16, 16, 128])
        C1 = ext(f"ssm_C1{l}", [16, 128, 16])
        C2 = ext(f"ssm_C2{l}", [16, 128, 16])
        dsk = ext(f"dskip{l}", [128, 2])
        wglu = ext(f"wglu{l}", [1024, 2048])
        bglu = ext(f"bglu{l}", [128, 16])
        wout = ext(f"wout{l}", [4096, 1024])
        m.ph_norm_stats(xin, stl, stg)
        m.ph_norm_apply(xin, lng, stg, htl, htg)
        m.ph_inproj(htg, win)
        m.ph_ssm(lam, ldt, BA, BAs, C1, C2, dsk)
        m.ph_pool(wpool, pscale, gpool, poolsel, poolselT)
        m.ph_attn(gattn)
        m.ph_glu(wglu, bglu, gssm, YS, wout_next=wout)
        m.ph_outproj(wout, YS, xin, XT[l])
        xin = XT[l]
    m.ph_norm_stats(xin, stl, stg)
    m.ph_norm_apply(xin, fing, stg, htl, htg, final_out=outT)
    return m


def host_consts(L):
    ident = np.eye(128, dtype=np.float32)
    negtri = -(np.arange(128)[:, None] >= np.arange(128)[None, :]).astype(np.float32)
    ones = np.ones((128, 128), np.float32)
    swap = np.zeros((128, 128), np.float32)
    swap[np.arange(128), (np.arange(128) + 64) % 128] = 1.0
    mats = np.concatenate([ident, negtri, ones, swap], axis=1)
    mask = np.zeros((128, 4, 512), np.float32)
    for r in range(4):
        mask[:, r, :] = np.where((128 * r + np.arange(128))[:, None] >= np.arange(512)[None, :], NEG, 0.0)
    pos = np.arange(1, L + 1, dtype=np.float32)
    invc = np.stack([1.0 / np.minimum(pos, float(w)) for w in (2, 4, 8, 16)]).astype(np.float32)
    sel = np.zeros((8, 256), np.float32)
    sel[0::2, 0:128] = 1.0
    sel[1::2, 128:256] = 1.0
    return {"c_mats": mats, "c_mask": mask.reshape(128, 2048), "c_invc": invc, "c_sel": sel}


def _pt(v, n):
    return np.ascontiguousarray(np.asarray(v, np.float32).reshape(n, 128).T)


def host_inputs(inp, L, depth=DEPTH):
    cs = host_consts(L)
    maps = []
    for core in range(8):
        b, c = core // 4, core % 4
        d = dict(cs)
        d["xT"] = np.ascontiguousarray(inp["x"][b][:, c * 1024:(c + 1) * 1024].T)
        sel = np.zeros((128, 4), np.float32)
        sel[:, c] = 1.0
        d["poolsel"] = sel
        d["poolselT"] = np.ascontiguousarray(sel.T)
        d["fing"] = _pt(inp["final_g"][c * 1024:(c + 1) * 1024], 8)
        for l in range(depth):
            w_in = inp["w_in"][l]
            cols = np.concatenate([
                np.arange(256 * c, 256 * c + 256), 1024 + np.arange(256 * c, 256 * c + 256),
                2048 + np.arange(512 * c, 512 * c + 512), 4096 + np.arange(512 * c, 512 * c + 512),
                6144 + np.arange(512 * c, 512 * c + 512), 8192 + np.arange(512 * c, 512 * c + 512),
                10240 + np.arange(256 * c, 256 * c + 256), 11264 + np.arange(256 * c, 256 * c + 256)])
            d[f"lng{l}"] = _pt(inp["ln_g"][l][c * 1024:(c + 1) * 1024], 8)
            d[f"win{l}"] = np.ascontiguousarray(w_in[:, cols])
            d[f"wpool{l}"] = np.ascontiguousarray(inp["w_pool"][l][c])
            d[f"pscale{l}"] = _pt(inp["pool_scale"][l][256 * c:256 * c + 256], 2)
            bg = inp["branch_g"][l]
            d[f"gpool{l}"] = _pt(bg[256 * c:256 * c + 256], 2)
            d[f"gattn{l}"] = _pt(bg[1024 + 512 * c:1024 + 512 * c + 512], 4)
            d[f"gssm{l}"] = _pt(bg[3072:4096], 8)
            gs = slice(16 * c, 16 * c + 16)
            lre = inp["lam_re"][l][gs].T
            lim = inp["lam_im"][l][gs].T
            d[f"ssm_lam{l}"] = np.ascontiguousarray(np.concatenate(
                [np.concatenate([lre, lre], 0), np.concatenate([lim, lim], 0)], axis=1).astype(np.float32))
            d[f"ssm_ldt{l}"] = np.ascontiguousarray(np.broadcast_to(inp["log_dt"][l][gs][None, :], (128, 16)).astype(np.float32))
            bre = inp["b_re"][l][gs].transpose(0, 2, 1)
            bim = inp["b_im"][l][gs].transpose(0, 2, 1)
            d[f"ssm_BA{l}"] = np.ascontiguousarray(np.concatenate([bre, bim], axis=2))
            d[f"ssm_BAs{l}"] = np.ascontiguousarray(np.concatenate([bim, bre], axis=2))
            cre = inp["c_re"][l][gs].transpose(0, 2, 1)
            cim = inp["c_im"][l][gs].transpose(0, 2, 1)
            d[f"ssm_C1{l}"] = np.ascontiguousarray(np.concatenate([cre, cim], axis=1))
            d[f"ssm_C2{l}"] = np.ascontiguousarray(np.concatenate([cim, cre], axis=1))
            d[f"dskip{l}"] = _pt(inp["d_skip"][l][256 * c:256 * c + 256], 2)
            d[f"wglu{l}"] = np.ascontiguousarray(inp["w_glu"][l])
            d[f"bglu{l}"] = _pt(inp["b_glu"][l], 16)
            d[f"wout{l}"] = np.ascontiguousarray(inp["w_out"][l][:, c * 1024:(c + 1) * 1024])
        maps.append(d)
    return maps


def kernel(**inputs):
    inp = {k: np.asarray(v) for k, v in inputs.items()}
    L = inp["x"].shape[1]
    m = build_full(L)
    maps = host_inputs(inp, L)
    res = run_bass_kernel_spmd(m.nc, maps, core_ids=list(range(8)))
    out = np.empty((NB, L, D), np.float32)
    for core in range(8):
        b, c = core // 4, core % 4
        out[b][:, c * 1024:(c + 1) * 1024] = res.results[core]["outT"].T
    return out
```

```python
import math
import contextlib
import numpy as np
import ml_dtypes
import concourse.bass as bass
import concourse.mybir as mybir
from concourse.bass_utils import run_bass_kernel_spmd

F32 = mybir.dt.float32
BF16 = mybir.dt.bfloat16
AF = mybir.ActivationFunctionType
ALU = mybir.AluOpType

D = 4096
NB = 2
DEPTH = 2
EPS = 1e-6
GROUPS = [[0, 1, 2, 3], [4, 5, 6, 7]]
NEG = -30000.0
ENGS = ("pe", "act", "dve", "pool", "sp")
CENGS = ("pe", "act", "dve", "pool")


class DSem:
    def __init__(self, h):
        self.h = h
        self.n = 0
        self.persist = False


class Buf:
    def __init__(self, name=""):
        self.name = name
        self.w = {}
        self.r = {}
        self.excl = False


class Tile:
    def __init__(self, t, name):
        self.t = t
        self.b = Buf(name)
        self.ds = None

    def __getitem__(self, k):
        return self.t[k]


class Prog:
    def __init__(self, L):
        self.L = L
        self.nc = bass.Bass("TRN2", target_bir_lowering=False)
        nc = self.nc
        self.glob = contextlib.ExitStack()
        self.esem = {e: nc.alloc_semaphore(name=f"e_{e}") for e in CENGS}
        self.ph_a = nc.alloc_semaphore(name="ph_a")
        self.ph_b = nc.alloc_semaphore(name="ph_b")
        self.dsems = [DSem(nc.alloc_semaphore(name=f"d{i}")) for i in range(90)]
        self.psum = []
        for i in range(8):
            t = self.glob.enter_context(nc.psum_tensor(f"ps{i}", [128, 512], F32))
            self.psum.append(Tile(t, f"ps{i}"))
            self.psum[-1].b.excl = True
        self.phase_no = 0
        self.in_phase = False
        self.persist_lo = len(self.dsems)
        self.pnamed = {}

    def begin(self):
        assert not self.in_phase
        self.in_phase = True
        self.ops = {e: [] for e in ENGS}
        self.cnt = {e: 0 for e in CENGS}
        self.seen = {e: {} for e in ENGS}
        self.ds_next = 0
        for d in self.dsems[: self.persist_lo]:
            d.n = 0
        for p in self.psum:
            p.b.w = {}
            p.b.r = {}
        self.sb = contextlib.ExitStack()

    def pds(self, name):
        if name not in self.pnamed:
            self.pnamed[name] = self.new_ds(persist=True)
        return self.pnamed[name]

    def new_ds(self, persist=False):
        if persist:
            self.persist_lo -= 1
            assert self.persist_lo >= self.ds_next
            self.dsems[self.persist_lo].persist = True
            return self.dsems[self.persist_lo]
        d = self.dsems[self.ds_next]
        self.ds_next += 1
        assert self.ds_next <= self.persist_lo
        return d

    def tile(self, name, shape, dt):
        t = self.sb.enter_context(self.nc.sbuf_tensor(f"{name}_{self.phase_no}", list(shape), dt))
        return Tile(t, name)

    def end(self):
        used = self.dsems[: self.ds_next]
        pers = [d for d in self.dsems[self.persist_lo:] if not getattr(d, "nobarrier", False)]
        for e in ENGS:
            for p in CENGS:
                if self.cnt[p] > 0:
                    self._wait1(e, p, self.cnt[p])
            for d in used + pers:
                if d.n > 0:
                    self._wait1(e, d, d.n)
        self.phase_no += 1
        k = self.phase_no
        for e in ("pe", "act", "dve", "sp"):
            self.ops[e].append(lambda E: E.sem_inc(self.ph_a, 1))
        self.ops["pool"].append(lambda E: E.wait_ge(self.ph_a, 4 * k))
        for p in CENGS:
            self.ops["pool"].append(lambda E, s=self.esem[p]: E.sem_clear(s))
        for d in used:
            self.ops["pool"].append(lambda E, s=d.h: E.sem_clear(s))
        self.ops["pool"].append(lambda E: E.sem_inc(self.ph_b, 1))
        for e in ENGS:
            self.ops[e].append(lambda E: E.wait_ge(self.ph_b, k))
        ops = self.ops
        with self.nc.Block() as block:
            @block.tensor
            def _(E):
                for f in ops["pe"]:
                    f(E)

            @block.scalar
            def _(E):
                for f in ops["act"]:
                    f(E)

            @block.vector
            def _(E):
                for f in ops["dve"]:
                    f(E)

            @block.gpsimd
            def _(E):
                for f in ops["pool"]:
                    f(E)

            @block.sync
            def _(E):
                for f in ops["sp"]:
                    f(E)
        self.sb.close()
        self.in_phase = False

    def _wait1(self, eng, key, val):
        if self.seen[eng].get(key, 0) >= val:
            return
        self.seen[eng][key] = val
        if isinstance(key, DSem):
            self.ops[eng].append(lambda E, s=key.h, v=val: E.wait_ge(s, v))
        else:
            self.ops[eng].append(lambda E, s=self.esem[key], v=val: E.wait_ge(s, v))

    def _waits(self, eng, r, w, after):
        for b in r:
            for k, v in b.w.items():
                self._wait1(eng, k, v)
            if b.excl:
                for k, v in b.r.items():
                    if k != eng:
                        self._wait1(eng, k, v)
        for b in w:
            for k, v in b.w.items():
                if eng == "pe" and k == "pe":
                    continue
                self._wait1(eng, k, v)
            for k, v in b.r.items():
                self._wait1(eng, k, v)
        for ev in after:
            if ev is not None:
                self._wait1(eng, ev[0], ev[1])

    @staticmethod
    def _bufs(xs):
        return [x.b if isinstance(x, Tile) else x for x in xs]

    def _record(self, ev, r, w):
        k, v = ev
        for b in r:
            if b.r.get(k, 0) < v:
                b.r[k] = v
        for b in w:
            b.w = {k: v}
            b.r = {}

    def op(self, eng, fn, r=(), w=(), after=(), sig=True):
        r = self._bufs(r)
        w = self._bufs(w)
        self._waits(eng, r, w, after)
        if sig:
            self.cnt[eng] += 1
            ev = (eng, self.cnt[eng])
            self.ops[eng].append(lambda E, fn=fn, s=self.esem[eng]: fn(E).then_inc(s, 1))
        else:
            ev = (eng, self.cnt[eng] + 1)
            self.ops[eng].append(lambda E, fn=fn: fn(E))
        self._record(ev, r, w)
        return ev

    def dma(self, q, out, in_, ds, r=(), w=(), after=()):
        r = self._bufs(r)
        w = self._bufs(w)
        self._waits(q, r, w, after)
        assert (q != "pool") or ds.persist
        ds.n += 16
        ev = (ds, ds.n)
        self.ops[q].append(lambda E, o=out, i=in_, s=ds.h: E.dma_start(out=o, in_=i).then_inc(s, 16))
        self._record(ev, r, w)
        return ev

    def ag_sem(self, name):
        d = self.pds(name)
        d.nobarrier = True
        return d

    def wait_all(self, eng, ds):
        if ds.n > 0:
            self._wait1(eng, ds, ds.n)

    def allgather(self, in_ap, out_ap, ds, after=()):
        self._waits("pool", [], [], after)
        assert ds.persist
        ds.n += 1
        ev = (ds, ds.n)
        self.ops["pool"].append(
            lambda E, i=in_ap, o=out_ap, s=ds.h: E.collective_compute(
                "AllGather", ALU.bypass, replica_groups=GROUPS, ins=[i], outs=[o]
            ).then_inc(s, 1)
        )
        return ev


class MK:
    def __init__(self, L, ext_in=(), ext_out=(), use_cc=True):
        self.L = L
        self.NCH = L // 512
        self.P = Prog(L)
        self.nc = self.P.nc
        self.ext_in = set(ext_in)
        self.ext_out = set(ext_out)
        self.use_cc = use_cc
        self.dr = {}
        self.xstack = contextlib.ExitStack()
        self.Wg = None
        self.Wo = None

    def dram(self, name, shape, dt, kind=None):
        if name in self.dr:
            return self.dr[name]
        if kind is None:
            if name in self.ext_in:
                kind = "ExternalInput"
            elif name in self.ext_out:
                kind = "ExternalOutput"
            else:
                kind = "Internal"
        t = self.nc.dram_tensor(name, list(shape), dt, kind=kind)
        self.dr[name] = t
        return t

    def consts_dram(self):
        self.c_mats = self.dram("c_mats", [128, 4 * 128], F32, "ExternalInput")
        self.c_mask = self.dram("c_mask", [128, 4 * 512], F32, "ExternalInput")
        self.c_invc = self.dram("c_invc", [4, self.L], F32, "ExternalInput")

    def ph_norm_stats(self, xt, stl, stg):
        P, L, NCH = self.P, self.L, self.NCH
        P.begin()
        ones = P.tile("ones", [128, 128], BF16)
        xs = [P.tile(f"x{i}", [128, L], F32) for i in range(2)]
        sq = [P.tile(f"sq{i}", [128, L], BF16) for i in range(2)]
        srow = P.tile("srow", [1, L], F32)
        for t in xs + [srow]:
            t.ds = P.new_ds()
        ones.ds = P.pds("ones")
        P.dma("pool", ones[:, :], self.c_mats[:, 256:384], ones.ds, w=[ones])
        import os
        CUT = int(os.environ.get("PH1_CUT", "9"))
        for i in range(8):
            x = xs[i % 2]
            s = sq[i % 2]
            P.dma("sp", x[:, :], xt[i * 128:(i + 1) * 128, :], x.ds, w=[x])
            P.op("act", lambda E, s=s, x=x: E.activation(out=s[:, :], in_=x[:, :], func=AF.Square), r=[x], w=[s])
            if CUT < 2:
                continue
            for ch in range(NCH):
                ps = P.psum[ch]
                P.op("pe", lambda E, ps=ps, s=s, ch=ch, i=i: E.matmul(
                    ps[:, :], lhsT=ones[:, :], rhs=s[:, ch * 512:(ch + 1) * 512], start=(i == 0), stop=(i == 7)),
                    r=[ones, s], w=[ps], sig=(ch == NCH - 1))
        for ch in range(NCH):
            if CUT < 3:
                continue
            ps = P.psum[ch]
            P.op("dve", lambda E, ps=ps, ch=ch: E.tensor_copy(out=srow[0:1, ch * 512:(ch + 1) * 512], in_=ps[0:1, :]),
                 r=[ps], w=[srow])
        ev = None
        if CUT >= 4:
            ev = P.dma("sp", stl[0:1, :], srow[0:1, :], srow.ds, r=[srow])
        if self.use_cc:
            P.allgather(stl.ap(), stg.ap(), P.pds("ag"), after=[ev])
        P.end()

    def ph_norm_apply(self, xt, g, stg, htl, htg, final_out=None):
        P, L, NCH = self.P, self.L, self.NCH
        P.begin()
        ones4 = P.tile("ones4", [4, 128], F32)
        st4 = P.tile("st4", [4, L], F32)
        gt = P.tile("gt", [128, 8], F32)
        R = P.tile("R", [128, L], F32)
        xs = [P.tile(f"x{i}", [128, L], F32) for i in range(2)]
        odt = F32 if final_out is not None else BF16
        hs = [P.tile(f"h{i}", [128, L], odt) for i in range(2)]
        for t in xs + hs + [ones4, st4, gt]:
            t.ds = P.new_ds()
        P.dma("sp", ones4[:, :], self.c_mats[0:4, 256:384], ones4.ds, w=[ones4])
        P.dma("sp", st4[:, :], stg[:, :], st4.ds, w=[st4])
        P.dma("sp", gt[:, :], g[:, :], gt.ds, w=[gt])
        for ch in range(NCH):
            ps = P.psum[ch]
            sl = slice(ch * 512, (ch + 1) * 512)
            P.op("pe", lambda E, ps=ps, sl=sl: E.matmul(ps[:, :], lhsT=ones4[:, :], rhs=st4[:, sl], start=True, stop=True),
                 r=[ones4, st4], w=[ps])
            P.op("dve", lambda E, ps=ps, sl=sl: E.tensor_scalar(out=R[:, sl], in0=ps[:, :], scalar1=1.0 / D, scalar2=EPS,
                                                                 op0=ALU.mult, op1=ALU.add), r=[ps], w=[R])
        P.op("act", lambda E: E.activation(out=R[:, :], in_=R[:, :], func=AF.Sqrt), r=[R], w=[R])
        P.op("dve", lambda E: E.reciprocal(out=R[:, :], in_=R[:, :]), r=[R], w=[R])
        agds = P.pds("ag")
        for i in range(8):
            x = xs[i % 2]
            h = hs[i % 2]
            P.dma("sp", x[:, :], xt[i * 128:(i + 1) * 128, :], x.ds, w=[x])
            P.op("dve", lambda E, x=x, h=h, i=i: E.scalar_tensor_tensor(
                out=h[:, :], in0=x[:, :], scalar=gt[:, i:i + 1], in1=R[:, :], op0=ALU.mult, op1=ALU.mult),
                r=[x, gt, R], w=[h])
            if final_out is not None:
                P.dma("sp", final_out[i * 128:(i + 1) * 128, :], h[:, :], h.ds, r=[h])
            else:
                ev = P.dma("sp", htl[i][:, :], h[:, :], h.ds, r=[h])
                if self.use_cc:
                    P.allgather(htl[i].ap(), htg[i].ap(), agds, after=[ev])
        P.end()

    def dump(self, src, name, rows=None):
        P = self.P
        shape = list(src.shape)
        if rows is not None:
            shape[0] = rows
        dst = self.dram(name, shape, src.dtype, "ExternalOutput")
        P.begin()
        ds = P.new_ds()
        rows = shape[0]
        step = 128 if rows >= 128 else rows
        for r0 in range(0, rows, step):
            P.dma("sp", dst[r0:r0 + step, :], src[r0:r0 + step, :], ds)
        P.end()
        return dst

    def scratch(self):
        L = self.L
        S = {}
        S["PX"] = self.dram("PX", [256, L], F32)
        S["PG"] = self.dram("PG", [256, L], BF16)
        S["QT"] = self.dram("QT", [512, L], BF16)
        S["KT"] = self.dram("KT", [512, L], BF16)
        S["V"] = self.dram("V", [L, 512], BF16)
        S["AG"] = self.dram("AGs", [512, L], BF16)
        S["UT"] = self.dram("UT", [256, L], BF16)
        S["ygl"] = [self.dram(f"ygl{a}", [128, L], BF16) for a in range(10)]
        S["ygg"] = [self.dram(f"ygg{a}", [512, L], BF16) for a in range(10)]
        S["bsl"] = self.dram("bsl", [2, L], F32)
        S["bsg"] = self.dram("bsg", [8, L], F32)
        self.S = S
        return S

    def ph_inproj(self, htg, win):
        P, L, NCH, S = self.P, self.L, self.NCH, self.S
        P.begin()
        Ws = [P.tile(f"W{i}", [128, 4, 8, 512], BF16) for i in range(2)]
        HTs = [P.tile(f"HT{i}", [128, 4, 8, 512], BF16) for i in range(2)]
        NO = 6
        Of = [P.tile(f"Of{i}", [128, 512], F32) for i in range(NO)]
        Ob = [P.tile(f"Ob{i}", [128, 512], BF16) for i in range(NO)]
        for i, t in enumerate(Ws):
            t.ds = P.pds(f"w{i}")
        for t in HTs + Of + Ob:
            t.ds = P.new_ds()
        steps = [(g, ch) for g in range(6) for ch in range(NCH)]

        def load_w(g):
            W = Ws[g % 2]
            for r in range(4):
                src = win[1024 * r:1024 * (r + 1), g * 512:(g + 1) * 512].rearrange("(i p) f -> p i f", p=128)
                P.dma("pool", W[:, r, :, :], src, W.ds, w=[W])

        def load_h(si):
            g, ch = steps[si]
            HT = HTs[si % 2]
            for i in range(8):
                src = htg[i][:, ch * 512:(ch + 1) * 512].rearrange("(r p) t -> p r t", p=128)
                P.dma("sp", HT[:, :, i, :], src, HT.ds, w=[HT])

        load_w(0)
        load_h(0)
        oi = 0
        for si, (g, ch) in enumerate(steps):
            if ch == 0 and g + 1 < 6:
                load_w(g + 1)
            if si + 1 < len(steps):
                load_h(si + 1)
            W = Ws[g % 2]
            HT = HTs[si % 2]
            pb = (si % 2) * 4
            for r in range(4):
                for i in range(8):
                    kt = r * 8 + i
                    for f in range(4):
                        ps = P.psum[pb + f]
                        if g == 3:
                            fn = lambda E, ps=ps, r=r, i=i, f=f, W=W, HT=HT, kt=kt: E.matmul(
                                ps[:, :], lhsT=HT[:, r, i, f * 128:(f + 1) * 128], rhs=W[:, r, i, :],
                                start=(kt == 0), stop=(kt == 31))
                        else:
                            fn = lambda E, ps=ps, r=r, i=i, f=f, W=W, HT=HT, kt=kt: E.matmul(
                                ps[:, :], lhsT=W[:, r, i, f * 128:(f + 1) * 128], rhs=HT[:, r, i, :],
                                start=(kt == 0), stop=(kt == 31))
                        P.op("pe", fn, r=[W, HT], w=[ps], sig=(kt == 31))
            csl = slice(ch * 512, (ch + 1) * 512)
            for f in range(4):
                ps = P.psum[pb + f]
                eng = "act" if (f % 2 == 0) else "dve"
                rs = slice((f % 2) * 128, (f % 2) * 128 + 128)
                fs = slice(f * 128, f * 128 + 128)
                kind = "copy"
                if g == 0:
                    if f < 2:
                        dst, odt = S["PX"][rs, csl], F32
                    else:
                        dst, odt, kind = S["PG"][rs, csl], BF16, "silu"
                elif g == 1:
                    dst, odt, kind = S["QT"][fs, csl], BF16, "scale"
                elif g == 2:
                    dst, odt = S["KT"][fs, csl], BF16
                elif g == 3:
                    dst, odt = S["V"][ch * 512 + f * 128: ch * 512 + f * 128 + 128, :], BF16
                elif g == 4:
                    dst, odt, kind = S["AG"][fs, csl], BF16, "silu"
                else:
                    if f < 2:
                        dst, odt = S["UT"][rs, csl], BF16
                    else:
                        dst, odt, kind = S["ygl"][8 + f - 2][:, csl], BF16, "silu"
                O = (Of if odt == F32 else Ob)[oi % NO]
                oi += 1
                if kind == "silu":
                    eng = "act"
                    P.op("act", lambda E, O=O, ps=ps: E.activation(out=O[:, :], in_=ps[:, :], func=AF.Silu), r=[ps], w=[O])
                elif kind == "scale":
                    sc = 128.0 ** -0.5
                    if eng == "act":
                        P.op("act", lambda E, O=O, ps=ps: E.activation(out=O[:, :], in_=ps[:, :], func=AF.Copy, scale=sc), r=[ps], w=[O])
                    else:
                        P.op("dve", lambda E, O=O, ps=ps: E.tensor_scalar(out=O[:, :], in0=ps[:, :], scalar1=sc, scalar2=None, op0=ALU.mult), r=[ps], w=[O])
                else:
                    if eng == "act":
                        P.op("act", lambda E, O=O, ps=ps: E.activation(out=O[:, :], in_=ps[:, :], func=AF.Copy), r=[ps], w=[O])
                    else:
                        P.op("dve", lambda E, O=O, ps=ps: E.tensor_copy(out=O[:, :], in_=ps[:, :]), r=[ps], w=[O])
                P.dma("sp", dst, O[:, :], O.ds, r=[O])
        if self.use_cc:
            evs = [(t.ds, t.ds.n) for t in Of + Ob if t.ds.n > 0]
            for a in (8, 9):
                P.allgather(S["ygl"][a].ap(), S["ygg"][a].ap(), P.ag_sem("ag_a"), after=evs)
        P.end()

    def ph_pool(self, wpool, pscale, gpool, poolsel, poolselT):
        P, L, NCH, S = self.P, self.L, self.NCH, self.S
        P.begin()
        ones = P.tile("ones", [128, 128], BF16)
        ones.ds = P.pds("ones")
        P.dma("pool", ones[:, :], self.c_mats[:, 256:384], ones.ds, w=[ones])
        wp = P.tile("wp", [128, 2, 256], BF16)
        wp.ds = P.pds("w0")
        P.dma("pool", wp[:, :, :], wpool[:, :].rearrange("(j p) d -> p j d", p=128), wp.ds, w=[wp])
        PB = [P.tile(f"PB{j}", [128, L], BF16) for j in range(2)]
        small = P.tile("small", [128, 8], F32)
        small.ds = P.new_ds()
        P.dma("sp", small[:, 0:2], pscale[:, :], small.ds, w=[small])
        P.dma("sp", small[:, 2:4], gpool[:, :], small.ds, w=[small])
        P.dma("sp", small[:, 4:8], poolsel[:, :], small.ds, w=[small])
        selT = P.tile("selT", [4, 128], BF16)
        selT.ds = P.pds("w1")
        P.dma("pool", selT[:, :], poolselT[:, :], selT.ds, w=[selT])
        inv4 = [P.tile(f"inv4_{i}", [4, 512], F32) for i in range(2)]
        ihi = [P.tile(f"ihi{i}", [4, 512], BF16) for i in range(2)]
        ilo = [P.tile(f"ilo{i}", [4, 512], BF16) for i in range(2)]
        for t in inv4:
            t.ds = P.new_ds()
        IC = P.tile("IC", [128, L], F32)
        for ch in range(NCH):
            ps = P.psum[6 + ch % 2]
            i4, hi, lo = inv4[ch % 2], ihi[ch % 2], ilo[ch % 2]
            P.dma("sp", i4[:, :], self.c_invc[:, ch * 512:(ch + 1) * 512], i4.ds, w=[i4])
            P.op("dve", lambda E, i4=i4, hi=hi: E.tensor_copy(out=hi[:, :], in_=i4[:, :]), r=[i4], w=[hi])
            P.op("dve", lambda E, i4=i4, hi=hi, lo=lo: E.tensor_tensor(out=lo[:, :], in0=i4[:, :], in1=hi[:, :], op=ALU.subtract), r=[i4, hi], w=[lo])
            P.op("pe", lambda E, ps=ps, hi=hi: E.matmul(ps[:, :], lhsT=selT[:, :], rhs=hi[:, :], start=True, stop=False),
                 r=[selT, hi], w=[ps], sig=False)
            P.op("pe", lambda E, ps=ps, lo=lo: E.matmul(ps[:, :], lhsT=selT[:, :], rhs=lo[:, :], start=False, stop=True),
                 r=[selT, lo], w=[ps])
            P.op("act", lambda E, ps=ps, ch=ch: E.activation(out=IC[:, ch * 512:(ch + 1) * 512], in_=ps[:, :], func=AF.Copy), r=[ps], w=[IC])
        import os
        PCUT = int(os.environ.get("POOL_CUT", "9"))
        if PCUT < 1:
            P.end()
            return
        X = P.tile("X", [128, L], F32)
        A = P.tile("A", [128, L], F32)
        B = P.tile("B", [128, L], F32)
        Sx = P.tile("Sx", [128, L], F32)
        PGt = [P.tile(f"PG{j}", [128, L], BF16) for j in range(2)]
        srow = P.tile("srow", [1, L], F32)
        for t in [X, srow] + PGt:
            t.ds = P.new_ds()
        for j in range(2):
            P.dma("sp", PGt[j][:, :], S["PG"][j * 128:(j + 1) * 128, :], PGt[j].ds, w=[PGt[j]])
        for j in range(2):
            P.dma("sp", X[:, :], S["PX"][j * 128:(j + 1) * 128, :], X.ds, w=[X])
            src = X
            bufs = [A, B]
            for wi, k in enumerate((1, 2, 4, 8)):
                dst = bufs[wi % 2]
                P.op("dve", lambda E, dst=dst, src=src, k=k: E.tensor_tensor(out=dst[:, k:L], in0=src[:, k:L], in1=src[:, 0:L - k], op=ALU.add),
                     r=[src], w=[dst])
                P.op("dve", lambda E, dst=dst, src=src, k=k: E.tensor_copy(out=dst[:, 0:k], in_=src[:, 0:k]), r=[src], w=[dst])
                if wi == 0:
                    P.op("dve", lambda E, dst=dst: E.tensor_scalar(out=Sx[:, :], in0=dst[:, :], scalar1=small[:, 4:5], scalar2=None, op0=ALU.mult),
                         r=[dst, small], w=[Sx])
                else:
                    P.op("dve", lambda E, dst=dst, wi=wi: E.scalar_tensor_tensor(out=Sx[:, :], in0=dst[:, :], scalar=small[:, 4 + wi:5 + wi],
                                                                               in1=Sx[:, :], op0=ALU.mult, op1=ALU.add), r=[dst, small, Sx], w=[Sx])
                src = dst
            P.op("dve", lambda E: E.tensor_tensor(out=Sx[:, :], in0=Sx[:, :], in1=IC[:, :], op=ALU.mult), r=[Sx, IC], w=[Sx])
            P.op("dve", lambda E, j=j: E.tensor_tensor(out=Sx[:, :], in0=Sx[:, :], in1=X[:, :], op=ALU.subtract), r=[Sx, X], w=[Sx])
            if os.environ.get("PB_FROM_PG"):
                P.op("act", lambda E, j=j: E.activation(out=PB[j][:, :], in_=PGt[j][:, :], func=AF.Copy), r=[Sx, PGt[j]], w=[PB[j]])
            else:
                P.op("act", lambda E, j=j: E.activation(out=PB[j][:, :], in_=Sx[:, :], func=AF.Copy), r=[Sx], w=[PB[j]])
        if PCUT < 2:
            for j in range(2):
                PB[j].ds = P.new_ds()
                P.dma("sp", S["ygl"][j][:, :], PB[j][:, :], PB[j].ds, r=[PB[j]])
            P.end()
            return
        NO = 4
        Yt = [P.tile(f"Y{i}", [128, 512], F32) for i in range(NO)]
        Qt = [P.tile(f"Q{i}", [128, 512], BF16) for i in range(NO)]
        Gt = [P.tile(f"G{i}", [128, 512], BF16) for i in range(NO)]
        for t in Gt:
            t.ds = P.new_ds()
        oi = 0
        for ch in range(NCH):
            csl = slice(ch * 512, (ch + 1) * 512)
            pst = P.psum[4 + ch % 2]
            for dt in range(2):
                ps = P.psum[oi % 4]
                Y, Q, G = Yt[oi % NO], Qt[oi % NO], Gt[oi % NO]
                oi += 1
                for j in range(2 if PCUT != 19 else 0):
                    P.op("pe", lambda E, ps=ps, j=j, dt=dt, csl=csl: E.matmul(ps[:, :], lhsT=wp[:, j, dt * 128:(dt + 1) * 128], rhs=(PGt if os.environ.get("USE_PG") else PB)[j][:, csl],
                                                                          start=(j == 0), stop=(j == 1)), r=[wp, PB[j], PGt[j]], w=[ps], sig=(j == 1))
                P.op("dve", lambda E, ps=ps, Y=Y, dt=dt: E.tensor_scalar(out=Y[:, :], in0=ps[:, :], scalar1=small[:, dt:dt + 1], scalar2=None, op0=ALU.mult),
                     r=[ps, small], w=[Y])
                P.op("act", lambda E, Y=Y, Q=Q: E.activation(out=Q[:, :], in_=Y[:, :], func=AF.Square), r=[Y], w=[Q])
                if PCUT >= 3 and PCUT < 20:
                    P.op("pe", lambda E, pst=pst, Q=Q, dt=dt: E.matmul(pst[:, :], lhsT=ones[:, :], rhs=Q[:, :], start=(dt == 0), stop=(dt == 1)),
                     r=[ones, Q], w=[pst], sig=True)
                if PCUT >= 4 and PCUT < 20:
                    P.op("dve", lambda E, Y=Y, G=G, dt=dt, csl=csl: E.scalar_tensor_tensor(out=G[:, :], in0=Y[:, :], scalar=small[:, 2 + dt:3 + dt],
                                                                                   in1=PGt[dt][:, csl], op0=ALU.mult, op1=ALU.mult),
                     r=[Y, small, PGt[dt]], w=[G])
                if PCUT >= 5 and PCUT < 20:
                    P.dma("sp", S["ygl"][dt][:, csl], G[:, :], G.ds, r=[G])
            if PCUT >= 6 and PCUT < 20:
                P.op("dve", lambda E, pst=pst, csl=csl: E.tensor_copy(out=srow[0:1, csl], in_=pst[0:1, :]), r=[pst], w=[srow])
        if PCUT >= 6 and PCUT < 20:
            P.dma("sp", S["bsl"][0:1, :], srow[0:1, :], srow.ds, r=[srow])
        if self.use_cc:
            evs = [(t.ds, t.ds.n) for t in Gt if t.ds.n > 0]
            for a in (0, 1):
                P.allgather(S["ygl"][a].ap(), S["ygg"][a].ap(), P.ag_sem("ag_b"), after=evs)
        P.end()

    def ph_attn(self, gattn, wglu_next=None):
        P, L, NCH, S = self.P, self.L, self.NCH, self.S
        NBLK = L // 128
        P.begin()
        mats = P.tile("mats", [128, 3, 128], BF16)
        mats.ds = P.pds("ones")
        P.dma("pool", mats[:, :, :], self.c_mats[:, 0:384].rearrange("p (a b) -> p a b", a=3), mats.ds, w=[mats])
        mask = P.tile("mask", [128, 4, 512], BF16)
        mask.ds = P.pds("w0")
        P.dma("pool", mask[:, :, :], self.c_mask[:, :].rearrange("p (a b) -> p a b", a=4), mask.ds, w=[mask])
        ga = P.tile("ga", [128, 4], F32)
        ga.ds = P.new_ds()
        P.dma("sp", ga[:, :], gattn[:, :], ga.ds, w=[ga])
        Qh = [P.tile(f"Qh{i}", [128, L], BF16) for i in range(2)]
        Kh = [P.tile(f"Kh{i}", [128, L], BF16) for i in range(2)]
        Vh = [P.tile(f"Vh{i}", [128, NBLK, 128], BF16) for i in range(2)]
        Gh = [P.tile(f"Gh{i}", [128, L], BF16) for i in range(2)]
        for t in Qh + Kh + Vh + Gh:
            t.ds = P.new_ds()
        NR = 4
        Eb = [P.tile(f"E{i}", [128, 512], F32) for i in range(NR)]
        Lb = [P.tile(f"Lb{i}", [128, 512], BF16) for i in range(NR)]
        Db = [P.tile(f"D{i}", [128, 512], F32) for i in range(NR)]
        Wb = [P.tile(f"Wt{i}", [128, 512], BF16) for i in range(NR)]
        Carry = P.tile("Carry", [128, 512], F32)
        Yc = [P.tile(f"Yc{i}", [128, 512], F32) for i in range(2)]
        Yg = [P.tile(f"Yg{i}", [128, 512], BF16) for i in range(2)]
        Sq = [P.tile(f"Sq{i}", [128, 512], F32) for i in range(2)]
        SQacc = P.tile("SQacc", [128, L], F32)
        SQb = P.tile("SQb", [128, L], BF16)
        srow = P.tile("srow", [1, L], F32)
        for t in Yg + [srow]:
            t.ds = P.new_ds()

        def load_head(h):
            s = h % 2
            hs = slice(h * 128, (h + 1) * 128)
            P.dma("sp", Qh[s][:, :], S["QT"][hs, :], Qh[s].ds, w=[Qh[s]])
            P.dma("sp", Kh[s][:, :], S["KT"][hs, :], Kh[s].ds, w=[Kh[s]])
            P.dma("sp", Vh[s][:, :, :], S["V"][:, hs].rearrange("(n p) d -> p n d", p=128), Vh[s].ds, w=[Vh[s]])
            P.dma("sp", Gh[s][:, :], S["AG"][hs, :], Gh[s].ds, w=[Gh[s]])

        tiles = []
        for h in range(4):
            for qc in range(NCH):
                nb = 4 * (qc + 1)
                for bi, j in enumerate(range(nb - 1, -1, -1)):
                    tiles.append((h, qc, j, bi == 0, bi == nb - 1))
        T = len(tiles)
        ident, negtri, ones = mats[:, 0, :], mats[:, 1, :], mats[:, 2, :]
        strm = {}
        sc = 0
        for t in tiles:
            if t[3]:
                strm[(t[0], t[1])] = sc
                sc += 1

        def zmm(ps, h, qc, j, last_stop):
            s = h % 2
            r = j - 4 * qc
            qsl = slice(qc * 512, (qc + 1) * 512)
            ksl = slice(j * 128, (j + 1) * 128)
            diag = r >= 0
            P.op("pe", lambda E: E.matmul(ps[:, :], lhsT=Kh[s][:, ksl], rhs=Qh[s][:, qsl], start=True, stop=(last_stop and not diag)),
                 r=[Kh[s], Qh[s]], w=[ps], sig=(last_stop and not diag))
            if diag:
                P.op("pe", lambda E: E.matmul(ps[:, :], lhsT=ident, rhs=mask[:, r, :], start=False, stop=last_stop),
                     r=[mats, mask], w=[ps], sig=last_stop)

        def stage_a(ti):
            h, qc, j, first, last = tiles[ti]
            psz = P.psum[ti % 2]
            zmm(psz, h, qc, j, True)
            E_, L_ = Eb[ti % NR], Lb[ti % NR]
            P.op("act", lambda E: E.activation(out=E_[:, :], in_=psz[:, :], func=AF.Exp), r=[psz], w=[E_])
            P.op("act", lambda E: E.activation(out=L_[:, :], in_=E_[:, :], func=AF.Ln, bias=1.0), r=[E_], w=[L_])

        def stage_b(ti):
            h, qc, j, first, last = tiles[ti]
            pse = P.psum[2 + ti % 2]
            pst = P.psum[4 + ti % 2]
            L_, D_ = Lb[ti % NR], Db[ti % NR]
            zmm(pse, h, qc, j, False)
            P.op("pe", lambda E: E.matmul(pse[:, :], lhsT=negtri, rhs=L_[:, :], start=False, stop=True), r=[mats, L_], w=[pse])
            P.op("pe", lambda E: E.matmul(pst[:, :], lhsT=ones, rhs=L_[:, :], start=True, stop=True), r=[mats, L_], w=[pst])
            if first:
                P.op("dve", lambda E: E.tensor_copy(out=D_[:, :], in_=pse[:, :]), r=[pse], w=[D_])
                P.op("dve", lambda E: E.tensor_copy(out=Carry[:, :], in_=pst[:, :]), r=[pst], w=[Carry])
            else:
                P.op("dve", lambda E: E.tensor_tensor(out=D_[:, :], in0=pse[:, :], in1=Carry[:, :], op=ALU.subtract), r=[pse, Carry], w=[D_])
                if not last:
                    P.op("dve", lambda E: E.tensor_tensor(out=Carry[:, :], in0=pst[:, :], in1=Carry[:, :], op=ALU.add), r=[pst, Carry], w=[Carry])

        def stage_c1(ti):
            D_, W_ = Db[ti % NR], Wb[ti % NR]
            P.op("act", lambda E: E.activation(out=W_[:, :], in_=D_[:, :], func=AF.Exp), r=[D_], w=[W_])

        def stage_c2(ti):
            h, qc, j, first, last = tiles[ti]
            s = h % 2
            si = strm[(h, qc)]
            pso = P.psum[6 + si % 2]
            W_ = Wb[ti % NR]
            P.op("pe", lambda E: E.matmul(pso[:, :], lhsT=Vh[s][:, j, :], rhs=W_[:, :], start=first, stop=last), r=[Vh[s], W_], w=[pso], sig=last)
            if last:
                qsl = slice(qc * 512, (qc + 1) * 512)
                yc, yg, sq = Yc[si % 2], Yg[si % 2], Sq[si % 2]
                P.op("dve", lambda E: E.tensor_copy(out=yc[:, :], in_=pso[:, :]), r=[pso], w=[yc])
                P.op("dve", lambda E: E.scalar_tensor_tensor(out=yg[:, :], in0=yc[:, :], scalar=ga[:, h:h + 1], in1=Gh[s][:, qsl],
                                                             op0=ALU.mult, op1=ALU.mult), r=[yc, ga, Gh[s]], w=[yg])
                P.dma("sp", S["ygl"][2 + h][:, qsl], yg[:, :], yg.ds, r=[yg])
                if h == 0:
                    P.op("pool", lambda E: E.tensor_tensor(out=SQacc[:, qsl], in0=yc[:, :], in1=yc[:, :], op=ALU.mult), r=[yc], w=[SQacc])
                else:
                    P.op("pool", lambda E: E.tensor_tensor(out=sq[:, :], in0=yc[:, :], in1=yc[:, :], op=ALU.mult), r=[yc], w=[sq])
                    dst = SQb if h == 3 else SQacc
                    P.op("pool", lambda E: E.tensor_tensor(out=dst[:, qsl], in0=sq[:, :], in1=SQacc[:, qsl], op=ALU.add), r=[sq, SQacc], w=[dst])

        load_head(0)
        load_head(1)
        for ti in range(T + 3):
            if ti < T:
                stage_a(ti)
            if 0 <= ti - 1 < T:
                stage_b(ti - 1)
            if 0 <= ti - 2 < T:
                stage_c1(ti - 2)
            if 0 <= ti - 3 < T:
                stage_c2(ti - 3)
                hh, qq, jj, ff, ll = tiles[ti - 3]
                if ff and qq == 0 and 1 <= hh <= 2:
                    load_head(hh + 1)
        for ch in range(NCH):
            ps = P.psum[ch % 2]
            csl = slice(ch * 512, (ch + 1) * 512)
            P.op("pe", lambda E, ps=ps, csl=csl: E.matmul(ps[:, :], lhsT=ones, rhs=SQb[:, csl], start=True, stop=True), r=[mats, SQb], w=[ps])
            P.op("dve", lambda E, ps=ps, csl=csl: E.tensor_copy(out=srow[0:1, csl], in_=ps[0:1, :]), r=[ps], w=[srow])
        ev = P.dma("sp", S["bsl"][1:2, :], srow[0:1, :], srow.ds, r=[srow])
        if self.use_cc:
            evs = [(t.ds, t.ds.n) for t in Yg if t.ds.n > 0] + [ev]
            for a in (2, 3, 4, 5):
                P.allgather(S["ygl"][a].ap(), S["ygg"][a].ap(), P.ag_sem("ag_b"), after=evs)
            P.allgather(S["bsl"].ap(), S["bsg"].ap(), P.ag_sem("ag_b"), after=evs)
        P.end()

    def ph_ssm(self, lam, ldt, BA, BAs, C1, C2, dskip):
        P, L, NCH, S = self.P, self.L, self.NCH, self.S
        NLEV = 10
        P.begin()
        TWO_PI = 2.0 * math.pi
        MAG = 12582912.0
        CW1 = 6.28125
        CW2 = float(TWO_PI - 6.28125)
        lamt = P.tile("lamt", [128, 32], F32)
        ldtt = P.tile("ldtt", [128, 16], F32)
        dsk = P.tile("dsk", [128, 2], F32)
        for t in (lamt, ldtt, dsk):
            t.ds = P.new_ds()
        P.dma("sp", lamt[:, :], lam[:, :], lamt.ds, w=[lamt])
        P.dma("sp", ldtt[:, :], ldt[:, :], ldtt.ds, w=[ldtt])
        P.dma("sp", dsk[:, :], dskip[:, :], dsk.ds, w=[dsk])
        sg = P.tile("sg", [128, 2], F32)
        P.op("dve", lambda E: E.memset(sg[0:64, 0:1], 1.0), w=[sg])
        P.op("dve", lambda E: E.memset(sg[64:128, 0:1], -1.0), w=[sg])
        P.op("dve", lambda E: E.memset(sg[0:64, 1:2], -1.0), w=[sg])
        P.op("dve", lambda E: E.memset(sg[64:128, 1:2], 1.0), w=[sg])
        names = ["dt", "are", "th", "r", "k", "phs", "phc", "c1", "s1", "nr", "ni", "den", "inv", "t1", "t2", "cre", "cim", "a2", "b2"]
        q = {n: P.tile("q_" + n, [128, 16], F32) for n in names}
        CK = [P.tile(f"CK{k}", [128, 16], F32) for k in range(NLEV)]
        SK = [P.tile(f"SK{k}", [128, 16], F32) for k in range(NLEV)]
        NSK = [P.tile(f"NSK{k}", [128, 16], F32) for k in range(NLEV)]
        lre, lim = lamt[:, 0:16], lamt[:, 16:32]

        def dve(fn, r, w):
            P.op("dve", fn, r=r, w=w)

        def act(fn, r, w):
            P.op("act", fn, r=r, w=w)

        import os
        SCUT = int(os.environ.get("SSM_CUT", "9"))
        if SCUT < 1:
            P.end()
            return
        act(lambda E: E.activation(out=q["dt"][:, :], in_=ldtt[:, :], func=AF.Exp), [ldtt], [q["dt"]])
        dve(lambda E: E.tensor_tensor(out=q["are"][:, :], in0=lre, in1=q["dt"][:, :], op=ALU.mult), [lamt, q["dt"]], [q["are"]])
        dve(lambda E: E.tensor_tensor(out=q["th"][:, :], in0=lim, in1=q["dt"][:, :], op=ALU.mult), [lamt, q["dt"]], [q["th"]])
        act(lambda E: E.activation(out=q["r"][:, :], in_=q["are"][:, :], func=AF.Exp), [q["are"]], [q["r"]])

        def reduce_angle(dst, shift):
            dve(lambda E: E.tensor_scalar(out=q["t1"][:, :], in0=q["th"][:, :], scalar1=shift, scalar2=None, op0=ALU.add), [q["th"]], [q["t1"]])
            dve(lambda E: E.tensor_scalar(out=q["k"][:, :], in0=q["t1"][:, :], scalar1=float(1.0 / TWO_PI), scalar2=MAG, op0=ALU.mult, op1=ALU.add),
                [q["t1"]], [q["k"]])
            dve(lambda E: E.tensor_single_scalar(out=q["k"][:, :], in_=q["k"][:, :], scalar=-MAG, op=ALU.add), [q["k"]], [q["k"]])
            dve(lambda E: E.scalar_tensor_tensor(out=q["t1"][:, :], in0=q["k"][:, :], scalar=-CW1, in1=q["t1"][:, :], op0=ALU.mult, op1=ALU.add),
                [q["k"], q["t1"]], [q["t1"]])
            dve(lambda E: E.scalar_tensor_tensor(out=dst[:, :], in0=q["k"][:, :], scalar=-CW2, in1=q["t1"][:, :], op0=ALU.mult, op1=ALU.add),
                [q["k"], q["t1"]], [dst])

        reduce_angle(q["phs"], 0.0)
        reduce_angle(q["phc"], float(math.pi / 2))
        act(lambda E: E.activation(out=q["s1"][:, :], in_=q["phs"][:, :], func=AF.Sin), [q["phs"]], [q["s1"]])
        act(lambda E: E.activation(out=q["c1"][:, :], in_=q["phc"][:, :], func=AF.Sin), [q["phc"]], [q["c1"]])
        dve(lambda E: E.tensor_tensor(out=q["nr"][:, :], in0=q["r"][:, :], in1=q["c1"][:, :], op=ALU.mult), [q["r"], q["c1"]], [q["nr"]])
        dve(lambda E: E.tensor_single_scalar(out=q["nr"][:, :], in_=q["nr"][:, :], scalar=-1.0, op=ALU.add), [q["nr"]], [q["nr"]])
        dve(lambda E: E.tensor_tensor(out=q["ni"][:, :], in0=q["r"][:, :], in1=q["s1"][:, :], op=ALU.mult), [q["r"], q["s1"]], [q["ni"]])
        dve(lambda E: E.tensor_tensor(out=q["den"][:, :], in0=lre, in1=lre, op=ALU.mult), [lamt], [q["den"]])
        dve(lambda E: E.tensor_tensor(out=q["t1"][:, :], in0=lim, in1=lim, op=ALU.mult), [lamt], [q["t1"]])
        dve(lambda E: E.tensor_tensor(out=q["den"][:, :], in0=q["den"][:, :], in1=q["t1"][:, :], op=ALU.add), [q["den"], q["t1"]], [q["den"]])
        dve(lambda E: E.reciprocal(out=q["inv"][:, :], in_=q["den"][:, :]), [q["den"]], [q["inv"]])
        dve(lambda E: E.tensor_tensor(out=q["t1"][:, :], in0=q["nr"][:, :], in1=lre, op=ALU.mult), [q["nr"], lamt], [q["t1"]])
        dve(lambda E: E.tensor_tensor(out=q["t2"][:, :], in0=q["ni"][:, :], in1=lim, op=ALU.mult), [q["ni"], lamt], [q["t2"]])
        dve(lambda E: E.tensor_tensor(out=q["t1"][:, :], in0=q["t1"][:, :], in1=q["t2"][:, :], op=ALU.add), [q["t1"], q["t2"]], [q["t1"]])
        dve(lambda E: E.tensor_tensor(out=q["cre"][:, :], in0=q["t1"][:, :], in1=q["inv"][:, :], op=ALU.mult), [q["t1"], q["inv"]], [q["cre"]])
        dve(lambda E: E.tensor_tensor(out=q["t1"][:, :], in0=q["ni"][:, :], in1=lre, op=ALU.mult), [q["ni"], lamt], [q["t1"]])
        dve(lambda E: E.tensor_tensor(out=q["t2"][:, :], in0=q["nr"][:, :], in1=lim, op=ALU.mult), [q["nr"], lamt], [q["t2"]])
        dve(lambda E: E.tensor_tensor(out=q["t1"][:, :], in0=q["t1"][:, :], in1=q["t2"][:, :], op=ALU.subtract), [q["t1"], q["t2"]], [q["t1"]])
        dve(lambda E: E.tensor_tensor(out=q["cim"][:, :], in0=q["t1"][:, :], in1=q["inv"][:, :], op=ALU.mult), [q["t1"], q["inv"]], [q["cim"]])
        dve(lambda E: E.tensor_scalar(out=q["a2"][:, :], in0=q["cim"][:, :], scalar1=sg[:, 1:2], scalar2=None, op0=ALU.mult), [q["cim"], sg], [q["a2"]])
        dve(lambda E: E.tensor_scalar(out=q["b2"][:, :], in0=q["cre"][:, :], scalar1=sg[:, 0:1], scalar2=None, op0=ALU.mult), [q["cre"], sg], [q["b2"]])
        dve(lambda E: E.tensor_copy(out=CK[0][:, :], in_=q["c1"][:, :]), [q["c1"]], [CK[0]])
        dve(lambda E: E.tensor_copy(out=SK[0][:, :], in_=q["s1"][:, :]), [q["s1"]], [SK[0]])
        for k in range(NLEV):
            dve(lambda E, k=k: E.tensor_single_scalar(out=NSK[k][:, :], in_=SK[k][:, :], scalar=-1.0, op=ALU.mult), [SK[k]], [NSK[k]])
            if k + 1 < NLEV:
                dve(lambda E, k=k: E.tensor_tensor(out=q["t1"][:, :], in0=CK[k][:, :], in1=CK[k][:, :], op=ALU.mult), [CK[k]], [q["t1"]])
                dve(lambda E, k=k: E.tensor_tensor(out=q["t2"][:, :], in0=SK[k][:, :], in1=SK[k][:, :], op=ALU.mult), [SK[k]], [q["t2"]])
                dve(lambda E, k=k: E.tensor_tensor(out=CK[k + 1][:, :], in0=q["t1"][:, :], in1=q["t2"][:, :], op=ALU.subtract), [q["t1"], q["t2"]], [CK[k + 1]])
                dve(lambda E, k=k: E.tensor_tensor(out=q["t1"][:, :], in0=CK[k][:, :], in1=SK[k][:, :], op=ALU.mult), [CK[k], SK[k]], [q["t1"]])
                dve(lambda E, k=k: E.tensor_single_scalar(out=SK[k + 1][:, :], in_=q["t1"][:, :], scalar=2.0, op=ALU.mult), [q["t1"]], [SK[k + 1]])
        TC = 512
        NL = 9
        isw = P.tile("isw", [128, 2, 128], F32)
        isw.ds = P.new_ds()
        P.dma("sp", isw[:, 0, :], self.c_mats[:, 0:128], isw.ds, w=[isw])
        P.dma("sp", isw[:, 1, :], self.c_mats[:, 384:512], isw.ds, w=[isw])
        s9s = P.tile("s9s", [128, 16], F32)
        dve(lambda E: E.tensor_scalar(out=s9s[:, :], in0=SK[NL][:, :], scalar1=sg[:, 0:1], scalar2=None, op0=ALU.mult), [SK[NL], sg], [s9s])
        TB = [P.tile(f"TB{i}", [128, 4, TC], F32) for i in range(8)]
        Rts = [P.tile(f"Rt{i}", [128, TC], F32) for i in range(8)]
        RotF = P.tile("RotF", [128, 128], F32)
        RotH = [P.tile(f"RotH{i}", [128, 128], BF16) for i in range(8)]
        RotL = [P.tile(f"RotL{i}", [128, 128], BF16) for i in range(8)]
        winit = [P.tile(f"winit{i}", [128, 1], F32) for i in range(8)]
        whl = [P.tile(f"whl{i}", [128, 2], BF16) for i in range(4)]
        onesf = P.tile("onesf", [128, TC], F32)
        P.op("dve", lambda E: E.memset(onesf[:, :], 1.0), w=[onesf])
        U = P.tile("U", [128, L], BF16)
        U.ds = P.new_ds()
        mats = {n: P.tile("m_" + n, [128, 8, 128], BF16) for n in ("BA", "BAs", "C1", "C2")}
        for n, t in mats.items():
            t.ds = P.pds("m_" + n)
            P.op("dve", lambda E, t=t: E.memset(t[:, :, :], 0.0), w=[t])
        NR = 4
        v1 = [P.tile(f"v1_{i}", [128, TC], F32) for i in range(NR)]
        v2 = [P.tile(f"v2_{i}", [128, TC], F32) for i in range(NR)]
        Vt = [P.tile(f"V_{i}", [128, TC], F32) for i in range(NR)]
        Wc = [P.tile(f"Wc_{i}", [128, TC], F32) for i in range(NR)]
        P1 = [P.tile(f"P1_{i}", [128, TC], BF16) for i in range(NR)]
        P2 = [P.tile(f"P2_{i}", [128, TC], BF16) for i in range(NR)]
        yb = [P.tile(f"yb_{i}", [128, TC], F32) for i in range(2)]
        hb = [P.tile(f"hb_{i}", [128, TC], BF16) for i in range(2)]
        for t in hb:
            t.ds = P.new_ds()
        psR = P.psum[6]
        it = 0
        for jt in range(2):
            P.dma("sp", U[:, :], S["UT"][jt * 128:(jt + 1) * 128, :], U.ds, w=[U])
            for n, srcs in (("BA", BA), ("BAs", BAs), ("C1", C1), ("C2", C2)):
                t = mats[n]
                P._waits("pool", [], [t.b], [])
                for gi in range(8):
                    g = jt * 8 + gi
                    if n in ("BA", "BAs"):
                        P.dma("pool", t[16 * gi:16 * gi + 16, gi, :], srcs[g], t.ds)
                    else:
                        P.dma("pool", t[:, gi, 16 * gi:16 * gi + 16], srcs[g], t.ds)
                t.b.w = {t.ds: t.ds.n}
                t.b.r = {}
            for gi in range(8):
                g = jt * 8 + gi
                gs = slice(g, g + 1)
                tb = TB[gi]
                cb, sb = Buf("cb"), Buf("sb")
                dve(lambda E, tb=tb: E.memset(tb[:, 0, 0:1], 1.0), [], [tb])
                dve(lambda E, tb=tb: E.memset(tb[:, 1, 0:1], 0.0), [], [tb])
                for k in range(NL):
                    n = 1 << k
                    dve(lambda E, k=k, n=n, gs=gs, tb=tb: E.tensor_scalar(out=tb[:, 0, n:2 * n], in0=tb[:, 0, 0:n], scalar1=CK[k][:, gs], scalar2=None, op0=ALU.mult),
                        [tb, CK[k]], [cb])
                    dve(lambda E, k=k, n=n, gs=gs, tb=tb: E.tensor_scalar(out=tb[:, 1, n:2 * n], in0=tb[:, 1, 0:n], scalar1=CK[k][:, gs], scalar2=None, op0=ALU.mult),
                        [tb, CK[k]], [sb])
                    dve(lambda E, k=k, n=n, gs=gs, tb=tb: E.scalar_tensor_tensor(out=tb[:, 0, n:2 * n], in0=tb[:, 1, 0:n], scalar=NSK[k][:, gs], in1=tb[:, 0, n:2 * n],
                                                                               op0=ALU.mult, op1=ALU.add), [tb, NSK[k], cb], [cb])
                    dve(lambda E, k=k, n=n, gs=gs, tb=tb: E.scalar_tensor_tensor(out=tb[:, 1, n:2 * n], in0=tb[:, 0, 0:n], scalar=SK[k][:, gs], in1=tb[:, 1, n:2 * n],
                                                                               op0=ALU.mult, op1=ALU.add), [tb, SK[k], sb], [sb])
                    tb.b.w = dict(cb.w)
                    tb.b.w.update(sb.w)
                    tb.b.r = {}
                dve(lambda E, gs=gs, tb=tb: E.tensor_scalar(out=tb[:, 2, :], in0=tb[:, 0, :], scalar1=q["cre"][:, gs], scalar2=None, op0=ALU.mult), [tb, q["cre"]], [tb])
                dve(lambda E, gs=gs, tb=tb: E.scalar_tensor_tensor(out=tb[:, 2, :], in0=tb[:, 1, :], scalar=q["cim"][:, gs], in1=tb[:, 2, :], op0=ALU.mult, op1=ALU.add),
                    [tb, q["cim"]], [tb])
                dve(lambda E, gs=gs, tb=tb: E.tensor_scalar(out=tb[:, 3, :], in0=tb[:, 0, :], scalar1=q["a2"][:, gs], scalar2=None, op0=ALU.mult), [tb, q["a2"]], [tb])
                dve(lambda E, gs=gs, tb=tb: E.scalar_tensor_tensor(out=tb[:, 3, :], in0=tb[:, 1, :], scalar=q["b2"][:, gs], in1=tb[:, 3, :], op0=ALU.mult, op1=ALU.add),
                    [tb, q["b2"]], [tb])
                dve(lambda E, tb=tb: E.tensor_scalar(out=tb[:, 0, :], in0=tb[:, 0, :], scalar1=sg[:, 0:1], scalar2=None, op0=ALU.mult), [tb, sg], [tb])
                dve(lambda E, tb=tb: E.tensor_single_scalar(out=tb[:, 1, :], in_=tb[:, 1, :], scalar=-1.0, op=ALU.mult), [tb], [tb])
                dve(lambda E, gs=gs, gi=gi: E.tensor_scalar(out=Rts[gi][:, :], in0=onesf[:, :], scalar1=q["r"][:, gs], scalar2=None, op0=ALU.mult), [onesf, q["r"]], [Rts[gi]])
                dve(lambda E, gs=gs: E.tensor_scalar(out=RotF[:, :], in0=isw[:, 0, :], scalar1=CK[NL][:, gs], scalar2=None, op0=ALU.mult), [isw, CK[NL]], [RotF])
                dve(lambda E, gs=gs: E.scalar_tensor_tensor(out=RotF[:, :], in0=isw[:, 1, :], scalar=s9s[:, gs], in1=RotF[:, :], op0=ALU.mult, op1=ALU.add),
                    [isw, s9s, RotF], [RotF])
                dve(lambda E, gi=gi: E.tensor_copy(out=RotH[gi][:, :], in_=RotF[:, :]), [RotF], [RotH[gi]])
                dve(lambda E, gi=gi: E.tensor_tensor(out=RotL[gi][:, :], in0=RotF[:, :], in1=RotH[gi][:, :], op=ALU.subtract), [RotF, RotH[gi]], [RotL[gi]])
            units = []
            for ch in range(NCH):
                for gi in range(8):
                    units.append((ch, gi, it))
                    it += 1

            def s1(ch, gi, itx):
                csl = slice(ch * TC, (ch + 1) * TC)
                psA, psA2 = P.psum[itx % 2], P.psum[2 + itx % 2]
                a, b, V_ = v1[itx % NR], v2[itx % NR], Vt[itx % NR]
                tb = TB[gi]
                P.op("pe", lambda E: E.matmul(psA[:, :], lhsT=mats["BA"][:, gi, :], rhs=U[:, csl], start=True, stop=True),
                     r=[mats["BA"], U], w=[psA])
                P.op("pe", lambda E: E.matmul(psA2[:, :], lhsT=mats["BAs"][:, gi, :], rhs=U[:, csl], start=True, stop=True),
                     r=[mats["BAs"], U], w=[psA2])
                dve(lambda E: E.tensor_tensor(out=a[:, :], in0=psA[:, :], in1=tb[:, 2, :], op=ALU.mult), [psA, tb], [a])
                dve(lambda E: E.tensor_tensor(out=b[:, :], in0=psA2[:, :], in1=tb[:, 3, :], op=ALU.mult), [psA2, tb], [b])
                P.op("pool", lambda E: E.tensor_tensor(out=V_[:, :], in0=a[:, :], in1=b[:, :], op=ALU.add), r=[a, b], w=[V_])

            def s2(ch, gi, itx, jt=jt):
                psY = P.psum[4 + ch % 2]
                V_, W_, p1, p2 = Vt[itx % NR], Wc[itx % NR], P1[itx % NR], P2[itx % NR]
                tb = TB[gi]
                wi = winit[gi]
                if ch == 0:
                    dve(lambda E: E.tensor_tensor_scan(out=W_[:, :], data0=Rts[gi][:, :], data1=V_[:, :], initial=0.0,
                                                       op0=ALU.mult, op1=ALU.add), [Rts[gi], V_], [W_])
                else:
                    dve(lambda E: E.tensor_tensor_scan(out=W_[:, :], data0=Rts[gi][:, :], data1=V_[:, :], initial=wi[:, 0:1],
                                                       op0=ALU.mult, op1=ALU.add), [Rts[gi], V_, wi], [W_])
                P.op("pool", lambda E: E.tensor_tensor(out=p1[:, :], in0=W_[:, :], in1=tb[:, 0, :], op=ALU.mult), r=[W_, tb], w=[p1])
                P.op("pool", lambda E: E.tensor_tensor(out=p2[:, :], in0=W_[:, :], in1=tb[:, 1, :], op=ALU.mult), r=[W_, tb], w=[p2])
                if ch + 1 < NCH:
                    hl = whl[itx % 4]
                    dve(lambda E: E.tensor_copy(out=hl[:, 0:1], in_=W_[:, TC - 1:TC]), [W_], [hl])
                    dve(lambda E: E.tensor_tensor(out=hl[:, 1:2], in0=W_[:, TC - 1:TC], in1=hl[:, 0:1], op=ALU.subtract), [W_, hl], [hl])

            def s3(ch, gi, itx, jt=jt):
                psY = P.psum[4 + ch % 2]
                p1, p2 = P1[itx % NR], P2[itx % NR]
                wi = winit[gi]
                P.op("pe", lambda E: E.matmul(psY[:, :], lhsT=mats["C1"][:, gi, :], rhs=p1[:, :], start=(gi == 0), stop=False),
                     r=[mats["C1"], p1], w=[psY], sig=False)
                P.op("pe", lambda E: E.matmul(psY[:, :], lhsT=mats["C2"][:, gi, :], rhs=p2[:, :], start=False, stop=(gi == 7)),
                     r=[mats["C2"], p2], w=[psY], sig=True)
                if ch + 1 < NCH:
                    hl = whl[itx % 4]
                    P.op("pe", lambda E: E.matmul(psR[:, gi:gi + 1], lhsT=RotH[gi][:, :], rhs=hl[:, 0:1], start=True, stop=False),
                         r=[RotH[gi], hl], w=[psR], sig=False)
                    P.op("pe", lambda E: E.matmul(psR[:, gi:gi + 1], lhsT=RotH[gi][:, :], rhs=hl[:, 1:2], start=False, stop=False),
                         r=[RotH[gi], hl], w=[psR], sig=False)
                    P.op("pe", lambda E: E.matmul(psR[:, gi:gi + 1], lhsT=RotL[gi][:, :], rhs=hl[:, 0:1], start=False, stop=True),
                         r=[RotL[gi], hl], w=[psR], sig=True)
                    dve(lambda E: E.tensor_copy(out=wi[:, 0:1], in_=psR[:, gi:gi + 1]), [psR], [wi])
                if gi == 7:
                    csl = slice(ch * TC, (ch + 1) * TC)
                    y_, h_ = yb[ch % 2], hb[ch % 2]
                    dve(lambda E: E.tensor_copy(out=y_[:, :], in_=psY[:, :]), [psY], [y_])
                    dve(lambda E: E.scalar_tensor_tensor(out=y_[:, :], in0=U[:, csl], scalar=dsk[:, jt:jt + 1], in1=y_[:, :],
                                                         op0=ALU.mult, op1=ALU.add), [U, dsk, y_], [y_])
                    act(lambda E: E.activation(out=h_[:, :], in_=y_[:, :], func=AF.Gelu_apprx_tanh), [y_], [h_])
                    P.dma("sp", S["ygl"][6 + jt][:, csl], h_[:, :], h_.ds, r=[h_])

            nu = len(units)
            for k in range(nu + 2):
                if k < nu:
                    s1(*units[k])
                if 0 <= k - 1 < nu:
                    s2(*units[k - 1])
                if 0 <= k - 2 < nu:
                    s3(*units[k - 2])
        if self.use_cc:
            evs = [(t.ds, t.ds.n) for t in hb if t.ds.n > 0]
            for a in (6, 7):
                P.allgather(S["ygl"][a].ap(), S["ygg"][a].ap(), P.ag_sem("ag_a"), after=evs)
        P.end()

    def ph_exchange(self):
        P, S = self.P, self.S
        if not self.use_cc:
            return
        P.begin()
        ds = P.pds("ag")
        for a in range(10):
            P.allgather(S["ygl"][a].ap(), S["ygg"][a].ap(), ds)
        P.allgather(S["bsl"].ap(), S["bsg"].ap(), ds)
        P.end()

    def prefetch_wg(self, wglu):
        P = self.P
        t = self.xstack.enter_context(self.nc.sbuf_tensor(f"Wg_{P.phase_no}", [128, 8, 2048], BF16))
        Wg = Tile(t, "Wg")
        Wg.ds = P.pds("wg")
        for r in range(4):
            for i in range(2):
                row0 = 256 * r + 128 * i
                P.dma("pool", Wg[:, r * 2 + i, :], wglu[row0:row0 + 128, :], Wg.ds)
        self.Wg = Wg

    def prefetch_wo_alloc(self):
        P = self.P
        t = self.xstack.enter_context(self.nc.sbuf_tensor(f"Wo_{P.phase_no}", [128, 32, 1024], BF16))
        self.Wo_t = Tile(t, "Wo")

    def prefetch_wo(self, wout):
        P = self.P
        if getattr(self, "Wo_t", None) is None:
            self.prefetch_wo_alloc()
        Wo = self.Wo_t
        self.Wo_t = None
        Wo.ds = P.pds("wo")
        kt = 0
        for a in range(2):
            for r in range(4):
                row0 = 256 * r + 128 * a
                P.dma("pool", Wo[:, kt, :], wout[row0:row0 + 128, :], Wo.ds)
                kt += 1
        for i in range(4):
            for r in range(4):
                row0 = 1024 + 512 * r + 128 * i
                P.dma("pool", Wo[:, kt, :], wout[row0:row0 + 128, :], Wo.ds)
                kt += 1
        for nt in range(8):
            row0 = 3072 + 128 * nt
            P.dma("pool", Wo[:, kt, :], wout[row0:row0 + 128, :], Wo.ds)
            kt += 1
        self.Wo = Wo

    def ph_glu(self, wglu, bglu, gssm, YS, wout_next=None):
        P, L, NCH, S = self.P, self.L, self.NCH, self.S
        P.begin()
        if wout_next is not None:
            self.prefetch_wo_alloc()
        ones = P.tile("ones", [128, 128], BF16)
        ones.ds = P.pds("ones")
        P.dma("pool", ones[:, :], self.c_mats[:, 256:384], ones.ds, w=[ones])
        Wg = P.tile("Wg", [128, 8, 2048], BF16)
        Wg.ds = P.pds("wg")
        for r in range(4):
            for i in range(2):
                row0 = 256 * r + 128 * i
                P.dma("pool", Wg[:, r * 2 + i, :], wglu[row0:row0 + 128, :], Wg.ds)
        Wg.b.w = {Wg.ds: Wg.ds.n}
        if wout_next is not None:
            self.prefetch_wo(wout_next)
        sm = P.tile("sm", [128, 24], F32)
        sm.ds = P.new_ds()
        P.dma("sp", sm[:, 0:16], bglu[:, :], sm.ds, w=[sm])
        P.dma("sp", sm[:, 16:24], gssm[:, :], sm.ds, w=[sm])
        HGs = [P.tile(f"HG{i}", [128, 4, 2, 512], BF16) for i in range(2)]
        SGs = [P.tile(f"SG{i}", [128, 4, 2, 512], BF16) for i in range(2)]
        ys = [P.tile(f"ys{i}", [128, 8, 512], F32) for i in range(2)]
        Rs = [P.tile(f"Rs{i}", [128, 512], F32) for i in range(2)]
        tmp = [P.tile(f"tmp{i}", [128, 512], F32) for i in range(2)]
        ob = [P.tile(f"ob{i}", [128, 512], BF16) for i in range(4)]
        for t in HGs + SGs + ob:
            t.ds = P.new_ds()

        if self.use_cc:
            P.wait_all("sp", P.ag_sem("ag_a"))

        def load(ch):
            csl = slice(ch * 512, (ch + 1) * 512)
            for i in range(2):
                P.dma("sp", HGs[ch % 2][:, :, i, :], S["ygg"][6 + i][:, csl].rearrange("(r p) t -> p r t", p=128), HGs[ch % 2].ds, w=[HGs[ch % 2]])
                P.dma("sp", SGs[ch % 2][:, :, i, :], S["ygg"][8 + i][:, csl].rearrange("(r p) t -> p r t", p=128), SGs[ch % 2].ds, w=[SGs[ch % 2]])

        load(0)
        units = [(ch, nt) for ch in range(NCH) for nt in range(8)]
        NSG = 4
        sgm = [P.tile(f"sgmx{i}", [128, 512], F32) for i in range(NSG)]
        sqb = [P.tile(f"sqbx{i}", [128, 512], BF16) for i in range(NSG)]
        oi = [0]

        def g1(u):
            ch, nt = units[u]
            HG, Y = HGs[ch % 2], ys[ch % 2]
            psv, psg = P.psum[u % 2], P.psum[2 + u % 2]
            g_ = sgm[u % NSG]
            for kt in range(8):
                r, i = kt // 2, kt % 2
                P.op("pe", lambda E, kt=kt, r=r, i=i: E.matmul(
                    psg[:, :], lhsT=Wg[:, kt, 1024 + nt * 128:1024 + (nt + 1) * 128], rhs=HG[:, r, i, :], start=(kt == 0), stop=(kt == 7)),
                    r=[Wg, HG], w=[psg], sig=(kt == 7))
            for kt in range(8):
                r, i = kt // 2, kt % 2
                P.op("pe", lambda E, kt=kt, r=r, i=i: E.matmul(
                    psv[:, :], lhsT=Wg[:, kt, nt * 128:(nt + 1) * 128], rhs=HG[:, r, i, :], start=(kt == 0), stop=(kt == 7)),
                    r=[Wg, HG], w=[psv], sig=(kt == 7))
            P.op("act", lambda E: E.activation(out=g_[:, :], in_=psg[:, :], func=AF.Sigmoid, bias=sm[:, 8 + nt:9 + nt]),
                 r=[psg, sm], w=[g_])
            P.op("dve", lambda E: E.scalar_tensor_tensor(out=Y[:, nt, :], in0=psv[:, :], scalar=sm[:, nt:nt + 1], in1=g_[:, :],
                                                         op0=ALU.add, op1=ALU.mult), r=[psv, sm, g_], w=[Y])

        def g2(u):
            ch, nt = units[u]
            Y = ys[ch % 2]
            q_ = sqb[u % NSG]
            P.op("act", lambda E: E.activation(out=q_[:, :], in_=Y[:, nt, :], func=AF.Square), r=[Y], w=[q_])

        def g3(u):
            ch, nt = units[u]
            csl = slice(ch * 512, (ch + 1) * 512)
            SG, Y, R = SGs[ch % 2], ys[ch % 2], Rs[ch % 2]
            pss = P.psum[4 + ch % 2]
            q_ = sqb[u % NSG]
            P.op("pe", lambda E: E.matmul(pss[:, :], lhsT=ones[:, :], rhs=q_[:, :], start=(nt == 0), stop=(nt == 7)),
                 r=[ones, q_], w=[pss], sig=True)
            if nt == 7:
                P.op("dve", lambda E: E.tensor_scalar(out=R[:, :], in0=pss[:, :], scalar1=1.0 / 1024, scalar2=EPS, op0=ALU.mult, op1=ALU.add),
                     r=[pss], w=[R])
                P.op("act", lambda E: E.activation(out=R[:, :], in_=R[:, :], func=AF.Sqrt), r=[R], w=[R])
                P.op("dve", lambda E: E.reciprocal(out=R[:, :], in_=R[:, :]), r=[R], w=[R])
                for n2 in range(8):
                    r, i = n2 // 2, n2 % 2
                    t_ = tmp[n2 % 2]
                    o_ = ob[oi[0] % 4]
                    oi[0] += 1
                    P.op("dve", lambda E, t_=t_, n2=n2, r=r, i=i: E.scalar_tensor_tensor(
                        out=t_[:, :], in0=Y[:, n2, :], scalar=sm[:, 16 + n2:17 + n2], in1=SG[:, r, i, :], op0=ALU.mult, op1=ALU.mult),
                        r=[Y, sm, SG], w=[t_])
                    P.op("pool", lambda E, o_=o_, t_=t_: E.tensor_tensor(out=o_[:, :], in0=t_[:, :], in1=R[:, :], op=ALU.mult), r=[t_, R], w=[o_])
                    P.dma("sp", YS[n2 * 128:(n2 + 1) * 128, csl], o_[:, :], o_.ds, r=[o_])
                if ch + 2 < NCH:
                    load(ch + 2)

        if NCH > 1:
            load(1)
        nu = len(units)
        for k in range(nu + 2):
            if k < nu:
                g1(k)
            if 0 <= k - 1 < nu:
                g2(k - 1)
            if 0 <= k - 2 < nu:
                g3(k - 2)
        P.end()

    def ph_outproj(self, wout, YS, xin, xout):
        P, L, NCH, S = self.P, self.L, self.NCH, self.S
        P.begin()
        if getattr(self, "Wo", None) is None:
            self.prefetch_wo(wout)
        Wo = self.Wo
        self.Wo = None
        Wo.b = Buf("Wo")
        Wo.b.w = {Wo.ds: Wo.ds.n}
        sel = P.tile("sel", [8, 256], BF16)
        sel.ds = P.pds("w0")
        P.dma("pool", sel[:, :], self.c_sel[:, :], sel.ds, w=[sel])
        st8 = [P.tile(f"st8_{i}", [8, 512], F32) for i in range(2)]
        shi = [P.tile(f"shi{i}", [8, 512], BF16) for i in range(2)]
        slo = [P.tile(f"slo{i}", [8, 512], BF16) for i in range(2)]
        for t in st8:
            t.ds = P.new_ds()
        A24 = [P.tile(f"A24_{i}", [128, 6, 4, 512], BF16) for i in range(2)]
        YSc = [P.tile(f"YSc{i}", [128, 8, 512], BF16) for i in range(2)]
        Rb = [[P.tile(f"Rb{b}_{i}", [128, 512], F32) for i in range(2)] for b in range(2)]
        xt = [P.tile(f"xt{i}", [128, 512], F32) for i in range(4)]
        xo = [P.tile(f"xo{i}", [128, 512], F32) for i in range(4)]
        for t in A24 + YSc + xt + xo:
            t.ds = P.new_ds()

        if self.use_cc:
            P.wait_all("sp", P.ag_sem("ag_a"))
            P.wait_all("sp", P.ag_sem("ag_b"))

        def load(ch):
            csl = slice(ch * 512, (ch + 1) * 512)
            A = A24[ch % 2]
            for a in range(6):
                P.dma("sp", A[:, a, :, :], S["ygg"][a][:, csl].rearrange("(r p) t -> p r t", p=128), A.ds, w=[A])
            Y = YSc[ch % 2]
            P.dma("sp", Y[:, :, :], YS[:, csl].rearrange("(n p) t -> p n t", p=128), Y.ds, w=[Y])

        def prep(ch):
            csl = slice(ch * 512, (ch + 1) * 512)
            A, Y = A24[ch % 2], YSc[ch % 2]
            s8, hi8, lo8 = st8[ch % 2], shi[ch % 2], slo[ch % 2]
            P.dma("sp", s8[:, :], S["bsg"][:, csl], s8.ds, w=[s8])
            P.op("dve", lambda E, s8=s8, hi8=hi8: E.tensor_copy(out=hi8[:, :], in_=s8[:, :]), r=[s8], w=[hi8])
            P.op("dve", lambda E, s8=s8, hi8=hi8, lo8=lo8: E.tensor_tensor(out=lo8[:, :], in0=s8[:, :], in1=hi8[:, :], op=ALU.subtract), r=[s8, hi8], w=[lo8])
            for b in range(2):
                ps = P.psum[6 + b]
                R = Rb[b][ch % 2]
                n_b = 1024.0 if b == 0 else 2048.0
                P.op("pe", lambda E, ps=ps, b=b, hi8=hi8: E.matmul(ps[:, :], lhsT=sel[:, b * 128:(b + 1) * 128], rhs=hi8[:, :], start=True, stop=False),
                     r=[sel, hi8], w=[ps], sig=False)
                P.op("pe", lambda E, ps=ps, b=b, lo8=lo8: E.matmul(ps[:, :], lhsT=sel[:, b * 128:(b + 1) * 128], rhs=lo8[:, :], start=False, stop=True),
                     r=[sel, lo8], w=[ps])
                P.op("dve", lambda E, R=R, ps=ps, n_b=n_b: E.tensor_scalar(out=R[:, :], in0=ps[:, :], scalar1=1.0 / n_b, scalar2=EPS, op0=ALU.mult, op1=ALU.add),
                     r=[ps], w=[R])
                P.op("act", lambda E, R=R: E.activation(out=R[:, :], in_=R[:, :], func=AF.Sqrt), r=[R], w=[R])
                P.op("dve", lambda E, R=R: E.reciprocal(out=R[:, :], in_=R[:, :]), r=[R], w=[R])
            cnt = 0
            for a in range(6):
                R = Rb[0 if a < 2 else 1][ch % 2]
                for r in range(4):
                    eng = "dve" if cnt % 2 == 0 else "pool"
                    cnt += 1
                    P.op(eng, lambda E, A=A, a=a, r=r, R=R: E.tensor_tensor(out=A[:, a, r, :], in0=A[:, a, r, :], in1=R[:, :], op=ALU.mult), r=[A, R], w=[A])

        xi_ = [0]

        def mm(ch):
            csl = slice(ch * 512, (ch + 1) * 512)
            A, Y = A24[ch % 2], YSc[ch % 2]
            for no in range(8):
                ps = P.psum[no % 4]
                x_, o_ = xt[xi_[0] % 4], xo[xi_[0] % 4]
                xi_[0] += 1
                P.dma("sp", x_[:, :], xin[no * 128:(no + 1) * 128, csl], x_.ds, w=[x_])
                for kt in range(32):
                    if kt < 24:
                        rhs_t, rhs = A, A[:, kt // 4, kt % 4, :]
                    else:
                        rhs_t, rhs = Y, Y[:, kt - 24, :]
                    P.op("pe", lambda E, ps=ps, kt=kt, no=no, rhs=rhs: E.matmul(ps[:, :], lhsT=Wo[:, kt, no * 128:(no + 1) * 128], rhs=rhs,
                                                                            start=(kt == 0), stop=(kt == 31)), r=[Wo, rhs_t], w=[ps], sig=(kt == 31))
                P.op("dve", lambda E, ps=ps, x_=x_, o_=o_: E.tensor_tensor(out=o_[:, :], in0=ps[:, :], in1=x_[:, :], op=ALU.add), r=[ps, x_], w=[o_])
                P.dma("sp", xout[no * 128:(no + 1) * 128, csl], o_[:, :], o_.ds, r=[o_])

        load(0)
        prep(0)
        if NCH > 1:
            load(1)
        for ch in range(NCH):
            if ch + 1 < NCH:
                prep(ch + 1)
            mm(ch)
            if ch + 2 < NCH:
                load(ch + 2)
        P.end()
        self.xstack.close()
        self.xstack = contextlib.ExitStack()


def build_full(L, depth=DEPTH, use_cc=True):
    m = MK(L, use_cc=use_cc)
    m.consts_dram()
    m.c_sel = m.dram("c_sel", [8, 256], F32, "ExternalInput")
    S = m.scratch()
    ext = lambda n, shp: m.dram(n, shp, F32, "ExternalInput")
    xT = ext("xT", [1024, L])
    poolsel = ext("poolsel", [128, 4])
    poolselT = ext("poolselT", [4, 128])
    fing = ext("fing", [128, 8])
    outT = m.dram("outT", [1024, L], F32, "ExternalOutput")
    stl = m.dram("stl", [1, L], F32)
    stg = m.dram("stg", [4, L], F32)
    htl = [m.dram(f"htl{i}", [128, L], BF16) for i in range(8)]
    htg = [m.dram(f"htg{i}", [512, L], BF16) for i in range(8)]
    YS = m.dram("YS", [1024, L], BF16)
    XT = [m.dram(f"XT{l}", [1024, L], F32) for l in range(depth)]
    xin = xT
    for l in range(depth):
        lng = ext(f"lng{l}", [128, 8])
        win = ext(f"win{l}", [4096, 3072])
        wpool = ext(f"wpool{l}", [256, 256])
        pscale = ext(f"pscale{l}", [128, 2])
        gpool = ext(f"gpool{l}", [128, 2])
        gattn = ext(f"gattn{l}", [128, 4])
        gssm = ext(f"gssm{l}", [128, 8])
        lam = ext(f"ssm_lam{l}", [128, 32])
        ldt = ext(f"ssm_ldt{l}", [128, 16])
        BA = ext(f"ssm_BA{l}", [16, 16, 128])
        BAs = ext(f"ssm_BAs{l}", [16, 16, 128])
        C1 = ext(f"ssm_C1{l}", [16, 128, 16])
        C2 = ext(f"ssm_C2{l}", [16, 128, 16])
        dsk = ext(f"dskip{l}", [128, 2])
        wglu = ext(f"wglu{l}", [1024, 2048])
        bglu = ext(f"bglu{l}", [128, 16])
        wout = ext(f"wout{l}", [4096, 1024])
        m.ph_norm_stats(xin, stl, stg)
        m.ph_norm_apply(xin, lng, stg, htl, htg)
        m.ph_inproj(htg, win)
        m.ph_ssm(lam, ldt, BA, BAs, C1, C2, dsk)
        m.ph_pool(wpool, pscale, gpool, poolsel, poolselT)
        m.ph_attn(gattn)
        m.ph_glu(wglu, bglu, gssm, YS, wout_next=wout)
        m.ph_outproj(wout, YS, xin, XT[l])
        xin = XT[l]
    m.ph_norm_stats(xin, stl, stg)
    m.ph_norm_apply(xin, fing, stg, htl, htg, final_out=outT)
    return m


def host_consts(L):
    ident = np.eye(128, dtype=np.float32)
    negtri = -(np.arange(128)[:, None] >= np.arange(128)[None, :]).astype(np.float32)
    ones = np.ones((128, 128), np.float32)
    swap = np.zeros((128, 128), np.float32)
    swap[np.arange(128), (np.arange(128) + 64) % 128] = 1.0
    mats = np.concatenate([ident, negtri, ones, swap], axis=1)
    mask = np.zeros((128, 4, 512), np.float32)
    for r in range(4):
        mask[:, r, :] = np.where((128 * r + np.arange(128))[:, None] >= np.arange(512)[None, :], NEG, 0.0)
    pos = np.arange(1, L + 1, dtype=np.float32)
    invc = np.stack([1.0 / np.minimum(pos, float(w)) for w in (2, 4, 8, 16)]).astype(np.float32)
    sel = np.zeros((8, 256), np.float32)
    sel[0::2, 0:128] = 1.0
    sel[1::2, 128:256] = 1.0
    return {"c_mats": mats, "c_mask": mask.reshape(128, 2048), "c_invc": invc, "c_sel": sel}


def _pt(v, n):
    return np.ascontiguousarray(np.asarray(v, np.float32).reshape(n, 128).T)


def host_inputs(inp, L, depth=DEPTH):
    cs = host_consts(L)
    maps = []
    for core in range(8):
        b, c = core // 4, core % 4
        d = dict(cs)
        d["xT"] = np.ascontiguousarray(inp["x"][b][:, c * 1024:(c + 1) * 1024].T)
        sel = np.zeros((128, 4), np.float32)
        sel[:, c] = 1.0
        d["poolsel"] = sel
        d["poolselT"] = np.ascontiguousarray(sel.T)
        d["fing"] = _pt(inp["final_g"][c * 1024:(c + 1) * 1024], 8)
        for l in range(depth):
            w_in = inp["w_in"][l]
            cols = np.concatenate([
                np.arange(256 * c, 256 * c + 256), 1024 + np.arange(256 * c, 256 * c + 256),
                2048 + np.arange(512 * c, 512 * c + 512), 4096 + np.arange(512 * c, 512 * c + 512),
                6144 + np.arange(512 * c, 512 * c + 512), 8192 + np.arange(512 * c, 512 * c + 512),
                10240 + np.arange(256 * c, 256 * c + 256), 11264 + np.arange(256 * c, 256 * c + 256)])
            d[f"lng{l}"] = _pt(inp["ln_g"][l][c * 1024:(c + 1) * 1024], 8)
            d[f"win{l}"] = np.ascontiguousarray(w_in[:, cols])
            d[f"wpool{l}"] = np.ascontiguousarray(inp["w_pool"][l][c])
            d[f"pscale{l}"] = _pt(inp["pool_scale"][l][256 * c:256 * c + 256], 2)
            bg = inp["branch_g"][l]
            d[f"gpool{l}"] = _pt(bg[256 * c:256 * c + 256], 2)
            d[f"gattn{l}"] = _pt(bg[1024 + 512 * c:1024 + 512 * c + 512], 4)
            d[f"gssm{l}"] = _pt(bg[3072:4096], 8)
            gs = slice(16 * c, 16 * c + 16)
            lre = inp["lam_re"][l][gs].T
            lim = inp["lam_im"][l][gs].T
            d[f"ssm_lam{l}"] = np.ascontiguousarray(np.concatenate(
                [np.concatenate([lre, lre], 0), np.concatenate([lim, lim], 0)], axis=1).astype(np.float32))
            d[f"ssm_ldt{l}"] = np.ascontiguousarray(np.broadcast_to(inp["log_dt"][l][gs][None, :], (128, 16)).astype(np.float32))
            bre = inp["b_re"][l][gs].transpose(0, 2, 1)
            bim = inp["b_im"][l][gs].transpose(0, 2, 1)
            d[f"ssm_BA{l}"] = np.ascontiguousarray(np.concatenate([bre, bim], axis=2))
            d[f"ssm_BAs{l}"] = np.ascontiguousarray(np.concatenate([bim, bre], axis=2))
            cre = inp["c_re"][l][gs].transpose(0, 2, 1)
            cim = inp["c_im"][l][gs].transpose(0, 2, 1)
            d[f"ssm_C1{l}"] = np.ascontiguousarray(np.concatenate([cre, cim], axis=1))
            d[f"ssm_C2{l}"] = np.ascontiguousarray(np.concatenate([cim, cre], axis=1))
            d[f"dskip{l}"] = _pt(inp["d_skip"][l][256 * c:256 * c + 256], 2)
            d[f"wglu{l}"] = np.ascontiguousarray(inp["w_glu"][l])
            d[f"bglu{l}"] = _pt(inp["b_glu"][l], 16)
            d[f"wout{l}"] = np.ascontiguousarray(inp["w_out"][l][:, c * 1024:(c + 1) * 1024])
        maps.append(d)
    return maps


def kernel(**inputs):
    inp = {k: np.asarray(v) for k, v in inputs.items()}
    L = inp["x"].shape[1]
    m = build_full(L)
    maps = host_inputs(inp, L)
    res = run_bass_kernel_spmd(m.nc, maps, core_ids=list(range(8)))
    out = np.empty((NB, L, D), np.float32)
    for core in range(8):
        b, c = core // 4, core % 4
        out[b][:, c * 1024:(c + 1) * 1024] = res.results[core]["outT"].T
    return out
```
